# Optimizing a Trainium2 kernel written in Bass

```python
import math
import jax, jax.numpy as jnp
from jax import lax
import numpy as np

D_MODEL = 1024
BATCH = 8
SEQ = 4096
DEPTH = 4

GRID_W = 64
CTX_LEN = 256
HEAD_DIM = 64
HY_WIDTH = D_MODEL // 4
HY_GROUP_DIM = 64
HY_ORDER = 2
HY_BANDS = 16
HY_POS_DIM = 1 + 2 * HY_BANDS
HY_FFN = 64
HY_DECAY_TARGET = 1e-2
HY_FAST_DECAY = 0.3
HY_SLOW_DECAY = 1.5
DA_WIDTH = D_MODEL // 2
DA_VDIM = 2 * HEAD_DIM
DA_HEADS = DA_WIDTH // DA_VDIM
Q_BLOCK = 128
ML_WIDTH = D_MODEL // 4
ML_HEADS = ML_WIDTH // HEAD_DIM
ML_CHUNK = 64
MIX_WIDTH = HY_WIDTH + DA_WIDTH + ML_WIDTH
D_FF = ((8 * D_MODEL // 3) + 127) // 128 * 128
SHORT_CONV = 3
ROPE_BASE = 10000.0
EPS = 1e-6
COL_SIZES = ((HY_ORDER + 1) * HY_WIDTH, 2 * DA_HEADS * HEAD_DIM, 2 * DA_HEADS * HEAD_DIM, DA_WIDTH,
             ML_WIDTH, ML_WIDTH, ML_WIDTH, ML_WIDTH, 4 * ML_HEADS)
IN_WIDTH = sum(COL_SIZES)
COL_SPLITS = tuple(sum(COL_SIZES[:i + 1]) for i in range(len(COL_SIZES) - 1))

kernel_name = 'hybrid_hyena_diffattn_mlstm_dit'


def rms_norm(x):
    xf = x.astype(jnp.float32)
    return (xf * lax.rsqrt(jnp.mean(xf * xf, axis=-1, keepdims=True) + EPS)).astype(x.dtype)


def group_rms_norm(x, group):
    shape = x.shape
    return rms_norm(x.reshape(shape[:-1] + (shape[-1] // group, group))).reshape(shape)


def modulate(h, shift, scale):
    return h * (1 + scale) + shift


def short_conv(u, w, b):
    L = u.shape[1]
    pad = SHORT_CONV // 2
    up = jnp.pad(u, ((0, 0), (pad, pad), (0, 0)))
    return sum(up[:, j:j + L] * w[j] for j in range(SHORT_CONV)) + b


def heads_merge(a):
    B, H, L, d = a.shape
    return a.transpose(0, 2, 1, 3).reshape(B, L, H * d)


def axial_rope_tables(n_tokens):
    rows_n = n_tokens // GRID_W
    rows = jnp.repeat(jnp.arange(rows_n, dtype=jnp.float32), GRID_W)
    cols = jnp.tile(jnp.arange(GRID_W, dtype=jnp.float32), rows_n)
    nf = HEAD_DIM // 4
    inv = ROPE_BASE ** (-jnp.arange(nf, dtype=jnp.float32) / nf)
    ang = jnp.concatenate([rows[:, None] * inv, cols[:, None] * inv], axis=-1)
    return jnp.cos(ang), jnp.sin(ang)


def apply_axial_rope(x, cos, sin):
    L = x.shape[1]
    nf = HEAD_DIM // 4
    xs = x.astype(jnp.float32).reshape(x.shape[:-1] + (2, 2, nf))
    x1, x2 = xs[..., 0, :], xs[..., 1, :]
    c = cos.reshape(L, 1, 1, 2, nf)
    s = sin.reshape(L, 1, 1, 2, nf)
    out = jnp.stack([x1 * c - x2 * s, x1 * s + x2 * c], axis=-2)
    return out.reshape(x.shape).astype(x.dtype)


def hyena_filter_freq(L, w1, b1, w2, b2, w3, b3):
    t = jnp.linspace(0.0, 1.0, L, dtype=jnp.float32)
    pos = jnp.arange(L, dtype=jnp.float32)
    f = jnp.linspace(1e-4, HY_BANDS - 1, HY_BANDS, dtype=jnp.float32)
    ang = (2.0 * math.pi / L) * pos[:, None] * f
    z = jnp.concatenate([t[:, None], jnp.cos(ang), jnp.sin(ang)], axis=-1)
    h = jnp.sin(z @ w1 + b1)
    h = jnp.sin(h @ w2 + b2)
    h = (h @ w3 + b3).astype(jnp.float32).reshape(L, HY_ORDER, 2, HY_WIDTH)
    deltas = jnp.abs(jnp.linspace(math.log(HY_DECAY_TARGET) / HY_FAST_DECAY,
                                  math.log(HY_DECAY_TARGET) / HY_SLOW_DECAY, HY_WIDTH, dtype=jnp.float32))
    h = h * jnp.exp(-t[:, None, None, None] * deltas)
    h_fwd, h_bwd = h[:, :, 0], h[:, :, 1]
    taps = jnp.concatenate([h_fwd, jnp.zeros((1, HY_ORDER, HY_WIDTH), jnp.float32),
                            jnp.flip(h_bwd[1:], axis=0)], axis=0)
    taps = taps / jnp.sum(jnp.abs(taps), axis=0, keepdims=True)
    return jnp.fft.rfft(taps, axis=0)


def fft_long_conv(u, filt_f):
    L = u.shape[1]
    uf = jnp.fft.rfft(u.astype(jnp.float32), n=2 * L, axis=1)
    return jnp.fft.irfft(uf * filt_f, n=2 * L, axis=1)[:, :L].astype(u.dtype)


def hyena_mixer(p, conv_w, conv_b, w1, b1, w2, b2, w3, b3, skip):
    L = p.shape[1]
    v, g1, g2 = jnp.split(short_conv(p, conv_w, conv_b), HY_ORDER + 1, axis=-1)
    filt_f = hyena_filter_freq(L, w1, b1, w2, b2, w3, b3)
    z = v
    for o, gate in enumerate((g1, g2)):
        z = gate * (fft_long_conv(z, filt_f[:, o]) + skip[o] * z)
    return z


def diff_attention(q, k, v, lam):
    s = jnp.einsum('bhmqd,bhmkd->bhmqk', q, k).astype(jnp.float32) * (HEAD_DIM ** -0.5)
    p = jax.nn.softmax(s, axis=-1)
    w = p[:, :, 0] - lam * p[:, :, 1]
    return jnp.einsum('bhqk,bhkv->bhqv', w.astype(v.dtype), v)


def diff_attention_blocked(q, k, v, lam):
    B, H, _, L, d = q.shape
    nb = L // Q_BLOCK
    qb = jnp.moveaxis(q.reshape(B, H, 2, nb, Q_BLOCK, d), 3, 0)
    out = lax.map(lambda blk: diff_attention(blk, k, v, lam), qb)
    return jnp.moveaxis(out, 0, 2).reshape(B, H, L, v.shape[-1])


def to_qk(a):
    return a.transpose(0, 2, 3, 1, 4)


def to_v(a):
    return a.transpose(0, 2, 1, 3)


def mlstm_zero_state(B):
    return (jnp.zeros((B, ML_HEADS, HEAD_DIM, HEAD_DIM), jnp.float32),
            jnp.zeros((B, ML_HEADS, HEAD_DIM), jnp.float32),
            jnp.zeros((B, ML_HEADS), jnp.float32))


def mlstm_scan(q, k, v, ig, lf, state):
    B, H, L, d = q.shape
    nc = L // ML_CHUNK

    def chunks(a):
        return jnp.moveaxis(a.reshape(a.shape[:2] + (nc, ML_CHUNK) + a.shape[3:]), 2, 0)

    tril = jnp.tril(jnp.ones((ML_CHUNK, ML_CHUNK), dtype=bool))

    def step(carry, xs):
        C, n, m = carry
        qc, kc, vc, ic, fc = xs
        b = jnp.cumsum(fc, axis=-1)
        log_d = jnp.where(tril, b[..., :, None] - b[..., None, :] + ic[..., None, :], -jnp.inf)
        log_inter = b + m[..., None]
        m_t = jnp.maximum(log_inter, jnp.max(log_d, axis=-1))
        d_w = jnp.exp(log_d - m_t[..., None])
        inter_w = jnp.exp(log_inter - m_t)
        s = jnp.einsum('bhjk,bhik->bhji', qc, kc) * d_w
        num = jnp.einsum('bhji,bhie->bhje', s, vc) + inter_w[..., None] * jnp.einsum('bhek,bhjk->bhje', C, qc)
        den = jnp.sum(s, axis=-1) + inter_w * jnp.einsum('bhk,bhjk->bhj', n, qc)
        h = num / jnp.maximum(jnp.abs(den), jnp.exp(-m_t))[..., None]
        m_new = m_t[..., -1]
        w = jnp.exp(b[..., -1:] - b + ic - m_new[..., None])
        decay = jnp.exp(b[..., -1] + m - m_new)
        C_new = decay[..., None, None] * C + jnp.einsum('bhi,bhie,bhik->bhek', w, vc, kc)
        n_new = decay[..., None] * n + jnp.einsum('bhi,bhik->bhk', w, kc)
        return (C_new, n_new, m_new), h

    state, h = lax.scan(step, state, (chunks(q), chunks(k), chunks(v), chunks(ig), chunks(lf)))
    return jnp.moveaxis(h, 0, 2).reshape(B, H, L, d), state


def mlstm_bidir(q, k, v, i_f, f_f, i_b, f_b, state_f, state_b):
    h_f, st_f = mlstm_scan(q, k, v, i_f, f_f, state_f)
    flip = lambda a: jnp.flip(a, axis=2)
    h_b, st_b = mlstm_scan(flip(q), flip(k), flip(v), flip(i_b), flip(f_b), state_b)
    return h_f + flip(h_b), st_f, st_b


def mlstm_inputs(mq, mk, mv, mg, conv_w, conv_b, gate_b):
    B, L, _ = mq.shape
    qk = jax.nn.silu(short_conv(jnp.concatenate([mq, mk], axis=-1), conv_w, conv_b))
    heads = lambda a: a.reshape(B, L, ML_HEADS, HEAD_DIM).transpose(0, 2, 1, 3).astype(jnp.float32)
    q = heads(qk[..., :ML_WIDTH])
    k = heads(qk[..., ML_WIDTH:]) * (HEAD_DIM ** -0.5)
    v = heads(mv)
    g = (mg + gate_b).astype(jnp.float32).reshape(B, L, 4, ML_HEADS).transpose(2, 0, 3, 1)
    return (q, k, v, g[0], jax.nn.log_sigmoid(g[1]), g[2], jax.nn.log_sigmoid(g[3]))


def merge_groups(y_hy, y_da, y_ml, gain, lam_init):
    return jnp.concatenate([group_rms_norm(y_hy, HY_GROUP_DIM),
                            group_rms_norm(y_da, DA_VDIM) * (1.0 - lam_init),
                            group_rms_norm(y_ml, HEAD_DIM)], axis=-1) * gain


def conv_ffn(h, up, cw, cb, down):
    u = short_conv(h @ up, cw, cb)
    a, g = jnp.split(u, 2, axis=-1)
    return (jax.nn.silu(g) * a) @ down


def layer(x, ctx, mod_x, mod_c, lam_init, rope_cos, rope_sin, w_in, w_out, hy_conv_w, hy_conv_b,
          hy_w1, hy_b1, hy_w2, hy_b2, hy_w3, hy_b3, hy_skip, da_lambda, ml_conv_w, ml_conv_b,
          ml_gate_b, mix_norm_w, ffn_up, ffn_conv_w, ffn_conv_b, ffn_down, update_ctx):
    B, L, _ = x.shape
    Lc = ctx.shape[1]
    sh1, sc1, g1, sh2, sc2, g2 = mod_x
    csh1, csc1, cg1, csh2, csc2, cg2 = mod_c
    px = jnp.split(modulate(rms_norm(x), sh1, sc1) @ w_in, COL_SPLITS, axis=-1)
    pc = jnp.split(modulate(rms_norm(ctx), csh1, csc1) @ w_in, COL_SPLITS, axis=-1)
    hy_params = (hy_conv_w, hy_conv_b, hy_w1, hy_b1, hy_w2, hy_b2, hy_w3, hy_b3, hy_skip)

    y_hy_x = hyena_mixer(px[0], *hy_params)

    lam = (jnp.exp(jnp.sum(da_lambda[0] * da_lambda[1])) -
           jnp.exp(jnp.sum(da_lambda[2] * da_lambda[3]))).astype(jnp.float32) + lam_init
    q_x = apply_axial_rope(px[1].reshape(B, L, DA_HEADS, 2, HEAD_DIM), rope_cos, rope_sin)
    k_x = apply_axial_rope(px[2].reshape(B, L, DA_HEADS, 2, HEAD_DIM), rope_cos, rope_sin)
    v_x = px[3].reshape(B, L, DA_HEADS, DA_VDIM)
    k_c = pc[2].reshape(B, Lc, DA_HEADS, 2, HEAD_DIM)
    v_c = pc[3].reshape(B, Lc, DA_HEADS, DA_VDIM)
    k_all = to_qk(jnp.concatenate([k_c, k_x], axis=1))
    v_all = to_v(jnp.concatenate([v_c, v_x], axis=1))
    y_da_x = heads_merge(diff_attention_blocked(to_qk(q_x), k_all, v_all, lam))

    ml_c = mlstm_inputs(pc[4], pc[5], pc[6], pc[8], ml_conv_w, ml_conv_b, ml_gate_b)
    ml_x = mlstm_inputs(px[4], px[5], px[6], px[8], ml_conv_w, ml_conv_b, ml_gate_b)
    h_c, st_f, st_b = mlstm_bidir(*ml_c, mlstm_zero_state(B), mlstm_zero_state(B))
    h_x, _, _ = mlstm_bidir(*ml_x, st_f, st_b)
    y_ml_x = heads_merge(h_x).astype(x.dtype) * jax.nn.sigmoid(px[7])

    x = x + g1 * (merge_groups(y_hy_x, y_da_x, y_ml_x, mix_norm_w, lam_init) @ w_out)
    x = x + g2 * conv_ffn(modulate(rms_norm(x), sh2, sc2), ffn_up, ffn_conv_w, ffn_conv_b, ffn_down)

    if update_ctx:
        y_hy_c = hyena_mixer(pc[0], *hy_params)
        q_c = to_qk(pc[1].reshape(B, Lc, DA_HEADS, 2, HEAD_DIM))
        y_da_c = heads_merge(diff_attention(q_c, to_qk(k_c), to_v(v_c), lam))
        y_ml_c = heads_merge(h_c).astype(ctx.dtype) * jax.nn.sigmoid(pc[7])
        ctx = ctx + cg1 * (merge_groups(y_hy_c, y_da_c, y_ml_c, mix_norm_w, lam_init) @ w_out)
        ctx = ctx + cg2 * conv_ffn(modulate(rms_norm(ctx), csh2, csc2), ffn_up, ffn_conv_w, ffn_conv_b, ffn_down)
    return x, ctx


def setup_inputs(seed: int = 0) -> dict:
    key = jax.random.key(seed)
    ks = iter(jax.random.split(key, 40))
    nrm = lambda shape, scale: jax.random.normal(next(ks), shape, jnp.float32) * scale
    fgate = jnp.linspace(3.0, 6.0, ML_HEADS, dtype=jnp.float32)
    ml_gate_b = jnp.concatenate([nrm((DEPTH, ML_HEADS), 0.1), fgate + nrm((DEPTH, ML_HEADS), 0.1),
                                 nrm((DEPTH, ML_HEADS), 0.1), fgate + nrm((DEPTH, ML_HEADS), 0.1)], axis=-1)
    return {
        'x': nrm((BATCH, SEQ, D_MODEL), 1.0),
        'c': nrm((BATCH, D_MODEL), 1.0),
        'ctx': nrm((BATCH, CTX_LEN, D_MODEL), 1.0),
        'c_ctx': nrm((D_MODEL,), 1.0),
        'ada_w': nrm((DEPTH, D_MODEL, 6 * D_MODEL), 0.5 * D_MODEL ** -0.5),
        'ada_b': nrm((DEPTH, 6 * D_MODEL), 0.02),
        'w_in': nrm((DEPTH, D_MODEL, IN_WIDTH), D_MODEL ** -0.5),
        'w_out': nrm((DEPTH, MIX_WIDTH, D_MODEL), MIX_WIDTH ** -0.5),
        'hy_conv_w': nrm((DEPTH, SHORT_CONV, (HY_ORDER + 1) * HY_WIDTH), SHORT_CONV ** -0.5),
        'hy_conv_b': nrm((DEPTH, (HY_ORDER + 1) * HY_WIDTH), 0.02),
        'hy_w1': nrm((DEPTH, HY_POS_DIM, HY_FFN), 1.0),
        'hy_b1': nrm((DEPTH, HY_FFN), 0.1),
        'hy_w2': nrm((DEPTH, HY_FFN, HY_FFN), HY_FFN ** -0.5),
        'hy_b2': nrm((DEPTH, HY_FFN), 0.1),
        'hy_w3': nrm((DEPTH, HY_FFN, HY_ORDER * 2 * HY_WIDTH), HY_FFN ** -0.5),
        'hy_b3': nrm((DEPTH, HY_ORDER * 2 * HY_WIDTH), 0.02),
        'hy_skip': nrm((DEPTH, HY_ORDER, HY_WIDTH), 0.5),
        'da_lambda': nrm((DEPTH, 4, HEAD_DIM), 0.1),
        'ml_conv_w': nrm((DEPTH, SHORT_CONV, 2 * ML_WIDTH), SHORT_CONV ** -0.5),
        'ml_conv_b': nrm((DEPTH, 2 * ML_WIDTH), 0.02),
        'ml_gate_b': ml_gate_b,
        'mix_norm_w': 1.0 + nrm((DEPTH, MIX_WIDTH), 0.02),
        'ffn_up': nrm((DEPTH, D_MODEL, 2 * D_FF), D_MODEL ** -0.5),
        'ffn_conv_w': nrm((DEPTH, SHORT_CONV, 2 * D_FF), SHORT_CONV ** -0.5),
        'ffn_conv_b': nrm((DEPTH, 2 * D_FF), 0.02),
        'ffn_down': nrm((DEPTH, D_FF, D_MODEL), D_FF ** -0.5),
        'final_norm_w': 1.0 + nrm((D_MODEL,), 0.02),
    }


def reference(x, c, ctx, c_ctx, ada_w, ada_b, w_in, w_out, hy_conv_w, hy_conv_b, hy_w1, hy_b1,
              hy_w2, hy_b2, hy_w3, hy_b3, hy_skip, da_lambda, ml_conv_w, ml_conv_b, ml_gate_b,
              mix_norm_w, ffn_up, ffn_conv_w, ffn_conv_b, ffn_down, final_norm_w):
    rope_cos, rope_sin = axial_rope_tables(x.shape[1])
    for l in range(DEPTH):
        mod_x = [m[:, None, :] for m in jnp.split(jax.nn.silu(c) @ ada_w[l] + ada_b[l], 6, axis=-1)]
        mod_c = jnp.split(jax.nn.silu(c_ctx) @ ada_w[l] + ada_b[l], 6, axis=-1)
        lam_init = 0.8 - 0.6 * math.exp(-0.3 * l)
        x, ctx = layer(x, ctx, mod_x, mod_c, lam_init, rope_cos, rope_sin, w_in[l], w_out[l],
                       hy_conv_w[l], hy_conv_b[l], hy_w1[l], hy_b1[l], hy_w2[l], hy_b2[l], hy_w3[l],
                       hy_b3[l], hy_skip[l], da_lambda[l], ml_conv_w[l], ml_conv_b[l], ml_gate_b[l],
                       mix_norm_w[l], ffn_up[l], ffn_conv_w[l], ffn_conv_b[l], ffn_down[l],
                       update_ctx=(l < DEPTH - 1))
    return rms_norm(x) * final_norm_w
```

```python
import math
import contextlib
import numpy as np
import ml_dtypes
import concourse.bass as bass
import concourse.mybir as mybir
from concourse.bass_utils import run_bass_kernel_spmd

F32 = mybir.dt.float32
BF16 = mybir.dt.bfloat16
AF = mybir.ActivationFunctionType
ALU = mybir.AluOpType
AX = mybir.AxisListType

DEPTH = 4
D = 1024
L = 4096
LC = 256
T = L + LC
NT = T // 128
KC = 8
DFF = 2816
INW = 3344
EPS = 1e-6
TP = T + 3
XOFF = LC + 2
COFF = 1


class Buf:
    __slots__ = ("w", "r", "name")

    def __init__(self, name=""):
        self.w = None
        self.r = {}
        self.name = name


class FW:
    NDMA = 8

    def __init__(self, nc, stack):
        self.nc = nc
        self.eng = {"pe": nc.tensor, "act": nc.scalar, "dve": nc.vector,
                    "pool": nc.gpsimd, "sp": nc.sync}
        self.sem = {}
        self.cnt = {}
        for e in ("pe", "act", "dve", "pool"):
            self.sem[e] = stack.enter_context(nc.semaphore("s_" + e))
            self.cnt[e] = 0
        self.dsem = {}
        self.dcnt = {}
        for q in ("sp", "pool"):
            self.dsem[q] = [stack.enter_context(nc.semaphore("d_%s%d" % (q, i)))
                            for i in range(self.NDMA)]
            self.dcnt[q] = 0
        self.seen = {e: {} for e in self.eng}
        self.nops = 0

    def _wait(self, engname, ev):
        key, sem, val, src = ev
        if src == "pe" and engname == "pe":
            return
        seen = self.seen[engname]
        if seen.get(key, 0) >= val:
            return
        self.eng[engname].wait_ge(sem, val)
        seen[key] = val

    def _deps(self, engname, reads, writes):
        for b in reads:
            if b.w is not None:
                self._wait(engname, b.w)
        for b in writes:
            if b.w is not None:
                self._wait(engname, b.w)
            for ev in b.r.values():
                self._wait(engname, ev)

    def _record(self, ev, reads, writes):
        for b in reads:
            b.r[ev[0]] = ev
        for b in writes:
            b.w = ev
            b.r = {}

    def op(self, engname, fn, reads=(), writes=()):
        self._deps(engname, reads, writes)
        ins = fn(self.eng[engname])
        self.cnt[engname] += 1
        ins.then_inc(self.sem[engname], 1)
        ev = (engname, self.sem[engname], self.cnt[engname], engname)
        self._record(ev, reads, writes)
        self.nops += 1
        return ev

    def dma(self, q, out, in_, reads=(), writes=(), **kw):
        i = self.dcnt[q]
        slot = i % self.NDMA
        sem = self.dsem[q][slot]
        key = "d_%s%d" % (q, slot)
        prev = 16 * (i // self.NDMA)
        if prev > 0:
            self._wait(q, (key, sem, prev, "dma"))
        self._deps(q, reads, writes)
        ins = self.eng[q].dma_start(out=out, in_=in_, **kw)
        ins.then_inc(sem, 16)
        self.dcnt[q] += 1
        ev = (key, sem, prev + 16, "dma")
        self._record(ev, reads, writes)
        self.nops += 1
        return ev

    def barrier(self):
        evs = []
        for e in ("pe", "act", "dve", "pool"):
            if self.cnt[e] > 0:
                evs.append((e, self.sem[e], self.cnt[e], e))
        for q in ("sp", "pool"):
            n = self.dcnt[q]
            for slot in range(self.NDMA):
                k = (n - slot + self.NDMA - 1) // self.NDMA
                if k > 0:
                    evs.append(("d_%s%d" % (q, slot), self.dsem[q][slot], 16 * k, "dma"))
        for e in self.eng:
            for ev in evs:
                key, sem, val, src = ev
                seen = self.seen[e]
                if seen.get(key, 0) >= val:
                    continue
                self.eng[e].wait_ge(sem, val)
                seen[key] = val


_CONST_CACHE = {}


def _bf(a):
    return np.ascontiguousarray(a.astype(ml_dtypes.bfloat16))


def _dft_tables(N, ntile):
    idx = np.arange(128 * ntile, dtype=np.int64)
    prod = (idx[:, None] * idx[None, :]) % N
    ang = prod.astype(np.float64) * (2.0 * np.pi / N)
    c = np.cos(ang).reshape(ntile, 128, ntile, 128)
    s = np.sin(ang).reshape(ntile, 128, ntile, 128)
    c = c.transpose(2, 1, 0, 3)
    s = s.transpose(2, 1, 0, 3)
    return _bf(c), _bf(s)


def _hy_pos_tables(Lh):
    t = np.linspace(0.0, 1.0, Lh, dtype=np.float32)
    pos = np.arange(Lh, dtype=np.float32)
    f = np.linspace(1e-4, 15.0, 16, dtype=np.float32)
    ang = (np.float32(2.0 * math.pi / Lh) * pos[:, None] * f).astype(np.float32)
    z = np.concatenate([t[:, None], np.cos(ang), np.sin(ang)], axis=-1).astype(np.float32)
    deltas = np.abs(np.linspace(math.log(1e-2) / 0.3, math.log(1e-2) / 1.5, 256, dtype=np.float32))
    dec = np.exp(-t[:, None] * deltas[None, :]).astype(np.float32)
    dec4 = np.concatenate([dec, dec, dec, dec], axis=1)
    dec4[0, 256:512] = 0.0
    dec4[0, 768:1024] = 0.0
    return np.ascontiguousarray(z.T), np.ascontiguousarray(dec4)


def _consts():
    if _CONST_CACHE:
        return _CONST_CACHE
    C = {}
    C["ident_bf"] = _bf(np.eye(128, dtype=np.float32))
    C["ident_f"] = np.eye(128, dtype=np.float32)
    rows = np.repeat(np.arange(64, dtype=np.float32), 64)
    cols = np.tile(np.arange(64, dtype=np.float32), 64)
    inv = (np.float32(10000.0) ** (-np.arange(16, dtype=np.float32) / np.float32(16))).astype(np.float32)
    ang = np.concatenate([rows[:, None] * inv, cols[:, None] * inv], axis=-1).astype(np.float32)
    cosv, sinv = np.cos(ang).astype(np.float32), np.sin(ang).astype(np.float32)
    ct = np.zeros((128, L), np.float32)
    st = np.zeros((128, L), np.float32)
    for m in range(2):
        for ax in range(2):
            for half in range(2):
                for f in range(16):
                    p = m * 64 + ax * 32 + half * 16 + f
                    ct[p] = cosv[:, ax * 16 + f]
                    st[p] = sinv[:, ax * 16 + f] * (-1.0 if half == 0 else 1.0)
    C["rope_c"] = ct
    C["rope_s"] = st
    zx, decx = _hy_pos_tables(L)
    zc, decc = _hy_pos_tables(LC)
    C["hy_zx"], C["hy_decx"], C["hy_zc"], C["hy_decc"] = zx, decx, zc, decc
    C["gx_c"], C["gx_s"] = _dft_tables(2 * L, 33)
    C["gc_c"], C["gc_s"] = _dft_tables(2 * LC, 3)
    for nm, Lh, nt in (("ckx", L, 33), ("ckc", LC, 3)):
        k = np.arange(128 * nt)
        ck = np.where((k == 0) | (k == Lh), 1.0, 2.0) / (2.0 * Lh)
        ck = np.where(k <= Lh, ck, 0.0)
        C[nm] = np.ascontiguousarray(ck.reshape(nt, 128).T.astype(np.float32))
    r = np.arange(128)[:, None]
    j = np.arange(512)[None, :]
    mf = np.stack([np.where(128 * o + r > j, -30000.0, 0.0) for o in range(4)], 0)
    mb = np.stack([np.where(128 * o + r < j, -30000.0, 0.0) for o in range(4)], 0)
    C["mask_f"] = _bf(mf.transpose(1, 0, 2))
    C["mask_b"] = _bf(mb.transpose(1, 0, 2))
    sel = np.zeros((16, 8, 128), np.float32)
    cmb = np.zeros((16, 8, 512), np.float32)
    for d in range(2):
        for h in range(4):
            sel[8 * d + 4 + h, d * 4 + h, :] = 1.0
            cmb[8 * d + h, d * 4 + h, :] = 1.0
            cmb[8 * d + 4 + h, d * 4 + h, :] = -1.0
    sel3 = np.zeros((128, 8, 128), np.float32)
    sel3[0:48] = np.concatenate([sel] * 3, 0)
    cmb3 = np.zeros((128, 8), np.float32)
    cmb3[0:48] = np.concatenate([cmb[:, :, 0]] * 3, 0)
    C["g_sel"], C["g_cmb8"] = _bf(sel3), _bf(cmb3)
    gm = np.zeros((16, 4), np.float32)
    gm[4:8, 0] = -1.0
    gm[12:16, 0] = -1.0
    gm[0:4, 1] = 1.0
    gm[8:12, 1] = 1.0
    gm[4:8, 2] = 1.0
    gm[12:16, 3] = 1.0
    C["g_fmask"] = gm
    _CONST_CACHE.update(C)
    return C


CONST_SPECS = {
    "ident_bf": ([128, 128], BF16), "ident_f": ([128, 128], F32),
    "rope_c": ([128, L], F32), "rope_s": ([128, L], F32),
    "hy_zx": ([33, L], F32), "hy_decx": ([L, 1024], F32),
    "hy_zc": ([33, LC], F32), "hy_decc": ([LC, 1024], F32),
    "gx_c": ([33, 128, 33, 128], BF16), "gx_s": ([33, 128, 33, 128], BF16),
    "gc_c": ([3, 128, 3, 128], BF16), "gc_s": ([3, 128, 3, 128], BF16),
    "ckx": ([128, 33], F32), "ckc": ([128, 3], F32),
    "mask_f": ([128, 4, 512], BF16), "mask_b": ([128, 4, 512], BF16),
    "g_sel": ([128, 8, 128], BF16), "g_cmb8": ([128, 8], BF16), "g_fmask": ([16, 4], F32),
}

WEIGHT_SPECS = {
    "ada_w": [DEPTH, D, 6 * D], "ada_b": [DEPTH, 6 * D], "w_in": [DEPTH, D, INW], "w_in_sw": [DEPTH, D, 1024],
    "w_out": [DEPTH, D, D], "hy_conv_w": [DEPTH, 3, 768], "hy_conv_b": [DEPTH, 768],
    "hy_w1": [DEPTH, 33, 64], "hy_b1": [DEPTH, 64], "hy_w2": [DEPTH, 64, 64], "hy_b2": [DEPTH, 64],
    "hy_w3": [DEPTH, 64, 1024], "hy_b3": [DEPTH, 1024], "hy_skip": [DEPTH, 2, 256],
    "da_lambda": [DEPTH, 4, 64], "ml_conv_w": [DEPTH, 3, 512], "ml_conv_b": [DEPTH, 512],
    "ml_gate_b": [DEPTH, 16], "mix_norm_w": [DEPTH, D], "ffn_up": [DEPTH, D, 2 * DFF],
    "ffn_conv_w": [DEPTH, 3, 2 * DFF], "ffn_conv_b": [DEPTH, 2 * DFF], "ffn_down": [DEPTH, DFF, D],
    "final_norm_w": [D],
}


def tcol(t):
    return t * 128 + (1 if t < 2 else 2)


QGROUPS = [(0, 256, 0, 2)] + [(256 + 512 * g, 512, 2 + 4 * g, 4) for g in range(8)]
FGROUPS = [(1, 256, 0)] + [(XOFF + 512 * g, 512, 256 + 512 * g) for g in range(8)]


def build_program(layers=(0, 1, 2, 3), dbg=(), stop_after=None, final=True, run_phases=('da', 'ml', 'hy')):
    nc = bass.Bass("TRN2", target_bir_lowering=False)
    I = {}
    I["x"] = nc.dram_tensor("x", [L, D], F32, kind="ExternalInput").ap()
    I["ctx"] = nc.dram_tensor("ctx", [LC, D], F32, kind="ExternalInput").ap()
    I["c2"] = nc.dram_tensor("c2", [2, D], F32, kind="ExternalInput").ap()
    for k, shp in WEIGHT_SPECS.items():
        I[k] = nc.dram_tensor(k, shp, F32, kind="ExternalInput").ap()
    for k, (shp, dt) in CONST_SPECS.items():
        I[k] = nc.dram_tensor(k, shp, dt, kind="ExternalInput").ap()
    OUT = nc.dram_tensor("out", [L, D], F32, kind="ExternalOutput").ap()

    def scratch(name, shape, dt):
        kind = "ExternalOutput" if name in dbg else "Internal"
        return nc.dram_tensor(name, shape, dt, kind=kind).ap()

    S = {}
    S["xres"] = scratch("xres", [T, D], F32)
    S["modv"] = scratch("modv", [2, 6 * D], F32)
    S["HY"] = scratch("HY", [T, 768], F32)
    S["QT"] = scratch("QT", [4, 128, T], BF16)
    S["KT"] = scratch("KT", [4, 128, T], BF16)
    S["VDA"] = scratch("VDA", [T, 4, 129], BF16)
    S["MQT"] = scratch("MQT", [2, 128, T], BF16)
    S["MKT"] = scratch("MKT", [2, 128, T], BF16)
    S["VML"] = scratch("VML", [T, 4, 65], BF16)
    S["MO"] = scratch("MO", [T, 256], F32)
    S["GA"] = scratch("GA", [16, T], F32)
    S["GA3"] = scratch("GA3", [48, T], BF16)
    S["YMIX"] = scratch("YMIX", [T, D], F32)
    S["Z2"] = scratch("Z2", [T, 256], F32)
    S["HFX"] = scratch("HFX", [33 * 128, 2, 512], F32)
    S["HFC"] = scratch("HFC", [3 * 128, 2, 512], F32)
    S["ACTT"] = scratch("ACTT", [DFF, T], BF16)
    B = {k: Buf(k) for k in S}
    BOUT = Buf("out")

    with contextlib.ExitStack() as gst:
        fw = FW(nc, gst)
        uid = [0]

        def sb(st, name, shape, dt):
            uid[0] += 1
            return st.enter_context(nc.sbuf_tensor("%s_%d" % (name, uid[0]), shape, dt))
        PS = [gst.enter_context(nc.psum_tensor("ps%d" % i, [128, 512], F32)) for i in range(8)]
        BPS = [Buf("ps%d" % i) for i in range(8)]
        ident_bf = sb(gst, "ident_bf", [128, 128], BF16)
        ident_f = sb(gst, "ident_f", [128, 128], F32)
        Bconst = Buf("const")
        fw.dma("sp", ident_bf[:], I["ident_bf"], writes=[Bconst])
        fw.dma("sp", ident_f[:], I["ident_f"], writes=[Bconst])
        fw.dma("sp", S["xres"][0:LC, :], I["ctx"], writes=[B["xres"]])
        for i in range(4):
            fw.dma("sp", S["xres"][LC + i * 1024: LC + (i + 1) * 1024, :], I["x"][i * 1024:(i + 1) * 1024, :],
                   writes=[B["xres"]])

        def mm(out, lhsT, rhs, start, stop, reads, writes):
            fw.op("pe", lambda e: e.matmul(out, lhsT=lhsT, rhs=rhs, start=start, stop=stop), reads, writes)

        def phase_mod(l):
            with contextlib.ExitStack() as st:
                c2T = sb(st, "c2T", [128, KC, 2], F32)
                sT = sb(st, "sT", [128, KC, 2], BF16)
                adab = sb(st, "adab", [2, 6 * D], F32)
                modv = sb(st, "modv_s", [2, 6 * D], F32)
                aw = [sb(st, "aw%d" % i, [128, KC, 512], BF16) for i in range(2)]
                Bc2T, BsT, Badab, Bmodv = Buf(), Buf(), Buf(), Buf()
                Baw = [Buf(), Buf()]
                for r in range(2):
                    fw.dma("sp", c2T[:, :, r], I["c2"][r, :].rearrange("(k p) -> p k", p=128), writes=[Bc2T],
                           allow_slow_non_contiguous=True)
                fw.dma("sp", adab[:], I["ada_b"][l, :].partition_broadcast(2), writes=[Badab])
                fw.op("act", lambda e: e.activation(out=sT[:], in_=c2T[:], func=AF.Silu), [Bc2T], [BsT])
                for j in range(12):
                    w = aw[j % 2]
                    fw.dma("pool", w[:], I["ada_w"][l, :, j * 512:(j + 1) * 512].rearrange("(k p) c -> p k c", p=128),
                           writes=[Baw[j % 2]])
                    pb = j % 2
                    for kc in range(KC):
                        mm(PS[pb][0:2, :], sT[:, kc, :], w[:, kc, :], kc == 0, kc == KC - 1,
                           [BsT, Baw[j % 2]], [BPS[pb]])
                    fw.op("dve", lambda e: e.tensor_tensor(out=modv[:, j * 512:(j + 1) * 512], in0=PS[pb][0:2, :],
                                                           in1=adab[:, j * 512:(j + 1) * 512], op=ALU.add),
                          [BPS[pb], Badab], [Bmodv])
                for seg in (1, 4):
                    fw.op("dve", lambda e: e.tensor_scalar_add(out=modv[:, seg * D:(seg + 1) * D],
                                                               in0=modv[:, seg * D:(seg + 1) * D], scalar1=1.0),
                          [Bmodv], [Bmodv])
                fw.dma("sp", S["modv"], modv[:], reads=[Bmodv], writes=[B["modv"]])
            fw.barrier()

        def load_mod(st, name, seg):
            tx = sb(st, name + "x", [128, D], F32)
            tc_ = sb(st, name + "c", [128, D], F32)
            Bt = Buf()
            fw.dma("sp", tx[:], S["modv"][0, seg * D:(seg + 1) * D].partition_broadcast(128), reads=[B["modv"]], writes=[Bt])
            fw.dma("sp", tc_[:], S["modv"][1, seg * D:(seg + 1) * D].partition_broadcast(128), reads=[B["modv"]], writes=[Bt])
            return tx, tc_, Bt

        def phase_norm(st0, l, which, hT, BhT, tiles):
            with contextlib.ExitStack() as st:
                shx, shc, Bsh = load_mod(st, "sh", 0 if which == 0 else 3)
                scx, scc, Bsc = load_mod(st, "sc", 1 if which == 0 else 4)
                xt = [sb(st, "xt%d" % i, [128, D], F32) for i in range(2)]
                junk = sb(st, "junk", [128, D], BF16)
                tmp = sb(st, "ntmp", [128, D], F32)
                hn = [sb(st, "hn%d" % i, [128, D], BF16) for i in range(2)]
                ss = [sb(st, "ss%d" % i, [128, 2], F32) for i in range(2)]
                Bxt, Bhn, Bss = [Buf(), Buf()], [Buf(), Buf()], [Buf(), Buf()]
                Bjunk, Btmp = Buf(), Buf()
                for it, t in enumerate(tiles):
                    i = it % 2
                    sh, sc = (shc, scc) if t < 2 else (shx, scx)
                    fw.dma("sp", xt[i][:], S["xres"][t * 128:(t + 1) * 128, :], reads=[B["xres"]], writes=[Bxt[i]])
                    fw.op("act", lambda e: e.activation(out=junk[:], in_=xt[i][:], func=AF.Square,
                                                        accum_out=ss[i][:, 0:1]), [Bxt[i]], [Bjunk, Bss[i]])
                    fw.op("dve", lambda e: e.tensor_scalar(out=ss[i][:, 1:2], in0=ss[i][:, 0:1], scalar1=1.0 / D,
                                                           scalar2=EPS, op0=ALU.mult, op1=ALU.add), [Bss[i]], [Bss[i]])
                    fw.op("act", lambda e: e.activation(out=ss[i][:, 1:2], in_=ss[i][:, 1:2], func=AF.Sqrt), [Bss[i]], [Bss[i]])
                    fw.op("dve", lambda e: e.reciprocal(out=ss[i][:, 0:1], in_=ss[i][:, 1:2]), [Bss[i]], [Bss[i]])
                    fw.op("dve", lambda e: e.scalar_tensor_tensor(out=tmp[:], in0=xt[i][:], scalar=ss[i][:, 0:1],
                                                                  in1=sc[:], op0=ALU.mult, op1=ALU.mult),
                          [Bxt[i], Bss[i], Bsc], [Btmp])
                    fw.op("dve", lambda e: e.tensor_tensor(out=hn[i][:], in0=tmp[:], in1=sh[:], op=ALU.add),
                          [Btmp, Bsh], [Bhn[i]])
                    pb = 6 + i
                    pT = PS[pb][:].bitcast(BF16)
                    for kc in range(KC):
                        fw.op("pe", lambda e: e.transpose(pT[:, kc * 128:(kc + 1) * 128], hn[i][:, kc * 128:(kc + 1) * 128],
                                                          ident_bf[:]), [Bhn[i], Bconst], [BPS[pb]])
                    c0 = tcol(t)
                    fw.op("act", lambda e: e.activation(out=hT[:, :, c0:c0 + 128],
                                                        in_=pT.rearrange("p (k c) -> p k c", k=KC), func=AF.Copy),
                          [BPS[pb]], [BhT])

        def phase_inproj(l, hT, BhT):
            W = I["w_in"]
            with contextlib.ExitStack() as st:
                w32 = sb(st, "w32", [128, KC, 768], F32)
                taps = sb(st, "taps", [128, 3, 768], F32)
                hyb = sb(st, "hyb", [128, 768], F32)
                wj = [sb(st, "wj%d" % j, [128, KC, 768], BF16) for j in range(3)]
                wv = sb(st, "wv", [128, KC, 1024], BF16)
                Bw32, Btaps, Bhyb, Bwj, Bwv = Buf(), Buf(), Buf(), Buf(), Buf()
                fw.dma("sp", w32[:], W[l, :, 0:768].rearrange("(k p) c -> p k c", p=128), writes=[Bw32])
                for j in range(3):
                    fw.dma("sp", taps[:, j, :], I["hy_conv_w"][l, j, :].partition_broadcast(128), writes=[Btaps])
                fw.dma("sp", hyb[:], I["hy_conv_b"][l, :].partition_broadcast(128), writes=[Bhyb])
                fw.dma("pool", wv[:, :, 0:512], W[l, :, 1792:2304].rearrange("(k p) c -> p k c", p=128), writes=[Bwv])
                fw.dma("pool", wv[:, :, 512:1024], W[l, :, 2816:3328].rearrange("(k p) c -> p k c", p=128), writes=[Bwv])
                for j in range(3):
                    for kc in range(KC):
                        fw.op("dve", lambda e: e.tensor_tensor(out=wj[j][:, kc, :], in0=w32[:, kc, :], in1=taps[:, j, :],
                                                               op=ALU.mult), [Bw32, Btaps], [Bwj])
                hyo = [sb(st, "hyo%d" % i, [128, 768], F32) for i in range(2)]
                vda = [sb(st, "vda%d" % i, [128, 4, 129], BF16) for i in range(2)]
                vml = [sb(st, "vml%d" % i, [128, 4, 65], BF16) for i in range(2)]
                mo = [sb(st, "mo%d" % i, [128, 256], F32) for i in range(2)]
                Bhyo, Bvda, Bvml, Bmo = [Buf(), Buf()], [Buf(), Buf()], [Buf(), Buf()], [Buf(), Buf()]
                for i in range(2):
                    fw.op("dve", lambda e: e.memset(vda[i][:, :, 128:129], 1.0), [], [Bvda[i]])
                    fw.op("dve", lambda e: e.memset(vml[i][:, :, 64:65], 1.0), [], [Bvml[i]])
                for t in range(NT):
                    i = t % 2
                    c0 = tcol(t)
                    for cc, (a, n) in enumerate(((0, 512), (512, 256))):
                        pb = cc
                        cnt = 0
                        for j in range(3):
                            for kc in range(KC):
                                mm(PS[pb][:, 0:n], hT[:, kc, c0 + j - 1:c0 + j - 1 + 128], wj[j][:, kc, a:a + n],
                                   cnt == 0, cnt == 23, [BhT, Bwj], [BPS[pb]])
                                cnt += 1
                        fw.op("dve", lambda e: e.tensor_tensor(out=hyo[i][:, a:a + n], in0=PS[pb][:, 0:n],
                                                               in1=hyb[:, a:a + n], op=ALU.add), [BPS[pb], Bhyb], [Bhyo[i]])
                    fw.dma("sp", S["HY"][t * 128:(t + 1) * 128, :], hyo[i][:], reads=[Bhyo[i]], writes=[B["HY"]])
                    for cc in range(2):
                        pb = 2 + cc
                        for kc in range(KC):
                            mm(PS[pb][:, :], hT[:, kc, c0:c0 + 128], wv[:, kc, cc * 512:(cc + 1) * 512],
                               kc == 0, kc == KC - 1, [BhT, Bwv], [BPS[pb]])
                    fw.op("act", lambda e: e.activation(out=vda[i][:, :, 0:128],
                                                        in_=PS[2][:, :].rearrange("p (h c) -> p h c", h=4), func=AF.Copy),
                          [BPS[2]], [Bvda[i]])
                    fw.op("act", lambda e: e.activation(out=vml[i][:, :, 0:64],
                                                        in_=PS[3][:, 0:256].rearrange("p (h c) -> p h c", h=4), func=AF.Copy),
                          [BPS[3]], [Bvml[i]])
                    fw.op("act", lambda e: e.activation(out=mo[i][:], in_=PS[3][:, 256:512], func=AF.Sigmoid),
                          [BPS[3]], [Bmo[i]])
                    fw.dma("sp", S["VDA"][t * 128:(t + 1) * 128, :, :], vda[i][:], reads=[Bvda[i]], writes=[B["VDA"]])
                    fw.dma("sp", S["VML"][t * 128:(t + 1) * 128, :, :], vml[i][:], reads=[Bvml[i]], writes=[B["VML"]])
                    fw.dma("sp", S["MO"][t * 128:(t + 1) * 128, :], mo[i][:], reads=[Bmo[i]], writes=[B["MO"]])
            fw.barrier()
            with contextlib.ExitStack() as st:
                wt = [sb(st, "wt%d" % i, [128, KC, 128], BF16) for i in range(2)]
                Bwt = [Buf(), Buf()]
                rowA = sb(st, "rowA", [128, TP], F32)
                rowB = sb(st, "rowB", [128, TP], F32)
                BrowA, BrowB = Buf(), Buf()
                fw.op("dve", lambda e: e.memset(rowA[:], 0.0), [], [BrowA])
                fw.op("dve", lambda e: e.memset(rowB[:], 0.0), [], [BrowB])
                rcf = sb(st, "rope_c", [128, T], F32)
                rsf = sb(st, "rope_s", [128, T], F32)
                rc = rcf[:, 0:L]
                rs = rsf[:, 0:L]
                Brope = Buf()
                fw.dma("sp", rc, I["rope_c"], writes=[Brope])
                fw.dma("sp", rs, I["rope_s"], writes=[Brope])
                t1f = sb(st, "rt1", [128, T], F32)
                t1 = t1f[:, 0:L]
                Bt1 = Buf()
                orow = [sb(st, "orow%d" % i, [128, T], BF16) for i in range(2)]
                Borow = [Buf(), Buf()]
                cw = sb(st, "cw", [128, 4], F32)
                Bcw = Buf()
                state = {"n": 0, "o": 0}

                def fm_chunk(wsrc, M, row, Brow):
                    k = state["n"] % 2
                    state["n"] += 1
                    fw.dma("pool", wt[k][:, :, 0:M], wsrc.rearrange("(k p) c -> p k c", p=128), writes=[Bwt[k]])
                    for gi, (c0, n, s0) in enumerate(FGROUPS):
                        pb = gi % 2
                        for kc in range(KC):
                            mm(PS[pb][0:M, 0:n], wt[k][:, kc, 0:M], hT[:, kc, c0:c0 + n], kc == 0, kc == KC - 1,
                               [Bwt[k], BhT], [BPS[pb]])
                        fw.op("act", lambda e: e.activation(out=row[0:M, c0:c0 + n], in_=PS[pb][0:M, 0:n], func=AF.Copy),
                              [BPS[pb]], [Brow])

                for kind, col0, sw0, dst in (("q", 768, 0, "QT"), ("k", 1280, 512, "KT")):
                    for h in range(4):
                        fm_chunk(W[l, :, col0 + h * 128: col0 + (h + 1) * 128], 128, rowA, BrowA)
                        fm_chunk(I["w_in_sw"][l, :, sw0 + h * 128: sw0 + (h + 1) * 128], 128, rowB, BrowB)
                        o = state["o"] % 2
                        state["o"] += 1
                        fw.op("dve", lambda e: e.tensor_tensor(out=t1, in0=rowA[:, XOFF:XOFF + L], in1=rc, op=ALU.mult),
                              [BrowA, Brope], [Bt1])
                        fw.op("pool", lambda e: e.tensor_tensor(out=rowB[:, XOFF:XOFF + L], in0=rowB[:, XOFF:XOFF + L],
                                                                in1=rs, op=ALU.mult), [BrowB, Brope], [BrowB])
                        fw.op("dve", lambda e: e.tensor_tensor(out=orow[o][:, LC:T], in0=t1, in1=rowB[:, XOFF:XOFF + L],
                                                               op=ALU.add), [Bt1, BrowB], [Borow[o]])
                        fw.op("act", lambda e: e.activation(out=orow[o][:, 0:LC], in_=rowA[:, COFF:COFF + LC], func=AF.Copy),
                              [BrowA], [Borow[o]])
                        fw.dma("sp", S[dst][h, :, :], orow[o][:], reads=[Borow[o]], writes=[B[dst]])
                for kind, col0, cw0, dst in (("q", 2304, 0, "MQT"), ("k", 2560, 256, "MKT")):
                    for c in range(2):
                        fm_chunk(W[l, :, col0 + c * 128: col0 + (c + 1) * 128], 128, rowA, BrowA)
                        fw.dma("sp", cw[:, 0:3], I["ml_conv_w"][l, :, cw0 + c * 128: cw0 + (c + 1) * 128].rearrange("j p -> p j"),
                               writes=[Bcw], allow_slow_non_contiguous=True)
                        fw.dma("sp", cw[:, 3:4], I["ml_conv_b"][l, cw0 + c * 128: cw0 + (c + 1) * 128].rearrange("(p o) -> p o", o=1),
                               writes=[Bcw], allow_slow_non_contiguous=True)
                        acc = rowB[:, 0:TP - 2]
                        fw.op("dve", lambda e: e.tensor_scalar(out=acc, in0=rowA[:, 0:TP - 2], scalar1=cw[:, 0:1],
                                                               scalar2=cw[:, 3:4], op0=ALU.mult, op1=ALU.add),
                              [BrowA, Bcw], [BrowB])
                        for j in (1, 2):
                            fw.op("dve", lambda e: e.scalar_tensor_tensor(out=acc, in0=rowA[:, j:TP - 2 + j], scalar=cw[:, j:j + 1],
                                                                          in1=acc, op0=ALU.mult, op1=ALU.add),
                                  [BrowA, Bcw, BrowB], [BrowB])
                        o = state["o"] % 2
                        state["o"] += 1
                        fw.op("act", lambda e: e.activation(out=orow[o][:, 0:LC], in_=rowB[:, 0:LC], func=AF.Silu),
                              [BrowB], [Borow[o]])
                        fw.op("act", lambda e: e.activation(out=orow[o][:, LC:T], in_=rowB[:, LC + 1:LC + 1 + L], func=AF.Silu),
                              [BrowB], [Borow[o]])
                        fw.dma("sp", S[dst][c, :, :], orow[o][:], reads=[Borow[o]], writes=[B[dst]])
                        if kind == "q" and c == 1:
                            pass
                fm_chunk(W[l, :, 3328:3344], 16, rowA, BrowA)
                gsb = sb(st, "gsb", [16, 8], F32)
                Bgsb = Buf()
                fw.dma("sp", gsb[:, 0:1], I["ml_gate_b"][l, :].rearrange("(p o) -> p o", o=1), writes=[Bgsb],
                       allow_slow_non_contiguous=True)
                fw.dma("sp", gsb[:, 2:6], I["g_fmask"], writes=[Bgsb])
                g = t1f[0:16, 0:T]
                fw.op("dve", lambda e: e.tensor_scalar(out=g[:, 0:LC], in0=rowA[0:16, COFF:COFF + LC], scalar1=gsb[:, 0:1],
                                                       scalar2=None, op0=ALU.add), [BrowA, Bgsb], [Bt1])
                fw.op("dve", lambda e: e.tensor_scalar(out=g[:, LC:T], in0=rowA[0:16, XOFF:XOFF + L], scalar1=gsb[:, 0:1],
                                                       scalar2=None, op0=ALU.add), [BrowA, Bgsb], [Bt1])
                e1 = rowB[0:16, 0:T]
                fw.op("act", lambda e: e.activation(out=e1, in_=g, func=AF.Exp, scale=-1.0), [Bt1], [BrowB])
                fw.op("act", lambda e: e.activation(out=e1, in_=e1, func=AF.Ln, bias=1.0), [BrowB], [BrowB])
                fw.op("dve", lambda e: e.tensor_scalar(out=e1, in0=e1, scalar1=gsb[:, 2:3], scalar2=None, op0=ALU.mult),
                      [BrowB, Bgsb], [BrowB])
                a0 = rowA[0:16, 0:T]
                fw.op("dve", lambda e: e.scalar_tensor_tensor(out=a0, in0=g, scalar=gsb[:, 3:4], in1=e1, op0=ALU.mult,
                                                              op1=ALU.add), [Bt1, Bgsb, BrowB], [BrowA])
                ones = rcf[0:16, 0:T]
                fw.op("dve", lambda e: e.memset(ones, 1.0), [], [Brope])
                F = t1f[0:16, 0:T]
                fw.op("dve", lambda e: e.tensor_tensor_scan(out=rsf[0:16, 0:T], data0=ones, data1=e1, initial=0.0,
                                                            op0=ALU.mult, op1=ALU.add), [Brope, BrowB], [Brope])
                Fc = rsf[0:16, 0:T]
                fw.op("dve", lambda e: e.tensor_tensor(out=e1, in0=e1, in1=Fc, op=ALU.subtract), [BrowB, Brope], [BrowB])
                fw.op("dve", lambda e: e.tensor_tensor(out=gsb[:, 1:2], in0=Fc[:, LC - 1:LC], in1=Fc[:, T - 1:T], op=ALU.subtract),
                      [Brope, Bgsb], [Bgsb])
                fw.op("dve", lambda e: e.tensor_scalar(out=e1[:, 0:LC], in0=e1[:, 0:LC], scalar1=gsb[:, 1:2], scalar2=None,
                                                       op0=ALU.add), [BrowB, Bgsb], [BrowB])
                fw.op("dve", lambda e: e.tensor_scalar(out=e1[:, LC:T], in0=e1[:, LC:T], scalar1=Fc[:, LC - 1:LC], scalar2=None,
                                                       op0=ALU.add), [BrowB, Brope], [BrowB])
                fw.op("dve", lambda e: e.tensor_scalar(out=F, in0=Fc, scalar1=gsb[:, 4:5], scalar2=None, op0=ALU.mult),
                      [Brope, Bgsb], [Bt1])
                fw.op("dve", lambda e: e.scalar_tensor_tensor(out=F, in0=a0, scalar=gsb[:, 3:4], in1=F, op0=ALU.mult, op1=ALU.add),
                      [BrowA, Bgsb, Bt1], [Bt1])
                fw.op("dve", lambda e: e.scalar_tensor_tensor(out=F, in0=e1, scalar=gsb[:, 5:6], in1=F, op0=ALU.mult, op1=ALU.add),
                      [BrowB, Bgsb, Bt1], [Bt1])
                fw.dma("sp", S["GA"], F, reads=[Bt1], writes=[B["GA"]])
                g3 = orow[0][0:16, 0:T]
                resid = rowA[0:16, 0:T]
                for part in range(3):
                    srcp = F if part == 0 else resid
                    Bsrcp = Bt1 if part == 0 else BrowA
                    fw.op("dve", lambda e: e.tensor_copy(out=g3, in_=srcp), [Bsrcp], [Borow[0]])
                    fw.dma("sp", S["GA3"][part * 16:(part + 1) * 16, :], g3, reads=[Borow[0]], writes=[B["GA3"]])
                    if part < 2:
                        fw.op("dve", lambda e: e.tensor_tensor(out=resid, in0=srcp, in1=g3, op=ALU.subtract), [Bsrcp, Borow[0]], [BrowA])
            fw.barrier()

        def phase_da(l, upd):
            lam_init = 0.8 - 0.6 * math.exp(-0.3 * l)
            with contextlib.ExitStack() as st:
                kz = [sb(st, "kz%d" % i, [128, T], BF16) for i in range(2)]
                qT = sb(st, "qT", [128, T], BF16)
                vx = sb(st, "vx", [128, NT, 129], BF16)
                BkT, BqT, Bvx = Buf(), Buf(), Buf()
                fw.op("dve", lambda e: e.memset(kz[0][64:128, :], 0.0), [], [BkT])
                fw.op("dve", lambda e: e.memset(kz[1][0:64, :], 0.0), [], [BkT])
                dl = sb(st, "dl", [128, 4, 64], F32)
                lm = sb(st, "lm", [128, 8], F32)
                Bdl, Blm = Buf(), Buf()
                fw.dma("sp", dl[:].rearrange("p a b -> p (a b)"), I["da_lambda"][l].rearrange("a b -> (a b)").partition_broadcast(128),
                       writes=[Bdl])
                for a in range(2):
                    fw.op("dve", lambda e: e.tensor_tensor(out=dl[:, 2 * a, :], in0=dl[:, 2 * a, :], in1=dl[:, 2 * a + 1, :],
                                                           op=ALU.mult), [Bdl], [Bdl])
                    fw.op("dve", lambda e: e.reduce_sum(out=lm[:, a:a + 1], in_=dl[:, 2 * a, :], axis=AX.X), [Bdl], [Blm])
                fw.op("act", lambda e: e.activation(out=lm[:, 2:4], in_=lm[:, 0:2], func=AF.Exp), [Blm], [Blm])
                fw.op("dve", lambda e: e.tensor_tensor(out=lm[:, 4:5], in0=lm[:, 3:4], in1=lm[:, 2:3], op=ALU.subtract), [Blm], [Blm])
                fw.op("dve", lambda e: e.tensor_scalar_add(out=lm[:, 5:6], in0=lm[:, 4:5], scalar1=-lam_init), [Blm], [Blm])
                nlam = lm[:, 5:6]
                P = [sb(st, "P%d" % i, [128, 512], BF16) for i in range(2)]
                BP = [Buf(), Buf()]
                o0 = sb(st, "o0", [128, 4, 128], F32)
                yb = [sb(st, "yb%d" % i, [128, 4, 128], F32) for i in range(2)]
                rec = sb(st, "rec", [128, 8], F32)
                Bo0, Byb, Brec = Buf(), [Buf(), Buf()], Buf()
                gcount = 0
                for h in range(4):
                    fw.dma("sp", kz[0][0:64, :], S["KT"][h, 0:64, :], reads=[B["KT"]], writes=[BkT])
                    fw.dma("sp", kz[1][64:128, :], S["KT"][h, 64:128, :], reads=[B["KT"]], writes=[BkT])
                    fw.dma("sp", qT[:], S["QT"][h], reads=[B["QT"]], writes=[BqT])
                    fw.dma("sp", vx[:], S["VDA"][:, h, :].rearrange("(n p) c -> p n c", p=128), reads=[B["VDA"]], writes=[Bvx])
                    for gi, (q0, n, t0, ntile) in enumerate(QGROUPS):
                        if gi == 0 and not upd:
                            continue
                        keys = [0, 1] if gi == 0 else list(range(NT))
                        y = yb[gcount % 2]
                        By = Byb[gcount % 2]
                        gcount += 1
                        for m in range(2):
                            def da_S(ki):
                                b, kt = ki % 2, keys[ki]
                                mm(PS[b][:, 0:n], kz[m][:, kt * 128:(kt + 1) * 128],
                                   qT[:, q0:q0 + n], True, True, [BkT, BqT], [BPS[b]])
                            da_S(0)
                            for ki, kt in enumerate(keys):
                                b = ki % 2
                                if ki + 1 < len(keys):
                                    da_S(ki + 1)
                                fw.op("act", lambda e: e.activation(out=P[b][:, 0:n], in_=PS[b][:, 0:n], func=AF.Exp, scale=0.125),
                                      [BPS[b]], [BP[b]])
                                for qt in range(ntile):
                                    mm(PS[2 + qt][:, 0:129], P[b][:, qt * 128:(qt + 1) * 128], vx[:, kt, :], ki == 0,
                                       ki == len(keys) - 1, [BP[b], Bvx], [BPS[2 + qt]])
                            for qt in range(ntile):
                                fw.op("dve", lambda e: e.reciprocal(out=rec[:, qt:qt + 1], in_=PS[2 + qt][:, 128:129]),
                                      [BPS[2 + qt]], [Brec])
                                if m == 0:
                                    fw.op("dve", lambda e: e.tensor_scalar(out=o0[:, qt, :], in0=PS[2 + qt][:, 0:128],
                                                                           scalar1=rec[:, qt:qt + 1], scalar2=None, op0=ALU.mult),
                                          [BPS[2 + qt], Brec], [Bo0])
                                else:
                                    fw.op("dve", lambda e: e.tensor_scalar(out=y[:, qt, :], in0=PS[2 + qt][:, 0:128],
                                                                           scalar1=rec[:, qt:qt + 1], scalar2=nlam, op0=ALU.mult,
                                                                           op1=ALU.mult), [BPS[2 + qt], Brec, Blm], [By])
                                    fw.op("dve", lambda e: e.tensor_tensor(out=y[:, qt, :], in0=y[:, qt, :], in1=o0[:, qt, :],
                                                                           op=ALU.add), [By, Bo0], [By])
                        fw.dma("sp", S["YMIX"][q0:q0 + n, 256 + h * 128:256 + (h + 1) * 128].rearrange("(n p) c -> p n c", p=128),
                               y[:, 0:ntile, :], reads=[By], writes=[B["YMIX"]])
            fw.barrier()

        def phase_ml(l, upd):
            with contextlib.ExitStack() as st:
                ga = sb(st, "ga", [128, T], BF16)
                sel = sb(st, "sel", [128, 8, 128], BF16)
                cmb8 = sb(st, "cmb8", [128, 8], BF16)
                mkf = sb(st, "mkf", [128, 4, 512], BF16)
                mkb = sb(st, "mkb", [128, 4, 512], BF16)
                cb = sb(st, "cb", [128, NT, 8], F32)
                UBs = [sb(st, "UB%d" % i, [128, 512], F32) for i in range(2)]
                Bga, Bc, Bcb, BUB = Buf(), Buf(), Buf(), [Buf(), Buf()]
                fw.op("dve", lambda e: e.memset(ga[:], 0.0), [], [Bga])
                fw.dma("sp", ga[0:48, :], S["GA3"], reads=[B["GA3"]], writes=[Bga])
                fw.dma("sp", sel[:], I["g_sel"], writes=[Bc])
                fw.dma("sp", cmb8[:], I["g_cmb8"], writes=[Bc])
                fw.dma("sp", mkf[:], I["mask_f"], writes=[Bc])
                fw.dma("sp", mkb[:], I["mask_b"], writes=[Bc])
                for t in range(NT):
                    mm(PS[0][:, t * 8:(t + 1) * 8], ga[:, t * 128:(t + 1) * 128], cmb8[:, :], True, True, [Bga, Bc], [BPS[0]])
                fw.op("act", lambda e: e.activation(out=cb[:].rearrange("p t r -> p (t r)"), in_=PS[0][:, 0:NT * 8], func=AF.Copy),
                      [BPS[0]], [Bcb])
                mq = sb(st, "mq", [128, T], BF16)
                mkz = [sb(st, "mkz%d" % i, [128, T], BF16) for i in range(2)]
                vm = sb(st, "vm", [128, NT, 65], BF16)
                Bmq, Bmk, Bvm = Buf(), Buf(), Buf()
                fw.op("dve", lambda e: e.memset(mkz[0][64:128, :], 0.0), [], [Bmk])
                fw.op("dve", lambda e: e.memset(mkz[1][0:64, :], 0.0), [], [Bmk])
                Dt = [sb(st, "Dt%d" % i, [128, 512], F32) for i in range(2)]
                P = [sb(st, "Pm%d" % i, [128, 512], BF16) for i in range(2)]
                BDt, BP = [Buf(), Buf()], [Buf(), Buf()]
                hf = sb(st, "hf", [128, 4, 64], F32)
                yb = [sb(st, "ym%d" % i, [128, 4, 64], F32) for i in range(2)]
                mot = [sb(st, "mot%d" % i, [128, 4, 64], F32) for i in range(2)]
                rec = sb(st, "mrec", [128, 8], F32)
                Bhf, Byb, Bmot, Brec = Buf(), [Buf(), Buf()], [Buf(), Buf()], Buf()
                gcount = 0
                gd = 0
                for c in range(2):
                    fw.dma("sp", mq[:], S["MQT"][c], reads=[B["MQT"]], writes=[Bmq])
                    fw.dma("sp", mkz[0][0:64, :], S["MKT"][c, 0:64, :], reads=[B["MKT"]], writes=[Bmk])
                    fw.dma("sp", mkz[1][64:128, :], S["MKT"][c, 64:128, :], reads=[B["MKT"]], writes=[Bmk])
                    for hh in range(2):
                        h = 2 * c + hh
                        fw.dma("sp", vm[:], S["VML"][:, h, :].rearrange("(n p) c -> p n c", p=128), reads=[B["VML"]], writes=[Bvm])
                        for gi, (q0, n, t0, ntile) in enumerate(QGROUPS):
                            if gi == 0 and not upd:
                                continue
                            g2 = gcount % 2
                            gcount += 1
                            y, By = yb[g2], Byb[g2]
                            fw.dma("sp", mot[g2][:, 0:ntile, :],
                                   S["MO"][q0:q0 + n, h * 64:(h + 1) * 64].rearrange("(n p) c -> p n c", p=128),
                                   reads=[B["MO"]], writes=[Bmot[g2]])
                            for d in range(2):
                                steps = []
                                if d == 0:
                                    for kt in range(0, t0):
                                        steps.append((kt, None))
                                    for o in range(ntile):
                                        steps.append((t0 + o, (mkf, o)))
                                else:
                                    if gi > 0:
                                        steps += [(0, None), (1, None)]
                                    for o in range(ntile):
                                        steps.append((t0 + o, (mkb, o)))
                                    if gi > 0:
                                        for kt in range(t0 + ntile, NT):
                                            steps.append((kt, None))
                                r = d * 4 + h
                                ub = gd % 2
                                gd += 1
                                mm(PS[2 + ub][:, 0:n], sel[:, r, :], ga[:, q0:q0 + n], True, True, [Bc, Bga], [BPS[2 + ub]])
                                fw.op("act", lambda e: e.activation(out=UBs[ub][:, 0:n], in_=PS[2 + ub][:, 0:n], func=AF.Copy),
                                      [BPS[2 + ub]], [BUB[ub]])

                                def ml_SLD(si):
                                    b = si % 2
                                    kt, msk = steps[si]
                                    mm(PS[b][:, 0:n], mkz[hh][:, kt * 128:(kt + 1) * 128],
                                       mq[:, q0:q0 + n], True, True, [Bmk, Bmq], [BPS[b]])
                                    if msk is not None:
                                        mm(PS[2 + b][:, 0:n], sel[:, r, :], ga[:, q0:q0 + n], True, False, [Bc, Bga], [BPS[2 + b]])
                                        mm(PS[2 + b][:, 0:n], ident_bf[:], msk[0][:, msk[1], 0:n], False, True, [Bc, Bconst], [BPS[2 + b]])
                                ml_SLD(0)
                                for si, (kt, msk) in enumerate(steps):
                                    b = si % 2
                                    if si + 1 < len(steps):
                                        ml_SLD(si + 1)
                                    if msk is not None:
                                        fw.op("act", lambda e: e.activation(out=Dt[b][:, 0:n], in_=PS[2 + b][:, 0:n], func=AF.Exp,
                                                                            bias=cb[:, kt, r:r + 1]), [BPS[2 + b], Bcb], [BDt[b]])
                                    else:
                                        fw.op("act", lambda e: e.activation(out=Dt[b][:, 0:n], in_=UBs[ub][:, 0:n], func=AF.Exp,
                                                                            bias=cb[:, kt, r:r + 1]), [BUB[ub], Bcb], [BDt[b]])
                                    fw.op("dve", lambda e: e.scalar_tensor_tensor(out=P[b][:, 0:n], in0=PS[b][:, 0:n], scalar=0.125,
                                                                                  in1=Dt[b][:, 0:n], op0=ALU.mult, op1=ALU.mult),
                                          [BPS[b], BDt[b]], [BP[b]])
                                    for qt in range(ntile):
                                        mm(PS[4 + qt][:, 0:65], P[b][:, qt * 128:(qt + 1) * 128], vm[:, kt, :], si == 0,
                                           si == len(steps) - 1, [BP[b], Bvm], [BPS[4 + qt]])
                                for qt in range(ntile):
                                    fw.op("act", lambda e: e.activation(out=rec[:, 4 + qt:5 + qt], in_=PS[4 + qt][:, 64:65], func=AF.Abs),
                                          [BPS[4 + qt]], [Brec])
                                    fw.op("dve", lambda e: e.tensor_scalar_max(out=rec[:, 4 + qt:5 + qt], in0=rec[:, 4 + qt:5 + qt],
                                                                               scalar1=1.0), [Brec], [Brec])
                                    fw.op("dve", lambda e: e.reciprocal(out=rec[:, qt:qt + 1], in_=rec[:, 4 + qt:5 + qt]), [Brec], [Brec])
                                    if d == 0:
                                        fw.op("dve", lambda e: e.tensor_scalar(out=hf[:, qt, :], in0=PS[4 + qt][:, 0:64],
                                                                               scalar1=rec[:, qt:qt + 1], scalar2=None, op0=ALU.mult),
                                              [BPS[4 + qt], Brec], [Bhf])
                                    else:
                                        fw.op("dve", lambda e: e.scalar_tensor_tensor(out=y[:, qt, :], in0=PS[4 + qt][:, 0:64],
                                                                                      scalar=rec[:, qt:qt + 1], in1=hf[:, qt, :],
                                                                                      op0=ALU.mult, op1=ALU.add),
                                              [BPS[4 + qt], Brec, Bhf], [By])
                                        fw.op("dve", lambda e: e.tensor_tensor(out=y[:, qt, :], in0=y[:, qt, :], in1=mot[g2][:, qt, :],
                                                                               op=ALU.mult), [By, Bmot[g2]], [By])
                            fw.dma("sp", S["YMIX"][q0:q0 + n, 768 + h * 64:768 + (h + 1) * 64].rearrange("(n p) c -> p n c", p=128),
                                   y[:, 0:ntile, :], reads=[By], writes=[B["YMIX"]])
            fw.barrier()

        def hyena_seg(l, Lh, s0, ztab, dectab, Gc, Gs, ck, HF, nk):
            npt = Lh // 128
            pn = min(512, Lh)
            TWO_PI = 2.0 * math.pi
            OFF = math.pi + TWO_PI * 16
            with contextlib.ExitStack() as st:
                w1 = sb(st, "hw1", [33, 64], F32)
                w2 = sb(st, "hw2", [64, 64], F32)
                w3 = sb(st, "hw3", [64, 1024], F32)
                bb = sb(st, "hbb", [64, 4], F32)
                b3 = sb(st, "hb3", [128, 1024], F32)
                zT = sb(st, "hzT", [33, Lh], F32)
                h1T = sb(st, "hh1", [64, Lh], F32)
                h2T = sb(st, "hh2", [64, Lh], F32)
                tmp = sb(st, "htmp", [64, 512], F32)
                tf = sb(st, "htf", [64, 512], F32)
                ti = sb(st, "hti", [64, 512], mybir.dt.int32)
                Bw, Bbb, BzT, Bh1, Bh2, Btmp, Btf, Bti = Buf(), Buf(), Buf(), Buf(), Buf(), Buf(), Buf(), Buf()
                fw.dma("sp", w1[:], I["hy_w1"][l], writes=[Bw])
                fw.dma("sp", w2[:], I["hy_w2"][l], writes=[Bw])
                fw.dma("sp", w3[:], I["hy_w3"][l], writes=[Bw])
                fw.dma("sp", bb[:, 0:1], I["hy_b1"][l, :].rearrange("(p o) -> p o", o=1), writes=[Bbb], allow_slow_non_contiguous=True)
                fw.dma("sp", bb[:, 1:2], I["hy_b2"][l, :].rearrange("(p o) -> p o", o=1), writes=[Bbb], allow_slow_non_contiguous=True)
                fw.dma("sp", b3[:], I["hy_b3"][l, :].partition_broadcast(128), writes=[Bw])
                fw.dma("sp", zT[:], I[ztab], writes=[BzT])
                fw.op("dve", lambda e: e.tensor_copy(out=bb[:, 2:4], in_=bb[:, 0:2]), [Bbb], [Bbb])
                for (w_, K_, src, Bsrc, dst, Bdst, bc) in ((w1, 33, zT, BzT, h1T, Bh1, 2), (w2, 64, h1T, Bh1, h2T, Bh2, 3)):
                    for pg in range(Lh // pn):
                        b = pg % 2
                        mm(PS[b][0:64, 0:pn], w_[:, :], src[0:K_, pg * pn:(pg + 1) * pn], True, True, [Bw, Bsrc], [BPS[b]])
                        fw.op("dve", lambda e: e.tensor_scalar(out=tmp[:, 0:pn], in0=PS[b][0:64, 0:pn], scalar1=bb[:, bc:bc + 1],
                                                               scalar2=None, op0=ALU.add), [BPS[b], Bbb], [Btmp])
                        fw.op("dve", lambda e: e.tensor_scalar(out=tmp[:, 0:pn], in0=tmp[:, 0:pn], scalar1=1.0 / TWO_PI, scalar2=16.0,
                                                               op0=ALU.mult, op1=ALU.add), [Btmp], [Btmp])
                        fw.op("dve", lambda e: e.tensor_copy(out=ti[:, 0:pn], in_=tmp[:, 0:pn]), [Btmp], [Bti])
                        fw.op("dve", lambda e: e.tensor_copy(out=tf[:, 0:pn], in_=ti[:, 0:pn]), [Bti], [Btf])
                        fw.op("dve", lambda e: e.tensor_tensor(out=tmp[:, 0:pn], in0=tmp[:, 0:pn], in1=tf[:, 0:pn], op=ALU.subtract),
                              [Btmp, Btf], [Btmp])
                        fw.op("act", lambda e: e.activation(out=tf[:, 0:pn], in_=tmp[:, 0:pn], func=AF.Sin, scale=math.pi), [Btmp], [Btf])
                        fw.op("dve", lambda e: e.tensor_scalar(out=tmp[:, 0:pn], in0=tmp[:, 0:pn], scalar1=-math.pi, scalar2=0.5 * math.pi,
                                                               op0=ALU.mult, op1=ALU.add), [Btmp], [Btmp])
                        fw.op("act", lambda e: e.activation(out=tmp[:, 0:pn], in_=tmp[:, 0:pn], func=AF.Sin), [Btmp], [Btmp])
                        fw.op("dve", lambda e: e.scalar_tensor_tensor(out=dst[:, pg * pn:(pg + 1) * pn], in0=tf[:, 0:pn], scalar=2.0,
                                                                      in1=tmp[:, 0:pn], op0=ALU.mult, op1=ALU.mult), [Btf, Btmp], [Bdst])
                Pt = sb(st, "hPt", [128, npt, 512], BF16)
                Mt = sb(st, "hMt", [128, npt, 512], BF16)
                dec = [sb(st, "hdec%d" % i, [128, 1024], F32) for i in range(2)]
                tp = sb(st, "htp", [128, 1024], F32)
                ab = sb(st, "hab", [128, 1024], F32)
                a2 = sb(st, "ha2", [128, 512], F32)
                ones_f = sb(st, "hones", [128, 128], F32)
                BPt, BMt, Bdec, Btp, Bab, Ba2, Bones = Buf(), Buf(), [Buf(), Buf()], Buf(), Buf(), Buf(), Buf()
                fw.op("dve", lambda e: e.memset(ones_f[:], 1.0), [], [Bones])
                v4 = lambda ap: ap.rearrange("p (o d c) -> p o d c", o=2, d=2)
                v3 = lambda ap: ap.rearrange("p (o c) -> p o c", o=2)
                for pt in range(npt):
                    i = pt % 2
                    fw.dma("sp", dec[i][:], I[dectab][pt * 128:(pt + 1) * 128, :], writes=[Bdec[i]])
                    for half in range(2):
                        mm(PS[half][:, :], h2T[:, pt * 128:(pt + 1) * 128], w3[:, half * 512:(half + 1) * 512], True, True,
                           [Bh2, Bw], [BPS[half]])
                        fw.op("dve", lambda e: e.tensor_tensor(out=tp[:, half * 512:(half + 1) * 512], in0=PS[half][:, :],
                                                               in1=b3[:, half * 512:(half + 1) * 512], op=ALU.add), [BPS[half], Bw], [Btp])
                    fw.op("dve", lambda e: e.tensor_tensor(out=tp[:], in0=tp[:], in1=dec[i][:], op=ALU.mult), [Btp, Bdec[i]], [Btp])
                    fw.op("act", lambda e: e.activation(out=ab[:], in_=tp[:], func=AF.Abs), [Btp], [Bab])
                    fw.op("pool", lambda e: e.tensor_tensor(out=v3(a2[:]), in0=v4(ab[:])[:, :, 0, :], in1=v4(ab[:])[:, :, 1, :], op=ALU.add),
                          [Bab], [Ba2])
                    mm(PS[2][:, :], ones_f[:], a2[:], pt == 0, pt == npt - 1, [Bones, Ba2], [BPS[2]])
                    fw.op("dve", lambda e: e.tensor_tensor(out=v3(Pt[:, pt, :]), in0=v4(tp[:])[:, :, 0, :], in1=v4(tp[:])[:, :, 1, :],
                                                           op=ALU.add), [Btp], [BPt])
                    fw.op("pool", lambda e: e.tensor_tensor(out=v3(Mt[:, pt, :]), in0=v4(tp[:])[:, :, 0, :], in1=v4(tp[:])[:, :, 1, :],
                                                            op=ALU.subtract), [Btp], [BMt])
                rn = sb(st, "hrn", [128, 512], F32)
                ckt = sb(st, "hck", [128, nk], F32)
                Brn, Bck = Buf(), Buf()
                fw.op("dve", lambda e: e.reciprocal(out=rn[:], in_=PS[2][:, :]), [BPS[2]], [Brn])
                fw.dma("sp", ckt[:], I[ck], writes=[Bck])
                gct = [sb(st, "hgc%d" % i, [128, npt, 128], BF16) for i in range(2)]
                gst = [sb(st, "hgs%d" % i, [128, npt, 128], BF16) for i in range(2)]
                hre = [sb(st, "hre%d" % i, [128, 512], F32) for i in range(2)]
                hs = [sb(st, "hs%d" % i, [128, 512], F32) for i in range(2)]
                Bg, Bh = [Buf(), Buf()], [Buf(), Buf()]
                for kt in range(nk):
                    b = kt % 2
                    fw.dma("sp", gct[b][:], I[Gc][kt, :, 0:npt, :], writes=[Bg[b]])
                    fw.dma("sp", gst[b][:], I[Gs][kt, :, 0:npt, :], writes=[Bg[b]])
                    for pt in range(npt):
                        mm(PS[4][:, :], gct[b][:, pt, :], Pt[:, pt, :], pt == 0, pt == npt - 1, [Bg[b], BPt], [BPS[4]])
                    for pt in range(npt):
                        mm(PS[5][:, :], gst[b][:, pt, :], Mt[:, pt, :], pt == 0, pt == npt - 1, [Bg[b], BMt], [BPS[5]])
                    fw.op("dve", lambda e: e.scalar_tensor_tensor(out=hre[b][:], in0=PS[4][:, :], scalar=ckt[:, kt:kt + 1], in1=rn[:],
                                                                  op0=ALU.mult, op1=ALU.mult), [BPS[4], Bck, Brn], [Bh[b]])
                    fw.op("dve", lambda e: e.scalar_tensor_tensor(out=hs[b][:], in0=PS[5][:, :], scalar=ckt[:, kt:kt + 1], in1=rn[:],
                                                                  op0=ALU.mult, op1=ALU.mult), [BPS[5], Bck, Brn], [Bh[b]])
                    fw.dma("sp", S[HF][kt * 128:(kt + 1) * 128, 0, :], hre[b][:], reads=[Bh[b]], writes=[B[HF]])
                    fw.dma("sp", S[HF][kt * 128:(kt + 1) * 128, 1, :], hs[b][:], reads=[Bh[b]], writes=[B[HF]])
            fw.barrier()
            with contextlib.ExitStack() as st:
                zt = sb(st, "czt", [128, npt, 256], BF16)
                Yre = sb(st, "cYre", [128, nk, 256], BF16)
                Ys = sb(st, "cYs", [128, nk, 256], BF16)
                Bzt, BYre, BYs = Buf(), Buf(), Buf()
                gct = [sb(st, "cgc%d" % i, [128, nk, 128], BF16) for i in range(2)]
                gst = [sb(st, "cgs%d" % i, [128, nk, 128], BF16) for i in range(2)]
                hre = [sb(st, "chre%d" % i, [128, 256], F32) for i in range(2)]
                hs = [sb(st, "chs%d" % i, [128, 256], F32) for i in range(2)]
                Bg, Bh = [Buf(), Buf()], [Buf(), Buf()]
                t1 = sb(st, "ct1", [128, 256], F32)
                t2 = sb(st, "ct2", [128, 256], F32)
                Bt1, Bt2 = Buf(), Buf()
                skb = sb(st, "cskb", [128, 2, 256], F32)
                Bskb = Buf()
                fw.dma("sp", skb[:].rearrange("p a b -> p (a b)"), I["hy_skip"][l].rearrange("a b -> (a b)").partition_broadcast(128),
                       writes=[Bskb])
                zf = [sb(st, "czf%d" % i, [128, 256], F32) for i in range(2)]
                gt = [sb(st, "cgt%d" % i, [128, 256], F32) for i in range(2)]
                zo = [sb(st, "czo%d" % i, [128, 256], F32) for i in range(2)]
                Bzf, Bgt, Bzo = [Buf(), Buf()], [Buf(), Buf()], [Buf(), Buf()]
                for pt in range(npt):
                    i = pt % 2
                    fw.dma("sp", zf[i][:], S["HY"][s0 + pt * 128:s0 + (pt + 1) * 128, 0:256], reads=[B["HY"]], writes=[Bzf[i]])
                    fw.op("act", lambda e: e.activation(out=zt[:, pt, :], in_=zf[i][:], func=AF.Copy), [Bzf[i]], [Bzt])
                for o in range(2):
                    for kt in range(nk):
                        b = kt % 2
                        fw.dma("sp", gct[b][:, 0:npt, :], I[Gc][kt, :, 0:npt, :], writes=[Bg[b]])
                        fw.dma("sp", gst[b][:, 0:npt, :], I[Gs][kt, :, 0:npt, :], writes=[Bg[b]])
                        fw.dma("sp", hre[b][:], S[HF][kt * 128:(kt + 1) * 128, 0, o * 256:(o + 1) * 256], reads=[B[HF]], writes=[Bh[b]])
                        fw.dma("sp", hs[b][:], S[HF][kt * 128:(kt + 1) * 128, 1, o * 256:(o + 1) * 256], reads=[B[HF]], writes=[Bh[b]])
                        pc, ps_ = 2 * b, 2 * b + 1
                        for pt in range(npt):
                            mm(PS[pc][:, 0:256], gct[b][:, pt, :], zt[:, pt, :], pt == 0, pt == npt - 1, [Bg[b], Bzt], [BPS[pc]])
                        for pt in range(npt):
                            mm(PS[ps_][:, 0:256], gst[b][:, pt, :], zt[:, pt, :], pt == 0, pt == npt - 1, [Bg[b], Bzt], [BPS[ps_]])
                        fw.op("dve", lambda e: e.tensor_tensor(out=t1[:], in0=PS[pc][:, 0:256], in1=hre[b][:], op=ALU.mult),
                              [BPS[pc], Bh[b]], [Bt1])
                        fw.op("dve", lambda e: e.tensor_tensor(out=t2[:], in0=PS[ps_][:, 0:256], in1=hs[b][:], op=ALU.mult),
                              [BPS[ps_], Bh[b]], [Bt2])
                        fw.op("pool", lambda e: e.tensor_tensor(out=Yre[:, kt, :], in0=t1[:], in1=t2[:], op=ALU.subtract),
                              [Bt1, Bt2], [BYre])
                        fw.op("dve", lambda e: e.tensor_tensor(out=t1[:], in0=PS[pc][:, 0:256], in1=hs[b][:], op=ALU.mult),
                              [BPS[pc], Bh[b]], [Bt1])
                        fw.op("dve", lambda e: e.tensor_tensor(out=t2[:], in0=PS[ps_][:, 0:256], in1=hre[b][:], op=ALU.mult),
                              [BPS[ps_], Bh[b]], [Bt2])
                        fw.op("pool", lambda e: e.tensor_tensor(out=Ys[:, kt, :], in0=t1[:], in1=t2[:], op=ALU.add),
                              [Bt1, Bt2], [BYs])
                    for nt in range(npt):
                        b = nt % 2
                        fw.dma("sp", gct[b][:], I[Gc][nt, :, 0:nk, :], writes=[Bg[b]])
                        fw.dma("sp", gst[b][:], I[Gs][nt, :, 0:nk, :], writes=[Bg[b]])
                        src = S["HY"][s0 + nt * 128:s0 + (nt + 1) * 128, 0:256] if o == 0 else S["Z2"][s0 + nt * 128:s0 + (nt + 1) * 128, :]
                        fw.dma("sp", zf[b][:], src, reads=[B["HY"], B["Z2"]], writes=[Bzf[b]])
                        fw.dma("sp", gt[b][:], S["HY"][s0 + nt * 128:s0 + (nt + 1) * 128, 256 * (o + 1):256 * (o + 2)], reads=[B["HY"]],
                               writes=[Bgt[b]])
                        pb = 4 + b
                        for kt in range(nk):
                            mm(PS[pb][:, 0:256], gct[b][:, kt, :], Yre[:, kt, :], kt == 0, False, [Bg[b], BYre], [BPS[pb]])
                        for kt in range(nk):
                            mm(PS[pb][:, 0:256], gst[b][:, kt, :], Ys[:, kt, :], False, kt == nk - 1, [Bg[b], BYs], [BPS[pb]])
                        fw.op("pool", lambda e: e.tensor_tensor(out=zo[b][:], in0=zf[b][:], in1=skb[:, o, :], op=ALU.mult),
                              [Bzf[b], Bskb], [Bzo[b]])
                        fw.op("dve", lambda e: e.tensor_tensor(out=zo[b][:], in0=zo[b][:], in1=PS[pb][:, 0:256], op=ALU.add),
                              [Bzo[b], BPS[pb]], [Bzo[b]])
                        fw.op("dve", lambda e: e.tensor_tensor(out=zo[b][:], in0=zo[b][:], in1=gt[b][:], op=ALU.mult),
                              [Bzo[b], Bgt[b]], [Bzo[b]])
                        if o == 0:
                            fw.dma("sp", S["Z2"][s0 + nt * 128:s0 + (nt + 1) * 128, :], zo[b][:], reads=[Bzo[b]], writes=[B["Z2"]])
                            fw.op("act", lambda e: e.activation(out=zt[:, nt, :], in_=zo[b][:], func=AF.Copy), [Bzo[b]], [Bzt])
                        else:
                            fw.dma("sp", S["YMIX"][s0 + nt * 128:s0 + (nt + 1) * 128, 0:256], zo[b][:], reads=[Bzo[b]],
                                   writes=[B["YMIX"]])
            fw.barrier()

        def phase_hyena(l, upd):
            hyena_seg(l, L, LC, "hy_zx", "hy_decx", "gx_c", "gx_s", "ckx", "HFX", 33)
            if upd:
                hyena_seg(l, LC, 0, "hy_zc", "hy_decc", "gc_c", "gc_s", "ckc", "HFC", 3)

        def transpose_into(hn_t, Bhn_t, hT, BhT, t, pb):
            pT = PS[pb][:].bitcast(BF16)
            for kc in range(KC):
                fw.op("pe", lambda e: e.transpose(pT[:, kc * 128:(kc + 1) * 128], hn_t[:, kc * 128:(kc + 1) * 128], ident_bf[:]),
                      [Bhn_t, Bconst], [BPS[pb]])
            c0 = tcol(t)
            fw.op("act", lambda e: e.activation(out=hT[:, :, c0:c0 + 128], in_=pT.rearrange("p (k c) -> p k c", k=KC), func=AF.Copy),
                  [BPS[pb]], [BhT])

        def residual_update(st, tiles_groups, seg, lhs_fn, nk, wmat, Bw, reads_extra):
            gx, gc, Bg = load_mod(st, "gate", seg)
            xt = [sb(st, "rxt%d" % i, [128, D], F32) for i in range(2)]
            tmp = sb(st, "rtmp", [128, D], F32)
            Bxt, Btmp = [Buf(), Buf()], Buf()
            for it, t in enumerate(tiles_groups):
                i = it % 2
                g = gc if t < 2 else gx
                fw.dma("sp", xt[i][:], S["xres"][t * 128:(t + 1) * 128, :], reads=[B["xres"]], writes=[Bxt[i]])
                for cc in range(2):
                    pb = 4 + cc
                    for k in range(nk):
                        mm(PS[pb][:, :], lhs_fn(t, k), wmat[:, k, cc * 512:(cc + 1) * 512], k == 0, k == nk - 1,
                           [Bw] + reads_extra, [BPS[pb]])
                    fw.op("dve", lambda e: e.tensor_tensor(out=tmp[:, cc * 512:(cc + 1) * 512], in0=PS[pb][:, :],
                                                           in1=g[:, cc * 512:(cc + 1) * 512], op=ALU.mult), [BPS[pb], Bg], [Btmp])
                fw.op("pool", lambda e: e.tensor_tensor(out=xt[i][:], in0=xt[i][:], in1=tmp[:], op=ALU.add), [Bxt[i], Btmp], [Bxt[i]])
                fw.dma("sp", S["xres"][t * 128:(t + 1) * 128, :], xt[i][:], reads=[Bxt[i]], writes=[B["xres"]])

        def phase_merge(l, upd, hT, BhT):
            lam_init = 0.8 - 0.6 * math.exp(-0.3 * l)
            tiles = list(range(NT)) if upd else list(range(2, NT))
            with contextlib.ExitStack() as st:
                gw = sb(st, "gw", [128, D], F32)
                Bgw = Buf()
                fw.dma("sp", gw[:], I["mix_norm_w"][l, :].partition_broadcast(128), writes=[Bgw])
                fw.op("dve", lambda e: e.tensor_scalar(out=gw[:, 256:768], in0=gw[:, 256:768], scalar1=1.0 - lam_init, scalar2=None,
                                                       op0=ALU.mult), [Bgw], [Bgw])
                ym = [sb(st, "ym%d" % i, [128, D], F32) for i in range(2)]
                sq = sb(st, "sq", [128, D], F32)
                hn = [sb(st, "mhn%d" % i, [128, D], BF16) for i in range(2)]
                s1 = sb(st, "s1", [128, 16], F32)
                s2 = sb(st, "s2", [128, 16], F32)
                t4 = sb(st, "t4", [128, 4], F32)
                Bym, Bhn = [Buf(), Buf()], [Buf(), Buf()]
                Bsq, Bs1, Bs2, Bt4 = Buf(), Buf(), Buf(), Buf()
                for it, t in enumerate(tiles):
                    i = it % 2
                    fw.dma("sp", ym[i][:], S["YMIX"][t * 128:(t + 1) * 128, :], reads=[B["YMIX"]], writes=[Bym[i]])
                    fw.op("pool", lambda e: e.tensor_tensor(out=sq[:], in0=ym[i][:], in1=ym[i][:], op=ALU.mult), [Bym[i]], [Bsq])
                    fw.op("dve", lambda e: e.reduce_sum(out=s1[:], in_=sq[:].rearrange("p (g c) -> p g c", c=64), axis=AX.X),
                          [Bsq], [Bs1])
                    fw.op("dve", lambda e: e.reduce_sum(out=t4[:], in_=s1[:, 4:12].rearrange("p (g c) -> p g c", c=2), axis=AX.X),
                          [Bs1], [Bt4])
                    fw.op("dve", lambda e: e.tensor_scalar(out=s2[:], in0=s1[:], scalar1=1.0 / 64, scalar2=EPS, op0=ALU.mult,
                                                           op1=ALU.add), [Bs1], [Bs2])
                    for j in range(2):
                        fw.op("dve", lambda e: e.tensor_scalar(out=s2[:, 4 + j:12:2], in0=t4[:], scalar1=1.0 / 128, scalar2=EPS,
                                                               op0=ALU.mult, op1=ALU.add), [Bt4, Bs2], [Bs2])
                    fw.op("act", lambda e: e.activation(out=s2[:], in_=s2[:], func=AF.Sqrt), [Bs2], [Bs2])
                    fw.op("dve", lambda e: e.reciprocal(out=s1[:], in_=s2[:]), [Bs2, Bs1], [Bs1])
                    for gq in range(16):
                        eng = "dve"
                        fw.op(eng, lambda e: e.scalar_tensor_tensor(out=hn[i][:, gq * 64:(gq + 1) * 64], in0=ym[i][:, gq * 64:(gq + 1) * 64],
                                                                    scalar=s1[:, gq:gq + 1], in1=gw[:, gq * 64:(gq + 1) * 64],
                                                                    op0=ALU.mult, op1=ALU.mult), [Bym[i], Bs1, Bgw], [Bhn[i]])
                    transpose_into(hn[i], Bhn[i], hT, BhT, t, 6 + i)
            fw.barrier()
            with contextlib.ExitStack() as st:
                wo = sb(st, "wo", [128, KC, D], BF16)
                Bwo = Buf()
                fw.dma("pool", wo[:], I["w_out"][l].rearrange("(k p) c -> p k c", p=128), writes=[Bwo])
                residual_update(st, tiles, 2, lambda t, k: hT[:, k, tcol(t):tcol(t) + 128], KC, wo, Bwo, [BhT])
            fw.barrier()

        def phase_ffn(l, upd, hT, BhT):
            groups = FGROUPS if upd else FGROUPS[1:]
            tiles = list(range(NT)) if upd else list(range(2, NT))
            with contextlib.ExitStack() as st:
                wt = [sb(st, "fwt%d" % i, [128, KC, 128], BF16) for i in range(2)]
                Bwt = [Buf(), Buf()]
                rows = [sb(st, "frow%d" % i, [128, TP], F32) for i in range(2)]
                accs = [sb(st, "facc%d" % i, [128, TP], F32) for i in range(2)]
                Brows, Baccs = [Buf(), Buf()], [Buf(), Buf()]
                for i in range(2):
                    fw.op("dve", lambda e: e.memset(rows[i][:], 0.0), [], [Brows[i]])
                cw = [sb(st, "fcw%d" % i, [128, 4], F32) for i in range(2)]
                Bcw = [Buf(), Buf()]
                orow = [sb(st, "forow%d" % i, [128, T], BF16) for i in range(2)]
                Borow = [Buf(), Buf()]
                for ci in range(DFF // 128):
                    for w_, col0 in ((0, ci * 128), (1, DFF + ci * 128)):
                        fw.dma("pool", wt[w_][:], I["ffn_up"][l, :, col0:col0 + 128].rearrange("(k p) c -> p k c", p=128),
                               writes=[Bwt[w_]])
                        fw.dma("sp", cw[w_][:, 0:3], I["ffn_conv_w"][l, :, col0:col0 + 128].rearrange("j p -> p j"),
                               writes=[Bcw[w_]], allow_slow_non_contiguous=True)
                        fw.dma("sp", cw[w_][:, 3:4], I["ffn_conv_b"][l, col0:col0 + 128].rearrange("(p o) -> p o", o=1),
                               writes=[Bcw[w_]], allow_slow_non_contiguous=True)
                        for gi, (c0, n, s0) in enumerate(groups):
                            pb = 2 * w_ + gi % 2
                            for kc in range(KC):
                                mm(PS[pb][:, 0:n], wt[w_][:, kc, :], hT[:, kc, c0:c0 + n], kc == 0, kc == KC - 1,
                                   [Bwt[w_], BhT], [BPS[pb]])
                            fw.op("act", lambda e: e.activation(out=rows[w_][:, c0:c0 + n], in_=PS[pb][:, 0:n], func=AF.Copy),
                                  [BPS[pb]], [Brows[w_]])
                        acc = accs[w_][:, 0:TP - 2]
                        eng = "dve"
                        fw.op(eng, lambda e: e.tensor_scalar(out=acc, in0=rows[w_][:, 0:TP - 2], scalar1=cw[w_][:, 0:1],
                                                             scalar2=cw[w_][:, 3:4], op0=ALU.mult, op1=ALU.add),
                              [Brows[w_], Bcw[w_]], [Baccs[w_]])
                        for j in (1, 2):
                            fw.op(eng, lambda e: e.scalar_tensor_tensor(out=acc, in0=rows[w_][:, j:TP - 2 + j], scalar=cw[w_][:, j:j + 1],
                                                                        in1=acc, op0=ALU.mult, op1=ALU.add),
                                  [Brows[w_], Bcw[w_], Baccs[w_]], [Baccs[w_]])
                    fw.op("act", lambda e: e.activation(out=accs[1][:, 0:TP - 2], in_=accs[1][:, 0:TP - 2], func=AF.Silu),
                          [Baccs[1]], [Baccs[1]])
                    o = ci % 2
                    fw.op("dve", lambda e: e.tensor_tensor(out=orow[o][:, 0:LC], in0=accs[0][:, 0:LC], in1=accs[1][:, 0:LC], op=ALU.mult),
                          [Baccs[0], Baccs[1]], [Borow[o]])
                    fw.op("dve", lambda e: e.tensor_tensor(out=orow[o][:, LC:T], in0=accs[0][:, LC + 1:LC + 1 + L],
                                                           in1=accs[1][:, LC + 1:LC + 1 + L], op=ALU.mult),
                          [Baccs[0], Baccs[1]], [Borow[o]])
                    fw.dma("sp", S["ACTT"][ci * 128:(ci + 1) * 128, :], orow[o][:], reads=[Borow[o]], writes=[B["ACTT"]])
            fw.barrier()

        def phase_ffn_down(l, upd):
            with contextlib.ExitStack() as st:
                wd = sb(st, "wd", [128, DFF // 128, D], BF16)
                Bwd = Buf()
                for half in range(2):
                    fw.dma("pool", wd[:, half * 11:(half + 1) * 11, :],
                           I["ffn_down"][l, half * 1408:(half + 1) * 1408, :].rearrange("(k p) c -> p k c", p=128), writes=[Bwd])
                at = [sb(st, "at%d" % i, [128, DFF // 128, 512], BF16) for i in range(2)]
                Bat = [Buf(), Buf()]
                gx, gc, Bg = load_mod(st, "g2", 5)
                xt = [sb(st, "dxt%d" % i, [128, D], F32) for i in range(2)]
                tmp = sb(st, "dtmp", [128, D], F32)
                Bxt, Btmp = [Buf(), Buf()], Buf()
                it = 0
                for gi, (q0, n, t0, ntile) in enumerate(QGROUPS):
                    if gi == 0 and not upd:
                        continue
                    a = gi % 2
                    fw.dma("sp", at[a][:, :, 0:n], S["ACTT"][:, q0:q0 + n].rearrange("(k p) t -> p k t", p=128),
                           reads=[B["ACTT"]], writes=[Bat[a]])
                    for tt in range(ntile):
                        t = t0 + tt
                        i = it % 2
                        it += 1
                        g = gc if t < 2 else gx
                        fw.dma("sp", xt[i][:], S["xres"][t * 128:(t + 1) * 128, :], reads=[B["xres"]], writes=[Bxt[i]])
                        for cc in range(2):
                            pb = 4 + cc
                            for k in range(DFF // 128):
                                mm(PS[pb][:, :], at[a][:, k, tt * 128:(tt + 1) * 128], wd[:, k, cc * 512:(cc + 1) * 512], k == 0,
                                   k == DFF // 128 - 1, [Bat[a], Bwd], [BPS[pb]])
                            fw.op("dve", lambda e: e.tensor_tensor(out=tmp[:, cc * 512:(cc + 1) * 512], in0=PS[pb][:, :],
                                                                   in1=g[:, cc * 512:(cc + 1) * 512], op=ALU.mult), [BPS[pb], Bg], [Btmp])
                        fw.op("pool", lambda e: e.tensor_tensor(out=xt[i][:], in0=xt[i][:], in1=tmp[:], op=ALU.add), [Bxt[i], Btmp], [Bxt[i]])
                        fw.dma("sp", S["xres"][t * 128:(t + 1) * 128, :], xt[i][:], reads=[Bxt[i]], writes=[B["xres"]])
            fw.barrier()

        def phase_final():
            with contextlib.ExitStack() as st:
                fnw = sb(st, "fnw", [128, D], F32)
                Bfnw = Buf()
                fw.dma("sp", fnw[:], I["final_norm_w"].partition_broadcast(128), writes=[Bfnw])
                xt = [sb(st, "fxt%d" % i, [128, D], F32) for i in range(2)]
                ot = [sb(st, "fot%d" % i, [128, D], F32) for i in range(2)]
                junk = sb(st, "fjunk", [128, D], BF16)
                ss = [sb(st, "fss%d" % i, [128, 2], F32) for i in range(2)]
                Bxt, Bot, Bss, Bjunk = [Buf(), Buf()], [Buf(), Buf()], [Buf(), Buf()], Buf()
                for t in range(2, NT):
                    i = t % 2
                    fw.dma("sp", xt[i][:], S["xres"][t * 128:(t + 1) * 128, :], reads=[B["xres"]], writes=[Bxt[i]])
                    fw.op("act", lambda e: e.activation(out=junk[:], in_=xt[i][:], func=AF.Square, accum_out=ss[i][:, 0:1]),
                          [Bxt[i]], [Bjunk, Bss[i]])
                    fw.op("dve", lambda e: e.tensor_scalar(out=ss[i][:, 1:2], in0=ss[i][:, 0:1], scalar1=1.0 / D, scalar2=EPS,
                                                           op0=ALU.mult, op1=ALU.add), [Bss[i]], [Bss[i]])
                    fw.op("act", lambda e: e.activation(out=ss[i][:, 1:2], in_=ss[i][:, 1:2], func=AF.Sqrt), [Bss[i]], [Bss[i]])
                    fw.op("dve", lambda e: e.reciprocal(out=ss[i][:, 0:1], in_=ss[i][:, 1:2]), [Bss[i]], [Bss[i]])
                    fw.op("dve", lambda e: e.scalar_tensor_tensor(out=ot[i][:], in0=xt[i][:], scalar=ss[i][:, 0:1], in1=fnw[:],
                                                                  op0=ALU.mult, op1=ALU.mult), [Bxt[i], Bss[i], Bfnw], [Bot[i]])
                    fw.dma("sp", OUT[(t - 2) * 128:(t - 1) * 128, :], ot[i][:], reads=[Bot[i]], writes=[BOUT])
            fw.barrier()

        prog = {"nc": nc, "fw": fw}
        for l in layers:
            upd = l < DEPTH - 1
            phase_mod(l)
            if stop_after == "mod":
                break
            with contextlib.ExitStack() as st:
                hT = sb(st, "hT", [128, KC, TP], BF16)
                BhT = Buf("hT")
                fw.op("dve", lambda e: e.memset(hT[:], 0.0), [], [BhT])
                phase_norm(st, l, 0, hT, BhT, list(range(NT)))
                fw.barrier()
                phase_inproj(l, hT, BhT)
            fw.barrier()
            if stop_after == "inproj":
                break
            if "da" in run_phases:
                phase_da(l, upd)
            if "ml" in run_phases:
                phase_ml(l, upd)
            if "hy" in run_phases:
                phase_hyena(l, upd)
            if stop_after == "attn":
                break
            tiles = list(range(NT)) if upd else list(range(2, NT))
            with contextlib.ExitStack() as st:
                hT = sb(st, "hT", [128, KC, TP], BF16)
                BhT = Buf("hT")
                phase_merge(l, upd, hT, BhT)
            fw.barrier()
            if stop_after == "merge":
                break
            with contextlib.ExitStack() as st:
                hT = sb(st, "hT", [128, KC, TP], BF16)
                BhT = Buf("hT")
                phase_norm(st, l, 1, hT, BhT, tiles)
                fw.barrier()
                phase_ffn(l, upd, hT, BhT)
            fw.barrier()
            phase_ffn_down(l, upd)
            if stop_after == "layer":
                break
        if final and stop_after is None:
            phase_final()
        fw.barrier()
        fw.barrier()
    return nc


def _swap_perm():
    perm = np.zeros(1024, np.int64)
    for blk in range(16):
        for ax in range(2):
            for half in range(2):
                for f in range(16):
                    d = ax * 32 + half * 16 + f
                    ds = ax * 32 + (1 - half) * 16 + f
                    perm[blk * 64 + d] = blk * 64 + ds
    return perm


def make_in_maps(inputs, cores=range(8)):
    C = _consts()
    shared = {}
    for k in WEIGHT_SPECS:
        if k == "w_in_sw":
            continue
        shared[k] = np.ascontiguousarray(np.asarray(inputs[k], dtype=np.float32))
    w_in = shared["w_in"]
    shared["w_in_sw"] = np.ascontiguousarray(w_in[:, :, 768:1792][:, :, _swap_perm()])
    for k in CONST_SPECS:
        shared[k] = C[k]
    maps = []
    for b in cores:
        m = dict(shared)
        m["x"] = np.ascontiguousarray(np.asarray(inputs["x"][b], dtype=np.float32))
        m["ctx"] = np.ascontiguousarray(np.asarray(inputs["ctx"][b], dtype=np.float32))
        m["c2"] = np.ascontiguousarray(np.stack([np.asarray(inputs["c"][b], dtype=np.float32),
                                                 np.asarray(inputs["c_ctx"], dtype=np.float32)], 0))
        maps.append(m)
    return maps


_PROG = {}


def kernel(**inputs):
    if "nc" not in _PROG:
        _PROG["nc"] = build_program()
    maps = make_in_maps(inputs, cores=range(8))
    res = run_bass_kernel_spmd(_PROG["nc"], maps, core_ids=list(range(8)))
    out = np.stack([np.asarray(r["out"], dtype=np.float32) for r in res.results], 0)
    return out
```

```python
import math
import contextlib
import numpy as np
import ml_dtypes
import concourse.bass as bass
import concourse.mybir as mybir
from concourse.bass_utils import run_bass_kernel_spmd

F32 = mybir.dt.float32
BF16 = mybir.dt.bfloat16
AF = mybir.ActivationFunctionType
ALU = mybir.AluOpType
AX = mybir.AxisListType

DEPTH = 4
D = 1024
L = 4096
LC = 256
T = L + LC
NT = T // 128
KC = 8
DFF = 2816
INW = 3344
EPS = 1e-6
TP = T + 3
XOFF = LC + 2
COFF = 1


class Buf:
    __slots__ = ("w", "r", "name")

    def __init__(self, name=""):
        self.w = None
        self.r = {}
        self.name = name


class FW:
    NDMA = 8

    def __init__(self, nc, stack):
        self.nc = nc
        self.eng = {"pe": nc.tensor, "act": nc.scalar, "dve": nc.vector,
                    "pool": nc.gpsimd, "sp": nc.sync}
        self.sem = {}
        self.cnt = {}
        for e in ("pe", "act", "dve", "pool"):
            self.sem[e] = stack.enter_context(nc.semaphore("s_" + e))
            self.cnt[e] = 0
        self.dsem = {}
        self.dcnt = {}
        for q in ("sp", "pool"):
            self.dsem[q] = [stack.enter_context(nc.semaphore("d_%s%d" % (q, i)))
                            for i in range(self.NDMA)]
            self.dcnt[q] = 0
        self.seen = {e: {} for e in self.eng}
        self.nops = 0

    def _wait(self, engname, ev):
        key, sem, val, src = ev
        if src == "pe" and engname == "pe":
            return
        seen = self.seen[engname]
        if seen.get(key, 0) >= val:
            return
        self.eng[engname].wait_ge(sem, val)
        seen[key] = val

    def _deps(self, engname, reads, writes):
        for b in reads:
            if b.w is not None:
                self._wait(engname, b.w)
        for b in writes:
            if b.w is not None:
                self._wait(engname, b.w)
            for ev in b.r.values():
                self._wait(engname, ev)

    def _record(self, ev, reads, writes):
        for b in reads:
            b.r[ev[0]] = ev
        for b in writes:
            b.w = ev
            b.r = {}

    def op(self, engname, fn, reads=(), writes=()):
        self._deps(engname, reads, writes)
        ins = fn(self.eng[engname])
        self.cnt[engname] += 1
        ins.then_inc(self.sem[engname], 1)
        ev = (engname, self.sem[engname], self.cnt[engname], engname)
        self._record(ev, reads, writes)
        self.nops += 1
        return ev

    def dma(self, q, out, in_, reads=(), writes=(), **kw):
        i = self.dcnt[q]
        slot = i % self.NDMA
        sem = self.dsem[q][slot]
        key = "d_%s%d" % (q, slot)
        prev = 16 * (i // self.NDMA)
        if prev > 0:
            self._wait(q, (key, sem, prev, "dma"))
        self._deps(q, reads, writes)
        ins = self.eng[q].dma_start(out=out, in_=in_, **kw)
        ins.then_inc(sem, 16)
        self.dcnt[q] += 1
        ev = (key, sem, prev + 16, "dma")
        self._record(ev, reads, writes)
        self.nops += 1
        return ev

    def barrier(self):
        evs = []
        for e in ("pe", "act", "dve", "pool"):
            if self.cnt[e] > 0:
                evs.append((e, self.sem[e], self.cnt[e], e))
        for q in ("sp", "pool"):
            n = self.dcnt[q]
            for slot in range(self.NDMA):
                k = (n - slot + self.NDMA - 1) // self.NDMA
                if k > 0:
                    evs.append(("d_%s%d" % (q, slot), self.dsem[q][slot], 16 * k, "dma"))
        for e in self.eng:
            for ev in evs:
                key, sem, val, src = ev
                seen = self.seen[e]
                if seen.get(key, 0) >= val:
                    continue
                self.eng[e].wait_ge(sem, val)
                seen[key] = val


_CONST_CACHE = {}


def _bf(a):
    return np.ascontiguousarray(a.astype(ml_dtypes.bfloat16))


def _dft_tables(N, ntile):
    idx = np.arange(128 * ntile, dtype=np.int64)
    prod = (idx[:, None] * idx[None, :]) % N
    ang = prod.astype(np.float64) * (2.0 * np.pi / N)
    c = np.cos(ang).reshape(ntile, 128, ntile, 128)
    s = np.sin(ang).reshape(ntile, 128, ntile, 128)
    c = c.transpose(2, 1, 0, 3)
    s = s.transpose(2, 1, 0, 3)
    return _bf(c), _bf(s)


def _hy_pos_tables(Lh):
    t = np.linspace(0.0, 1.0, Lh, dtype=np.float32)
    pos = np.arange(Lh, dtype=np.float32)
    f = np.linspace(1e-4, 15.0, 16, dtype=np.float32)
    ang = (np.float32(2.0 * math.pi / Lh) * pos[:, None] * f).astype(np.float32)
    z = np.concatenate([t[:, None], np.cos(ang), np.sin(ang)], axis=-1).astype(np.float32)
    deltas = np.abs(np.linspace(math.log(1e-2) / 0.3, math.log(1e-2) / 1.5, 256, dtype=np.float32))
    dec = np.exp(-t[:, None] * deltas[None, :]).astype(np.float32)
    dec4 = np.concatenate([dec, dec, dec, dec], axis=1)
    dec4[0, 256:512] = 0.0
    dec4[0, 768:1024] = 0.0
    return np.ascontiguousarray(z.T), np.ascontiguousarray(dec4)


def _consts():
    if _CONST_CACHE:
        return _CONST_CACHE
    C = {}
    C["ident_bf"] = _bf(np.eye(128, dtype=np.float32))
    C["ident_f"] = np.eye(128, dtype=np.float32)
    rows = np.repeat(np.arange(64, dtype=np.float32), 64)
    cols = np.tile(np.arange(64, dtype=np.float32), 64)
    inv = (np.float32(10000.0) ** (-np.arange(16, dtype=np.float32) / np.float32(16))).astype(np.float32)
    ang = np.concatenate([rows[:, None] * inv, cols[:, None] * inv], axis=-1).astype(np.float32)
    cosv, sinv = np.cos(ang).astype(np.float32), np.sin(ang).astype(np.float32)
    ct = np.zeros((128, L), np.float32)
    st = np.zeros((128, L), np.float32)
    for m in range(2):
        for ax in range(2):
            for half in range(2):
                for f in range(16):
                    p = m * 64 + ax * 32 + half * 16 + f
                    ct[p] = cosv[:, ax * 16 + f]
                    st[p] = sinv[:, ax * 16 + f] * (-1.0 if half == 0 else 1.0)
    C["rope_c"] = ct
    C["rope_s"] = st
    zx, decx = _hy_pos_tables(L)
    zc, decc = _hy_pos_tables(LC)
    C["hy_zx"], C["hy_decx"], C["hy_zc"], C["hy_decc"] = zx, decx, zc, decc
    C["gx_c"], C["gx_s"] = _dft_tables(2 * L, 33)
    C["gc_c"], C["gc_s"] = _dft_tables(2 * LC, 3)
    for nm, Lh, nt in (("ckx", L, 33), ("ckc", LC, 3)):
        k = np.arange(128 * nt)
        ck = np.where((k == 0) | (k == Lh), 1.0, 2.0) / (2.0 * Lh)
        ck = np.where(k <= Lh, ck, 0.0)
        C[nm] = np.ascontiguousarray(ck.reshape(nt, 128).T.astype(np.float32))
    r = np.arange(128)[:, None]
    j = np.arange(512)[None, :]
    mf = np.stack([np.where(128 * o + r > j, -30000.0, 0.0) for o in range(4)], 0)
    mb = np.stack([np.where(128 * o + r < j, -30000.0, 0.0) for o in range(4)], 0)
    C["mask_f"] = _bf(mf.transpose(1, 0, 2))
    C["mask_b"] = _bf(mb.transpose(1, 0, 2))
    sel = np.zeros((16, 8, 128), np.float32)
    cmb = np.zeros((16, 8, 512), np.float32)
    for d in range(2):
        for h in range(4):
            sel[8 * d + 4 + h, d * 4 + h, :] = 1.0
            cmb[8 * d + h, d * 4 + h, :] = 1.0
            cmb[8 * d + 4 + h, d * 4 + h, :] = -1.0
    sel3 = np.zeros((128, 8, 128), np.float32)
    sel3[0:48] = np.concatenate([sel] * 3, 0)
    cmb3 = np.zeros((128, 8), np.float32)
    cmb3[0:48] = np.concatenate([cmb[:, :, 0]] * 3, 0)
    usel3 = np.zeros((128, 8), np.float32)
    usel3[0:48] = np.concatenate([sel[:, :, 0]] * 3, 0)
    C["g_sel"], C["g_c16"] = _bf(sel3), _bf(np.concatenate([cmb3, usel3], 1))
    C["m01_f"] = _bf((mf.transpose(1, 0, 2) == 0).astype(np.float32))
    C["m01_b"] = _bf((mb.transpose(1, 0, 2) == 0).astype(np.float32))
    gm = np.zeros((16, 4), np.float32)
    gm[4:8, 0] = -1.0
    gm[12:16, 0] = -1.0
    gm[0:4, 1] = 1.0
    gm[8:12, 1] = 1.0
    gm[4:8, 2] = 1.0
    gm[12:16, 3] = 1.0
    C["g_fmask"] = gm
    _CONST_CACHE.update(C)
    return C


CONST_SPECS = {
    "ident_bf": ([128, 128], BF16), "ident_f": ([128, 128], F32),
    "rope_c": ([128, L], F32), "rope_s": ([128, L], F32),
    "hy_zx": ([33, L], F32), "hy_decx": ([L, 1024], F32),
    "hy_zc": ([33, LC], F32), "hy_decc": ([LC, 1024], F32),
    "gx_c": ([33, 128, 33, 128], BF16), "gx_s": ([33, 128, 33, 128], BF16),
    "gc_c": ([3, 128, 3, 128], BF16), "gc_s": ([3, 128, 3, 128], BF16),
    "ckx": ([128, 33], F32), "ckc": ([128, 3], F32),
    "mask_f": ([128, 4, 512], BF16), "mask_b": ([128, 4, 512], BF16),
    "g_sel": ([128, 8, 128], BF16), "g_c16": ([128, 16], BF16),
    "m01_f": ([128, 4, 512], BF16), "m01_b": ([128, 4, 512], BF16), "g_fmask": ([16, 4], F32),
}

WEIGHT_SPECS = {
    "ada_w": [DEPTH, D, 6 * D], "ada_b": [DEPTH, 6 * D], "w_in": [DEPTH, D, INW], "w_in_sw": [DEPTH, D, 1024],
    "w_out": [DEPTH, D, D], "hy_conv_w": [DEPTH, 3, 768], "hy_conv_b": [DEPTH, 768],
    "hy_w1": [DEPTH, 33, 64], "hy_b1": [DEPTH, 64], "hy_w2": [DEPTH, 64, 64], "hy_b2": [DEPTH, 64],
    "hy_w3": [DEPTH, 64, 1024], "hy_b3": [DEPTH, 1024], "hy_skip": [DEPTH, 2, 256],
    "da_lambda": [DEPTH, 4, 64], "ml_conv_w": [DEPTH, 3, 512], "ml_conv_b": [DEPTH, 512],
    "ml_gate_b": [DEPTH, 16], "mix_norm_w": [DEPTH, D], "ffn_up": [DEPTH, D, 2 * DFF],
    "ffn_conv_w": [DEPTH, 3, 2 * DFF], "ffn_conv_b": [DEPTH, 2 * DFF], "ffn_down": [DEPTH, DFF, D],
    "final_norm_w": [D],
}


def tcol(t):
    return t * 128 + (1 if t < 2 else 2)


QGROUPS = [(0, 256, 0, 2)] + [(256 + 512 * g, 512, 2 + 4 * g, 4) for g in range(8)]
FGROUPS = [(1, 256, 0)] + [(XOFF + 512 * g, 512, 256 + 512 * g) for g in range(8)]


def build_program(layers=(0, 1, 2, 3), dbg=(), stop_after=None, final=True, run_phases=('da', 'ml', 'hy')):
    nc = bass.Bass("TRN2", target_bir_lowering=False)
    I = {}
    I["x"] = nc.dram_tensor("x", [L, D], F32, kind="ExternalInput").ap()
    I["ctx"] = nc.dram_tensor("ctx", [LC, D], F32, kind="ExternalInput").ap()
    I["c2"] = nc.dram_tensor("c2", [2, D], F32, kind="ExternalInput").ap()
    for k, shp in WEIGHT_SPECS.items():
        I[k] = nc.dram_tensor(k, shp, F32, kind="ExternalInput").ap()
    for k, (shp, dt) in CONST_SPECS.items():
        I[k] = nc.dram_tensor(k, shp, dt, kind="ExternalInput").ap()
    OUT = nc.dram_tensor("out", [L, D], F32, kind="ExternalOutput").ap()

    def scratch(name, shape, dt):
        kind = "ExternalOutput" if name in dbg else "Internal"
        return nc.dram_tensor(name, shape, dt, kind=kind).ap()

    S = {}
    S["xres"] = scratch("xres", [T, D], F32)
    S["modv"] = scratch("modv", [2, 6 * D], F32)
    S["HY"] = scratch("HY", [T, 768], F32)
    S["QT"] = scratch("QT", [4, 128, T], BF16)
    S["KT"] = scratch("KT", [4, 128, T], BF16)
    S["VDA"] = scratch("VDA", [T, 4, 129], BF16)
    S["MQT"] = scratch("MQT", [2, 128, T], BF16)
    S["MKT"] = scratch("MKT", [2, 128, T], BF16)
    S["VML"] = scratch("VML", [T, 4, 65], BF16)
    S["MO"] = scratch("MO", [T, 256], F32)
    S["GA"] = scratch("GA", [16, T], F32)
    S["GA3"] = scratch("GA3", [48, T], BF16)
    S["YMIX"] = scratch("YMIX", [T, D], F32)
    S["Z2"] = scratch("Z2", [T, 256], F32)
    S["HFX"] = scratch("HFX", [33 * 128, 2, 512], F32)
    S["HFC"] = scratch("HFC", [3 * 128, 2, 512], F32)
    S["ACTT"] = scratch("ACTT", [DFF, T], BF16)
    B = {k: Buf(k) for k in S}
    BOUT = Buf("out")

    with contextlib.ExitStack() as gst:
        fw = FW(nc, gst)
        uid = [0]

        def sb(st, name, shape, dt):
            uid[0] += 1
            return st.enter_context(nc.sbuf_tensor("%s_%d" % (name, uid[0]), shape, dt))
        PS = [gst.enter_context(nc.psum_tensor("ps%d" % i, [128, 512], F32)) for i in range(8)]
        BPS = [Buf("ps%d" % i) for i in range(8)]
        ident_bf = sb(gst, "ident_bf", [128, 128], BF16)
        ident_f = sb(gst, "ident_f", [128, 128], F32)
        Bconst = Buf("const")
        fw.dma("sp", ident_bf[:], I["ident_bf"], writes=[Bconst])
        fw.dma("sp", ident_f[:], I["ident_f"], writes=[Bconst])
        fw.dma("sp", S["xres"][0:LC, :], I["ctx"], writes=[B["xres"]])
        for i in range(4):
            fw.dma("sp", S["xres"][LC + i * 1024: LC + (i + 1) * 1024, :], I["x"][i * 1024:(i + 1) * 1024, :],
                   writes=[B["xres"]])

        def mm(out, lhsT, rhs, start, stop, reads, writes):
            fw.op("pe", lambda e: e.matmul(out, lhsT=lhsT, rhs=rhs, start=start, stop=stop), reads, writes)

        def phase_mod(l):
            with contextlib.ExitStack() as st:
                c2T = sb(st, "c2T", [128, KC, 2], F32)
                sT = sb(st, "sT", [128, KC, 2], F32)
                adab = sb(st, "adab", [2, 6 * D], F32)
                modv = sb(st, "modv_s", [2, 6 * D], F32)
                aw = [sb(st, "aw%d" % i, [128, KC, 512], F32) for i in range(2)]
                Bc2T, BsT, Badab, Bmodv = Buf(), Buf(), Buf(), Buf()
                Baw = [Buf(), Buf()]
                for r in range(2):
                    fw.dma("sp", c2T[:, :, r], I["c2"][r, :].rearrange("(k p) -> p k", p=128), writes=[Bc2T],
                           allow_slow_non_contiguous=True)
                fw.dma("sp", adab[:], I["ada_b"][l, :].partition_broadcast(2), writes=[Badab])
                fw.op("act", lambda e: e.activation(out=sT[:], in_=c2T[:], func=AF.Silu), [Bc2T], [BsT])
                for j in range(12):
                    w = aw[j % 2]
                    fw.dma("sp", w[:], I["ada_w"][l, :, j * 512:(j + 1) * 512].rearrange("(k p) c -> p k c", p=128),
                           writes=[Baw[j % 2]])
                    pb = j % 2
                    for kc in range(KC):
                        mm(PS[pb][0:2, :], sT[:, kc, :], w[:, kc, :], kc == 0, kc == KC - 1,
                           [BsT, Baw[j % 2]], [BPS[pb]])
                    fw.op("dve", lambda e: e.tensor_tensor(out=modv[:, j * 512:(j + 1) * 512], in0=PS[pb][0:2, :],
                                                           in1=adab[:, j * 512:(j + 1) * 512], op=ALU.add),
                          [BPS[pb], Badab], [Bmodv])
                for seg in (1, 4):
                    fw.op("dve", lambda e: e.tensor_scalar_add(out=modv[:, seg * D:(seg + 1) * D],
                                                               in0=modv[:, seg * D:(seg + 1) * D], scalar1=1.0),
                          [Bmodv], [Bmodv])
                fw.dma("sp", S["modv"], modv[:], reads=[Bmodv], writes=[B["modv"]])
            fw.barrier()

        def load_mod(st, name, seg):
            tx = sb(st, name + "x", [128, D], F32)
            tc_ = sb(st, name + "c", [128, D], F32)
            Bt = Buf()
            fw.dma("sp", tx[:], S["modv"][0, seg * D:(seg + 1) * D].partition_broadcast(128), reads=[B["modv"]], writes=[Bt])
            fw.dma("sp", tc_[:], S["modv"][1, seg * D:(seg + 1) * D].partition_broadcast(128), reads=[B["modv"]], writes=[Bt])
            return tx, tc_, Bt

        def phase_norm(st0, l, which, hT, BhT, tiles):
            with contextlib.ExitStack() as st:
                shx, shc, Bsh = load_mod(st, "sh", 0 if which == 0 else 3)
                scx, scc, Bsc = load_mod(st, "sc", 1 if which == 0 else 4)
                xt = [sb(st, "xt%d" % i, [128, D], F32) for i in range(2)]
                junk = sb(st, "junk", [128, D], BF16)
                tmp = sb(st, "ntmp", [128, D], F32)
                hn = [sb(st, "hn%d" % i, [128, D], BF16) for i in range(2)]
                ss = [sb(st, "ss%d" % i, [128, 2], F32) for i in range(2)]
                Bxt, Bhn, Bss = [Buf(), Buf()], [Buf(), Buf()], [Buf(), Buf()]
                Bjunk, Btmp = Buf(), Buf()
                for it, t in enumerate(tiles):
                    i = it % 2
                    sh, sc = (shc, scc) if t < 2 else (shx, scx)
                    fw.dma("sp", xt[i][:], S["xres"][t * 128:(t + 1) * 128, :], reads=[B["xres"]], writes=[Bxt[i]])
                    fw.op("act", lambda e: e.activation(out=junk[:], in_=xt[i][:], func=AF.Square,
                                                        accum_out=ss[i][:, 0:1]), [Bxt[i]], [Bjunk, Bss[i]])
                    fw.op("dve", lambda e: e.tensor_scalar(out=ss[i][:, 1:2], in0=ss[i][:, 0:1], scalar1=1.0 / D,
                                                           scalar2=EPS, op0=ALU.mult, op1=ALU.add), [Bss[i]], [Bss[i]])
                    fw.op("act", lambda e: e.activation(out=ss[i][:, 1:2], in_=ss[i][:, 1:2], func=AF.Sqrt), [Bss[i]], [Bss[i]])
                    fw.op("dve", lambda e: e.reciprocal(out=ss[i][:, 0:1], in_=ss[i][:, 1:2]), [Bss[i]], [Bss[i]])
                    fw.op("dve", lambda e: e.scalar_tensor_tensor(out=tmp[:], in0=xt[i][:], scalar=ss[i][:, 0:1],
                                                                  in1=sc[:], op0=ALU.mult, op1=ALU.mult),
                          [Bxt[i], Bss[i], Bsc], [Btmp])
                    fw.op("dve", lambda e: e.tensor_tensor(out=hn[i][:], in0=tmp[:], in1=sh[:], op=ALU.add),
                          [Btmp, Bsh], [Bhn[i]])
                    pb = 6 + i
                    pT = PS[pb][:].bitcast(BF16)
                    for kc in range(KC):
                        fw.op("pe", lambda e: e.transpose(pT[:, kc * 128:(kc + 1) * 128], hn[i][:, kc * 128:(kc + 1) * 128],
                                                          ident_bf[:]), [Bhn[i], Bconst], [BPS[pb]])
                    c0 = tcol(t)
                    fw.op("act", lambda e: e.activation(out=hT[:, :, c0:c0 + 128],
                                                        in_=pT.rearrange("p (k c) -> p k c", k=KC), func=AF.Copy),
                          [BPS[pb]], [BhT])

        def phase_inproj(l, hT, BhT):
            W = I["w_in"]
            with contextlib.ExitStack() as st:
                w32 = sb(st, "w32", [128, KC, 768], F32)
                taps = sb(st, "taps", [128, 3, 768], F32)
                hyb = sb(st, "hyb", [128, 768], F32)
                wj = [sb(st, "wj%d" % j, [128, KC, 768], BF16) for j in range(3)]
                wv = sb(st, "wv", [128, KC, 1024], BF16)
                Bw32, Btaps, Bhyb, Bwj, Bwv = Buf(), Buf(), Buf(), Buf(), Buf()
                fw.dma("sp", w32[:], W[l, :, 0:768].rearrange("(k p) c -> p k c", p=128), writes=[Bw32])
                for j in range(3):
                    fw.dma("sp", taps[:, j, :], I["hy_conv_w"][l, j, :].partition_broadcast(128), writes=[Btaps])
                fw.dma("sp", hyb[:], I["hy_conv_b"][l, :].partition_broadcast(128), writes=[Bhyb])
                fw.dma("pool", wv[:, :, 0:512], W[l, :, 1792:2304].rearrange("(k p) c -> p k c", p=128), writes=[Bwv])
                fw.dma("pool", wv[:, :, 512:1024], W[l, :, 2816:3328].rearrange("(k p) c -> p k c", p=128), writes=[Bwv])
                for j in range(3):
                    for kc in range(KC):
                        fw.op("dve", lambda e: e.tensor_tensor(out=wj[j][:, kc, :], in0=w32[:, kc, :], in1=taps[:, j, :],
                                                               op=ALU.mult), [Bw32, Btaps], [Bwj])
                hyo = [sb(st, "hyo%d" % i, [128, 768], F32) for i in range(2)]
                vda = [sb(st, "vda%d" % i, [128, 4, 129], BF16) for i in range(2)]
                vml = [sb(st, "vml%d" % i, [128, 4, 65], BF16) for i in range(2)]
                mo = [sb(st, "mo%d" % i, [128, 256], F32) for i in range(2)]
                Bhyo, Bvda, Bvml, Bmo = [Buf(), Buf()], [Buf(), Buf()], [Buf(), Buf()], [Buf(), Buf()]
                for i in range(2):
                    fw.op("dve", lambda e: e.memset(vda[i][:, :, 128:129], 1.0), [], [Bvda[i]])
                    fw.op("dve", lambda e: e.memset(vml[i][:, :, 64:65], 1.0), [], [Bvml[i]])
                for t in range(NT):
                    i = t % 2
                    c0 = tcol(t)
                    for cc, (a, n) in enumerate(((0, 512), (512, 256))):
                        pb = cc
                        cnt = 0
                        for j in range(3):
                            for kc in range(KC):
                                mm(PS[pb][:, 0:n], hT[:, kc, c0 + j - 1:c0 + j - 1 + 128], wj[j][:, kc, a:a + n],
                                   cnt == 0, cnt == 23, [BhT, Bwj], [BPS[pb]])
                                cnt += 1
                        fw.op("dve", lambda e: e.tensor_tensor(out=hyo[i][:, a:a + n], in0=PS[pb][:, 0:n],
                                                               in1=hyb[:, a:a + n], op=ALU.add), [BPS[pb], Bhyb], [Bhyo[i]])
                    fw.dma("sp", S["HY"][t * 128:(t + 1) * 128, :], hyo[i][:], reads=[Bhyo[i]], writes=[B["HY"]])
                    for cc in range(2):
                        pb = 2 + cc
                        for kc in range(KC):
                            mm(PS[pb][:, :], hT[:, kc, c0:c0 + 128], wv[:, kc, cc * 512:(cc + 1) * 512],
                               kc == 0, kc == KC - 1, [BhT, Bwv], [BPS[pb]])
                    fw.op("act", lambda e: e.activation(out=vda[i][:, :, 0:128],
                                                        in_=PS[2][:, :].rearrange("p (h c) -> p h c", h=4), func=AF.Copy),
                          [BPS[2]], [Bvda[i]])
                    fw.op("act", lambda e: e.activation(out=vml[i][:, :, 0:64],
                                                        in_=PS[3][:, 0:256].rearrange("p (h c) -> p h c", h=4), func=AF.Copy),
                          [BPS[3]], [Bvml[i]])
                    fw.op("act", lambda e: e.activation(out=mo[i][:], in_=PS[3][:, 256:512], func=AF.Sigmoid),
                          [BPS[3]], [Bmo[i]])
                    fw.dma("sp", S["VDA"][t * 128:(t + 1) * 128, :, :], vda[i][:], reads=[Bvda[i]], writes=[B["VDA"]])
                    fw.dma("sp", S["VML"][t * 128:(t + 1) * 128, :, :], vml[i][:], reads=[Bvml[i]], writes=[B["VML"]])
                    fw.dma("sp", S["MO"][t * 128:(t + 1) * 128, :], mo[i][:], reads=[Bmo[i]], writes=[B["MO"]])
            fw.barrier()
            with contextlib.ExitStack() as st:
                wt = [sb(st, "wt%d" % i, [128, KC, 128], BF16) for i in range(2)]
                Bwt = [Buf(), Buf()]
                rowA = sb(st, "rowA", [128, TP], F32)
                rowB = sb(st, "rowB", [128, TP], F32)
                BrowA, BrowB = Buf(), Buf()
                fw.op("dve", lambda e: e.memset(rowA[:], 0.0), [], [BrowA])
                fw.op("dve", lambda e: e.memset(rowB[:], 0.0), [], [BrowB])
                rcf = sb(st, "rope_c", [128, T], F32)
                rsf = sb(st, "rope_s", [128, T], F32)
                rc = rcf[:, 0:L]
                rs = rsf[:, 0:L]
                Brope = Buf()
                fw.dma("sp", rc, I["rope_c"], writes=[Brope])
                fw.dma("sp", rs, I["rope_s"], writes=[Brope])
                t1f = sb(st, "rt1", [128, T], F32)
                t1 = t1f[:, 0:L]
                Bt1 = Buf()
                orow = [sb(st, "orow%d" % i, [128, T], BF16) for i in range(2)]
                Borow = [Buf(), Buf()]
                cw = sb(st, "cw", [128, 4], F32)
                Bcw = Buf()
                state = {"n": 0, "o": 0}

                def fm_chunk(wsrc, M, row, Brow):
                    k = state["n"] % 2
                    state["n"] += 1
                    fw.dma("pool", wt[k][:, :, 0:M], wsrc.rearrange("(k p) c -> p k c", p=128), writes=[Bwt[k]])
                    for gi, (c0, n, s0) in enumerate(FGROUPS):
                        pb = gi % 2
                        for kc in range(KC):
                            mm(PS[pb][0:M, 0:n], wt[k][:, kc, 0:M], hT[:, kc, c0:c0 + n], kc == 0, kc == KC - 1,
                               [Bwt[k], BhT], [BPS[pb]])
                        fw.op("act", lambda e: e.activation(out=row[0:M, c0:c0 + n], in_=PS[pb][0:M, 0:n], func=AF.Copy),
                              [BPS[pb]], [Brow])

                for kind, col0, sw0, dst in (("q", 768, 0, "QT"), ("k", 1280, 512, "KT")):
                    for h in range(4):
                        fm_chunk(W[l, :, col0 + h * 128: col0 + (h + 1) * 128], 128, rowA, BrowA)
                        fm_chunk(I["w_in_sw"][l, :, sw0 + h * 128: sw0 + (h + 1) * 128], 128, rowB, BrowB)
                        o = state["o"] % 2
                        state["o"] += 1
                        fw.op("dve", lambda e: e.tensor_tensor(out=t1, in0=rowA[:, XOFF:XOFF + L], in1=rc, op=ALU.mult),
                              [BrowA, Brope], [Bt1])
                        fw.op("pool", lambda e: e.tensor_tensor(out=rowB[:, XOFF:XOFF + L], in0=rowB[:, XOFF:XOFF + L],
                                                                in1=rs, op=ALU.mult), [BrowB, Brope], [BrowB])
                        fw.op("dve", lambda e: e.tensor_tensor(out=orow[o][:, LC:T], in0=t1, in1=rowB[:, XOFF:XOFF + L],
                                                               op=ALU.add), [Bt1, BrowB], [Borow[o]])
                        fw.op("act", lambda e: e.activation(out=orow[o][:, 0:LC], in_=rowA[:, COFF:COFF + LC], func=AF.Copy),
                              [BrowA], [Borow[o]])
                        fw.dma("sp", S[dst][h, :, :], orow[o][:], reads=[Borow[o]], writes=[B[dst]])
                for kind, col0, cw0, dst in (("q", 2304, 0, "MQT"), ("k", 2560, 256, "MKT")):
                    for c in range(2):
                        fm_chunk(W[l, :, col0 + c * 128: col0 + (c + 1) * 128], 128, rowA, BrowA)
                        fw.dma("sp", cw[:, 0:3], I["ml_conv_w"][l, :, cw0 + c * 128: cw0 + (c + 1) * 128].rearrange("j p -> p j"),
                               writes=[Bcw], allow_slow_non_contiguous=True)
                        fw.dma("sp", cw[:, 3:4], I["ml_conv_b"][l, cw0 + c * 128: cw0 + (c + 1) * 128].rearrange("(p o) -> p o", o=1),
                               writes=[Bcw], allow_slow_non_contiguous=True)
                        acc = rowB[:, 0:TP - 2]
                        fw.op("dve", lambda e: e.tensor_scalar(out=acc, in0=rowA[:, 0:TP - 2], scalar1=cw[:, 0:1],
                                                               scalar2=cw[:, 3:4], op0=ALU.mult, op1=ALU.add),
                              [BrowA, Bcw], [BrowB])
                        for j in (1, 2):
                            fw.op("dve", lambda e: e.scalar_tensor_tensor(out=acc, in0=rowA[:, j:TP - 2 + j], scalar=cw[:, j:j + 1],
                                                                          in1=acc, op0=ALU.mult, op1=ALU.add),
                                  [BrowA, Bcw, BrowB], [BrowB])
                        o = state["o"] % 2
                        state["o"] += 1
                        fw.op("act", lambda e: e.activation(out=orow[o][:, 0:LC], in_=rowB[:, 0:LC], func=AF.Silu),
                              [BrowB], [Borow[o]])
                        fw.op("act", lambda e: e.activation(out=orow[o][:, LC:T], in_=rowB[:, LC + 1:LC + 1 + L], func=AF.Silu),
                              [BrowB], [Borow[o]])
                        fw.dma("sp", S[dst][c, :, :], orow[o][:], reads=[Borow[o]], writes=[B[dst]])
                        if kind == "q" and c == 1:
                            pass
                fm_chunk(W[l, :, 3328:3344], 16, rowA, BrowA)
                gsb = sb(st, "gsb", [16, 8], F32)
                Bgsb = Buf()
                fw.dma("sp", gsb[:, 0:1], I["ml_gate_b"][l, :].rearrange("(p o) -> p o", o=1), writes=[Bgsb],
                       allow_slow_non_contiguous=True)
                fw.dma("sp", gsb[:, 2:6], I["g_fmask"], writes=[Bgsb])
                g = t1f[0:16, 0:T]
                fw.op("dve", lambda e: e.tensor_scalar(out=g[:, 0:LC], in0=rowA[0:16, COFF:COFF + LC], scalar1=gsb[:, 0:1],
                                                       scalar2=None, op0=ALU.add), [BrowA, Bgsb], [Bt1])
                fw.op("dve", lambda e: e.tensor_scalar(out=g[:, LC:T], in0=rowA[0:16, XOFF:XOFF + L], scalar1=gsb[:, 0:1],
                                                       scalar2=None, op0=ALU.add), [BrowA, Bgsb], [Bt1])
                e1 = rowB[0:16, 0:T]
                fw.op("act", lambda e: e.activation(out=e1, in_=g, func=AF.Exp, scale=-1.0), [Bt1], [BrowB])
                fw.op("act", lambda e: e.activation(out=e1, in_=e1, func=AF.Ln, bias=1.0), [BrowB], [BrowB])
                fw.op("dve", lambda e: e.tensor_scalar(out=e1, in0=e1, scalar1=gsb[:, 2:3], scalar2=None, op0=ALU.mult),
                      [BrowB, Bgsb], [BrowB])
                a0 = rowA[0:16, 0:T]
                fw.op("dve", lambda e: e.scalar_tensor_tensor(out=a0, in0=g, scalar=gsb[:, 3:4], in1=e1, op0=ALU.mult,
                                                              op1=ALU.add), [Bt1, Bgsb, BrowB], [BrowA])
                ones = rcf[0:16, 0:T]
                fw.op("dve", lambda e: e.memset(ones, 1.0), [], [Brope])
                F = t1f[0:16, 0:T]
                fw.op("dve", lambda e: e.tensor_tensor_scan(out=rsf[0:16, 0:T], data0=ones, data1=e1, initial=0.0,
                                                            op0=ALU.mult, op1=ALU.add), [Brope, BrowB], [Brope])
                Fc = rsf[0:16, 0:T]
                fw.op("dve", lambda e: e.tensor_tensor(out=e1, in0=e1, in1=Fc, op=ALU.subtract), [BrowB, Brope], [BrowB])
                fw.op("dve", lambda e: e.tensor_tensor(out=gsb[:, 1:2], in0=Fc[:, LC - 1:LC], in1=Fc[:, T - 1:T], op=ALU.subtract),
                      [Brope, Bgsb], [Bgsb])
                fw.op("dve", lambda e: e.tensor_scalar(out=e1[:, 0:LC], in0=e1[:, 0:LC], scalar1=gsb[:, 1:2], scalar2=None,
                                                       op0=ALU.add), [BrowB, Bgsb], [BrowB])
                fw.op("dve", lambda e: e.tensor_scalar(out=e1[:, LC:T], in0=e1[:, LC:T], scalar1=Fc[:, LC - 1:LC], scalar2=None,
                                                       op0=ALU.add), [BrowB, Brope], [BrowB])
                fw.op("dve", lambda e: e.tensor_scalar(out=F, in0=Fc, scalar1=gsb[:, 4:5], scalar2=None, op0=ALU.mult),
                      [Brope, Bgsb], [Bt1])
                fw.op("dve", lambda e: e.scalar_tensor_tensor(out=F, in0=a0, scalar=gsb[:, 3:4], in1=F, op0=ALU.mult, op1=ALU.add),
                      [BrowA, Bgsb, Bt1], [Bt1])
                fw.op("dve", lambda e: e.scalar_tensor_tensor(out=F, in0=e1, scalar=gsb[:, 5:6], in1=F, op0=ALU.mult, op1=ALU.add),
                      [BrowB, Bgsb, Bt1], [Bt1])
                fw.dma("sp", S["GA"], F, reads=[Bt1], writes=[B["GA"]])
                g3 = orow[0][0:16, 0:T]
                resid = rowA[0:16, 0:T]
                for part in range(3):
                    srcp = F if part == 0 else resid
                    Bsrcp = Bt1 if part == 0 else BrowA
                    fw.op("dve", lambda e: e.tensor_copy(out=g3, in_=srcp), [Bsrcp], [Borow[0]])
                    fw.dma("sp", S["GA3"][part * 16:(part + 1) * 16, :], g3, reads=[Borow[0]], writes=[B["GA3"]])
                    if part < 2:
                        fw.op("dve", lambda e: e.tensor_tensor(out=resid, in0=srcp, in1=g3, op=ALU.subtract), [Bsrcp, Borow[0]], [BrowA])
            fw.barrier()

        def phase_da(l, upd):
            lam_init = 0.8 - 0.6 * math.exp(-0.3 * l)
            with contextlib.ExitStack() as st:
                kz = [sb(st, "kz%d" % i, [128, T], BF16) for i in range(2)]
                qT = sb(st, "qT", [128, T], BF16)
                vx = sb(st, "vx", [128, NT, 129], BF16)
                BkT, BqT, Bvx = Buf(), Buf(), Buf()
                fw.op("dve", lambda e: e.memset(kz[0][64:128, :], 0.0), [], [BkT])
                fw.op("dve", lambda e: e.memset(kz[1][0:64, :], 0.0), [], [BkT])
                dl = sb(st, "dl", [128, 4, 64], F32)
                lm = sb(st, "lm", [128, 8], F32)
                Bdl, Blm = Buf(), Buf()
                fw.dma("sp", dl[:].rearrange("p a b -> p (a b)"), I["da_lambda"][l].rearrange("a b -> (a b)").partition_broadcast(128),
                       writes=[Bdl])
                for a in range(2):
                    fw.op("dve", lambda e: e.tensor_tensor(out=dl[:, 2 * a, :], in0=dl[:, 2 * a, :], in1=dl[:, 2 * a + 1, :],
                                                           op=ALU.mult), [Bdl], [Bdl])
                    fw.op("dve", lambda e: e.reduce_sum(out=lm[:, a:a + 1], in_=dl[:, 2 * a, :], axis=AX.X), [Bdl], [Blm])
                fw.op("act", lambda e: e.activation(out=lm[:, 2:4], in_=lm[:, 0:2], func=AF.Exp), [Blm], [Blm])
                fw.op("dve", lambda e: e.tensor_tensor(out=lm[:, 4:5], in0=lm[:, 3:4], in1=lm[:, 2:3], op=ALU.subtract), [Blm], [Blm])
                fw.op("dve", lambda e: e.tensor_scalar_add(out=lm[:, 5:6], in0=lm[:, 4:5], scalar1=-lam_init), [Blm], [Blm])
                nlam = lm[:, 5:6]
                P = [sb(st, "P%d" % i, [128, 512], BF16) for i in range(2)]
                BP = [Buf(), Buf()]
                o0 = sb(st, "o0", [128, 4, 128], F32)
                yb = [sb(st, "yb%d" % i, [128, 4, 128], F32) for i in range(2)]
                rec = sb(st, "rec", [128, 8], F32)
                Bo0, Byb, Brec = Buf(), [Buf(), Buf()], Buf()
                gcount = 0
                for h in range(4):
                    fw.dma("sp", kz[0][0:64, :], S["KT"][h, 0:64, :], reads=[B["KT"]], writes=[BkT])
                    fw.dma("sp", kz[1][64:128, :], S["KT"][h, 64:128, :], reads=[B["KT"]], writes=[BkT])
                    fw.dma("sp", qT[:], S["QT"][h], reads=[B["QT"]], writes=[BqT])
                    fw.dma("sp", vx[:], S["VDA"][:, h, :].rearrange("(n p) c -> p n c", p=128), reads=[B["VDA"]], writes=[Bvx])
                    for gi, (q0, n, t0, ntile) in enumerate(QGROUPS):
                        if gi == 0 and not upd:
                            continue
                        keys = [0, 1] if gi == 0 else list(range(NT))
                        y = yb[gcount % 2]
                        By = Byb[gcount % 2]
                        gcount += 1
                        for m in range(2):
                            def da_S(ki):
                                b, kt = ki % 2, keys[ki]
                                mm(PS[b][:, 0:n], kz[m][:, kt * 128:(kt + 1) * 128],
                                   qT[:, q0:q0 + n], True, True, [BkT, BqT], [BPS[b]])
                            da_S(0)
                            for ki, kt in enumerate(keys):
                                b = ki % 2
                                if ki + 1 < len(keys):
                                    da_S(ki + 1)
                                fw.op("act", lambda e: e.activation(out=P[b][:, 0:n], in_=PS[b][:, 0:n], func=AF.Exp, scale=0.125),
                                      [BPS[b]], [BP[b]])
                                for qt in range(ntile):
                                    mm(PS[2 + qt][:, 0:129], P[b][:, qt * 128:(qt + 1) * 128], vx[:, kt, :], ki == 0,
                                       ki == len(keys) - 1, [BP[b], Bvx], [BPS[2 + qt]])
                            for qt in range(ntile):
                                fw.op("dve", lambda e: e.reciprocal(out=rec[:, qt:qt + 1], in_=PS[2 + qt][:, 128:129]),
                                      [BPS[2 + qt]], [Brec])
                                if m == 0:
                                    fw.op("dve", lambda e: e.tensor_scalar(out=o0[:, qt, :], in0=PS[2 + qt][:, 0:128],
                                                                           scalar1=rec[:, qt:qt + 1], scalar2=None, op0=ALU.mult),
                                          [BPS[2 + qt], Brec], [Bo0])
                                else:
                                    fw.op("dve", lambda e: e.tensor_scalar(out=y[:, qt, :], in0=PS[2 + qt][:, 0:128],
                                                                           scalar1=rec[:, qt:qt + 1], scalar2=nlam, op0=ALU.mult,
                                                                           op1=ALU.mult), [BPS[2 + qt], Brec, Blm], [By])
                                    fw.op("dve", lambda e: e.tensor_tensor(out=y[:, qt, :], in0=y[:, qt, :], in1=o0[:, qt, :],
                                                                           op=ALU.add), [By, Bo0], [By])
                        fw.dma("sp", S["YMIX"][q0:q0 + n, 256 + h * 128:256 + (h + 1) * 128].rearrange("(n p) c -> p n c", p=128),
                               y[:, 0:ntile, :], reads=[By], writes=[B["YMIX"]])
            fw.barrier()

        def phase_ml(l, upd):
            with contextlib.ExitStack() as st:
                ga = sb(st, "ga", [128, T], BF16)
                sel = sb(st, "sel", [128, 8, 128], BF16)
                cmb8 = sb(st, "cmb8", [128, 8], BF16)
                mkf = sb(st, "mkf", [128, 4, 512], BF16)
                mkb = sb(st, "mkb", [128, 4, 512], BF16)
                cb = sb(st, "cb", [128, NT, 8], F32)
                UBs = [sb(st, "UB%d" % i, [128, 512], F32) for i in range(2)]
                Bga, Bc, Bcb, BUB = Buf(), Buf(), Buf(), [Buf(), Buf()]
                fw.op("dve", lambda e: e.memset(ga[:], 0.0), [], [Bga])
                fw.dma("sp", ga[0:48, :], S["GA3"], reads=[B["GA3"]], writes=[Bga])
                fw.dma("sp", sel[:], I["g_sel"], writes=[Bc])
                fw.dma("sp", cmb8[:], I["g_c16"][:, 0:8], writes=[Bc])
                fw.dma("sp", mkf[:], I["mask_f"], writes=[Bc])
                fw.dma("sp", mkb[:], I["mask_b"], writes=[Bc])
                for t in range(NT):
                    mm(PS[0][:, t * 8:(t + 1) * 8], ga[:, t * 128:(t + 1) * 128], cmb8[:, :], True, True, [Bga, Bc], [BPS[0]])
                fw.op("act", lambda e: e.activation(out=cb[:].rearrange("p t r -> p (t r)"), in_=PS[0][:, 0:NT * 8], func=AF.Copy),
                      [BPS[0]], [Bcb])
                mq = sb(st, "mq", [128, T], BF16)
                mkz = [sb(st, "mkz%d" % i, [128, T], BF16) for i in range(2)]
                vm = sb(st, "vm", [128, NT, 65], BF16)
                Bmq, Bmk, Bvm = Buf(), Buf(), Buf()
                fw.op("dve", lambda e: e.memset(mkz[0][64:128, :], 0.0), [], [Bmk])
                fw.op("dve", lambda e: e.memset(mkz[1][0:64, :], 0.0), [], [Bmk])
                Dt = [sb(st, "Dt%d" % i, [128, 512], F32) for i in range(3)]
                P = [sb(st, "Pm%d" % i, [128, 512], BF16) for i in range(3)]
                BDt, BP = [Buf(), Buf(), Buf()], [Buf(), Buf(), Buf()]
                hf = sb(st, "hf", [128, 4, 64], F32)
                yb = [sb(st, "ym%d" % i, [128, 4, 64], F32) for i in range(2)]
                mot = [sb(st, "mot%d" % i, [128, 4, 64], F32) for i in range(2)]
                rec = sb(st, "mrec", [128, 8], F32)
                Bhf, Byb, Bmot, Brec = Buf(), [Buf(), Buf()], [Buf(), Buf()], Buf()
                gcount = 0
                gd = 0
                for c in range(2):
                    fw.dma("sp", mq[:], S["MQT"][c], reads=[B["MQT"]], writes=[Bmq])
                    fw.dma("sp", mkz[0][0:64, :], S["MKT"][c, 0:64, :], reads=[B["MKT"]], writes=[Bmk])
                    fw.dma("sp", mkz[1][64:128, :], S["MKT"][c, 64:128, :], reads=[B["MKT"]], writes=[Bmk])
                    for hh in range(2):
                        h = 2 * c + hh
                        fw.dma("sp", vm[:], S["VML"][:, h, :].rearrange("(n p) c -> p n c", p=128), reads=[B["VML"]], writes=[Bvm])
                        for gi, (q0, n, t0, ntile) in enumerate(QGROUPS):
                            if gi == 0 and not upd:
                                continue
                            g2 = gcount % 2
                            gcount += 1
                            y, By = yb[g2], Byb[g2]
                            fw.dma("sp", mot[g2][:, 0:ntile, :],
                                   S["MO"][q0:q0 + n, h * 64:(h + 1) * 64].rearrange("(n p) c -> p n c", p=128),
                                   reads=[B["MO"]], writes=[Bmot[g2]])
                            for d in range(2):
                                steps = []
                                if d == 0:
                                    for kt in range(0, t0):
                                        steps.append((kt, None))
                                    for o in range(ntile):
                                        steps.append((t0 + o, (mkf, o)))
                                else:
                                    if gi > 0:
                                        steps += [(0, None), (1, None)]
                                    for o in range(ntile):
                                        steps.append((t0 + o, (mkb, o)))
                                    if gi > 0:
                                        for kt in range(t0 + ntile, NT):
                                            steps.append((kt, None))
                                r = d * 4 + h
                                ub = gd % 2
                                gd += 1
                                mm(PS[3][:, 0:n], sel[:, r, :], ga[:, q0:q0 + n], True, True, [Bc, Bga], [BPS[3]])
                                fw.op("act", lambda e: e.activation(out=UBs[ub][:, 0:n], in_=PS[3][:, 0:n], func=AF.Copy),
                                      [BPS[3]], [BUB[ub]])

                                def ml_SLD(si):
                                    b = si % 3
                                    kt, msk = steps[si]
                                    mm(PS[b][:, 0:n], mkz[hh][:, kt * 128:(kt + 1) * 128],
                                       mq[:, q0:q0 + n], True, True, [Bmk, Bmq], [BPS[b]])
                                ml_SLD(0)
                                if len(steps) > 1:
                                    ml_SLD(1)
                                for si, (kt, msk) in enumerate(steps):
                                    b = si % 3
                                    if si + 2 < len(steps):
                                        ml_SLD(si + 2)
                                    if msk is not None:
                                        mm(PS[3][:, 0:n], sel[:, r, :], ga[:, q0:q0 + n], True, False, [Bc, Bga], [BPS[3]])
                                        mm(PS[3][:, 0:n], ident_bf[:], msk[0][:, msk[1], 0:n], False, True, [Bc, Bconst], [BPS[3]])
                                        fw.op("act", lambda e: e.activation(out=Dt[b][:, 0:n], in_=PS[3][:, 0:n], func=AF.Exp,
                                                                            bias=cb[:, kt, r:r + 1]), [BPS[3], Bcb], [BDt[b]])
                                    else:
                                        fw.op("act", lambda e: e.activation(out=Dt[b][:, 0:n], in_=UBs[ub][:, 0:n], func=AF.Exp,
                                                                            bias=cb[:, kt, r:r + 1]), [BUB[ub], Bcb], [BDt[b]])
                                    fw.op("dve", lambda e: e.scalar_tensor_tensor(out=P[b][:, 0:n], in0=PS[b][:, 0:n], scalar=0.125,
                                                                                  in1=Dt[b][:, 0:n], op0=ALU.mult, op1=ALU.mult),
                                          [BPS[b], BDt[b]], [BP[b]])
                                    for qt in range(ntile):
                                        mm(PS[4 + qt][:, 0:65], P[b][:, qt * 128:(qt + 1) * 128], vm[:, kt, :], si == 0,
                                           si == len(steps) - 1, [BP[b], Bvm], [BPS[4 + qt]])
                                for qt in range(ntile):
                                    fw.op("act", lambda e: e.activation(out=rec[:, 4 + qt:5 + qt], in_=PS[4 + qt][:, 64:65], func=AF.Abs),
                                          [BPS[4 + qt]], [Brec])
                                    fw.op("dve", lambda e: e.tensor_scalar_max(out=rec[:, 4 + qt:5 + qt], in0=rec[:, 4 + qt:5 + qt],
                                                                               scalar1=1.0), [Brec], [Brec])
                                    fw.op("dve", lambda e: e.reciprocal(out=rec[:, qt:qt + 1], in_=rec[:, 4 + qt:5 + qt]), [Brec], [Brec])
                                    if d == 0:
                                        fw.op("dve", lambda e: e.tensor_scalar(out=hf[:, qt, :], in0=PS[4 + qt][:, 0:64],
                                                                               scalar1=rec[:, qt:qt + 1], scalar2=None, op0=ALU.mult),
                                              [BPS[4 + qt], Brec], [Bhf])
                                    else:
                                        fw.op("dve", lambda e: e.scalar_tensor_tensor(out=y[:, qt, :], in0=PS[4 + qt][:, 0:64],
                                                                                      scalar=rec[:, qt:qt + 1], in1=hf[:, qt, :],
                                                                                      op0=ALU.mult, op1=ALU.add),
                                              [BPS[4 + qt], Brec, Bhf], [By])
                                        fw.op("dve", lambda e: e.tensor_tensor(out=y[:, qt, :], in0=y[:, qt, :], in1=mot[g2][:, qt, :],
                                                                               op=ALU.mult), [By, Bmot[g2]], [By])
                            fw.dma("sp", S["YMIX"][q0:q0 + n, 768 + h * 64:768 + (h + 1) * 64].rearrange("(n p) c -> p n c", p=128),
                                   y[:, 0:ntile, :], reads=[By], writes=[B["YMIX"]])
            fw.barrier()

        def hyena_seg(l, Lh, s0, ztab, dectab, Gc, Gs, ck, HF, nk):
            npt = Lh // 128
            pn = min(512, Lh)
            TWO_PI = 2.0 * math.pi
            OFF = math.pi + TWO_PI * 16
            with contextlib.ExitStack() as st:
                w1 = sb(st, "hw1", [33, 64], F32)
                w2 = sb(st, "hw2", [64, 64], F32)
                w3 = sb(st, "hw3", [64, 1024], F32)
                bb = sb(st, "hbb", [64, 4], F32)
                b3 = sb(st, "hb3", [128, 1024], F32)
                zT = sb(st, "hzT", [33, Lh], F32)
                h1T = sb(st, "hh1", [64, Lh], F32)
                h2T = sb(st, "hh2", [64, Lh], F32)
                tmp = sb(st, "htmp", [64, 512], F32)
                tf = sb(st, "htf", [64, 512], F32)
                ti = sb(st, "hti", [64, 512], mybir.dt.int32)
                Bw, Bbb, BzT, Bh1, Bh2, Btmp, Btf, Bti = Buf(), Buf(), Buf(), Buf(), Buf(), Buf(), Buf(), Buf()
                fw.dma("sp", w1[:], I["hy_w1"][l], writes=[Bw])
                fw.dma("sp", w2[:], I["hy_w2"][l], writes=[Bw])
                fw.dma("sp", w3[:], I["hy_w3"][l], writes=[Bw])
                fw.dma("sp", bb[:, 0:1], I["hy_b1"][l, :].rearrange("(p o) -> p o", o=1), writes=[Bbb], allow_slow_non_contiguous=True)
                fw.dma("sp", bb[:, 1:2], I["hy_b2"][l, :].rearrange("(p o) -> p o", o=1), writes=[Bbb], allow_slow_non_contiguous=True)
                fw.dma("sp", b3[:], I["hy_b3"][l, :].partition_broadcast(128), writes=[Bw])
                fw.dma("sp", zT[:], I[ztab], writes=[BzT])
                fw.op("dve", lambda e: e.tensor_copy(out=bb[:, 2:4], in_=bb[:, 0:2]), [Bbb], [Bbb])
                for (w_, K_, src, Bsrc, dst, Bdst, bc) in ((w1, 33, zT, BzT, h1T, Bh1, 2), (w2, 64, h1T, Bh1, h2T, Bh2, 3)):
                    for pg in range(Lh // pn):
                        b = pg % 2
                        mm(PS[b][0:64, 0:pn], w_[:, :], src[0:K_, pg * pn:(pg + 1) * pn], True, True, [Bw, Bsrc], [BPS[b]])
                        fw.op("dve", lambda e: e.tensor_scalar(out=tmp[:, 0:pn], in0=PS[b][0:64, 0:pn], scalar1=bb[:, bc:bc + 1],
                                                               scalar2=None, op0=ALU.add), [BPS[b], Bbb], [Btmp])
                        fw.op("dve", lambda e: e.tensor_scalar(out=tmp[:, 0:pn], in0=tmp[:, 0:pn], scalar1=1.0 / TWO_PI, scalar2=16.0,
                                                               op0=ALU.mult, op1=ALU.add), [Btmp], [Btmp])
                        fw.op("dve", lambda e: e.tensor_copy(out=ti[:, 0:pn], in_=tmp[:, 0:pn]), [Btmp], [Bti])
                        fw.op("dve", lambda e: e.tensor_copy(out=tf[:, 0:pn], in_=ti[:, 0:pn]), [Bti], [Btf])
                        fw.op("dve", lambda e: e.tensor_tensor(out=tmp[:, 0:pn], in0=tmp[:, 0:pn], in1=tf[:, 0:pn], op=ALU.subtract),
                              [Btmp, Btf], [Btmp])
                        fw.op("act", lambda e: e.activation(out=tf[:, 0:pn], in_=tmp[:, 0:pn], func=AF.Sin, scale=math.pi), [Btmp], [Btf])
                        fw.op("dve", lambda e: e.tensor_scalar(out=tmp[:, 0:pn], in0=tmp[:, 0:pn], scalar1=-math.pi, scalar2=0.5 * math.pi,
                                                               op0=ALU.mult, op1=ALU.add), [Btmp], [Btmp])
                        fw.op("act", lambda e: e.activation(out=tmp[:, 0:pn], in_=tmp[:, 0:pn], func=AF.Sin), [Btmp], [Btmp])
                        fw.op("dve", lambda e: e.scalar_tensor_tensor(out=dst[:, pg * pn:(pg + 1) * pn], in0=tf[:, 0:pn], scalar=2.0,
                                                                      in1=tmp[:, 0:pn], op0=ALU.mult, op1=ALU.mult), [Btf, Btmp], [Bdst])
                Pt = sb(st, "hPt", [128, npt, 512], BF16)
                Mt = sb(st, "hMt", [128, npt, 512], BF16)
                dec = [sb(st, "hdec%d" % i, [128, 1024], F32) for i in range(2)]
                tp = sb(st, "htp", [128, 1024], F32)
                ab = sb(st, "hab", [128, 1024], F32)
                a2 = sb(st, "ha2", [128, 512], F32)
                ones_f = sb(st, "hones", [128, 128], F32)
                BPt, BMt, Bdec, Btp, Bab, Ba2, Bones = Buf(), Buf(), [Buf(), Buf()], Buf(), Buf(), Buf(), Buf()
                fw.op("dve", lambda e: e.memset(ones_f[:], 1.0), [], [Bones])
                v4 = lambda ap: ap.rearrange("p (o d c) -> p o d c", o=2, d=2)
                v3 = lambda ap: ap.rearrange("p (o c) -> p o c", o=2)
                for pt in range(npt):
                    i = pt % 2
                    fw.dma("sp", dec[i][:], I[dectab][pt * 128:(pt + 1) * 128, :], writes=[Bdec[i]])
                    for half in range(2):
                        mm(PS[half][:, :], h2T[:, pt * 128:(pt + 1) * 128], w3[:, half * 512:(half + 1) * 512], True, True,
                           [Bh2, Bw], [BPS[half]])
                        fw.op("dve", lambda e: e.tensor_tensor(out=tp[:, half * 512:(half + 1) * 512], in0=PS[half][:, :],
                                                               in1=b3[:, half * 512:(half + 1) * 512], op=ALU.add), [BPS[half], Bw], [Btp])
                    fw.op("dve", lambda e: e.tensor_tensor(out=tp[:], in0=tp[:], in1=dec[i][:], op=ALU.mult), [Btp, Bdec[i]], [Btp])
                    fw.op("act", lambda e: e.activation(out=ab[:], in_=tp[:], func=AF.Abs), [Btp], [Bab])
                    fw.op("pool", lambda e: e.tensor_tensor(out=v3(a2[:]), in0=v4(ab[:])[:, :, 0, :], in1=v4(ab[:])[:, :, 1, :], op=ALU.add),
                          [Bab], [Ba2])
                    mm(PS[2][:, :], ones_f[:], a2[:], pt == 0, pt == npt - 1, [Bones, Ba2], [BPS[2]])
                    fw.op("dve", lambda e: e.tensor_tensor(out=v3(Pt[:, pt, :]), in0=v4(tp[:])[:, :, 0, :], in1=v4(tp[:])[:, :, 1, :],
                                                           op=ALU.add), [Btp], [BPt])
                    fw.op("pool", lambda e: e.tensor_tensor(out=v3(Mt[:, pt, :]), in0=v4(tp[:])[:, :, 0, :], in1=v4(tp[:])[:, :, 1, :],
                                                            op=ALU.subtract), [Btp], [BMt])
                rn = sb(st, "hrn", [128, 512], F32)
                ckt = sb(st, "hck", [128, nk], F32)
                Brn, Bck = Buf(), Buf()
                fw.op("dve", lambda e: e.reciprocal(out=rn[:], in_=PS[2][:, :]), [BPS[2]], [Brn])
                fw.dma("sp", ckt[:], I[ck], writes=[Bck])
                gct = [sb(st, "hgc%d" % i, [128, npt, 128], BF16) for i in range(2)]
                gst = [sb(st, "hgs%d" % i, [128, npt, 128], BF16) for i in range(2)]
                hre = [sb(st, "hre%d" % i, [128, 512], F32) for i in range(2)]
                hs = [sb(st, "hs%d" % i, [128, 512], F32) for i in range(2)]
                Bg, Bh = [Buf(), Buf()], [Buf(), Buf()]
                for kt in range(nk):
                    b = kt % 2
                    fw.dma("sp", gct[b][:], I[Gc][kt, :, 0:npt, :], writes=[Bg[b]])
                    fw.dma("sp", gst[b][:], I[Gs][kt, :, 0:npt, :], writes=[Bg[b]])
                    for pt in range(npt):
                        mm(PS[4][:, :], gct[b][:, pt, :], Pt[:, pt, :], pt == 0, pt == npt - 1, [Bg[b], BPt], [BPS[4]])
                    for pt in range(npt):
                        mm(PS[5][:, :], gst[b][:, pt, :], Mt[:, pt, :], pt == 0, pt == npt - 1, [Bg[b], BMt], [BPS[5]])
                    fw.op("dve", lambda e: e.scalar_tensor_tensor(out=hre[b][:], in0=PS[4][:, :], scalar=ckt[:, kt:kt + 1], in1=rn[:],
                                                                  op0=ALU.mult, op1=ALU.mult), [BPS[4], Bck, Brn], [Bh[b]])
                    fw.op("dve", lambda e: e.scalar_tensor_tensor(out=hs[b][:], in0=PS[5][:, :], scalar=ckt[:, kt:kt + 1], in1=rn[:],
                                                                  op0=ALU.mult, op1=ALU.mult), [BPS[5], Bck, Brn], [Bh[b]])
                    fw.dma("sp", S[HF][kt * 128:(kt + 1) * 128, 0, :], hre[b][:], reads=[Bh[b]], writes=[B[HF]])
                    fw.dma("sp", S[HF][kt * 128:(kt + 1) * 128, 1, :], hs[b][:], reads=[Bh[b]], writes=[B[HF]])
            fw.barrier()
            with contextlib.ExitStack() as st:
                zt = sb(st, "czt", [128, npt, 256], BF16)
                Yre = sb(st, "cYre", [128, nk, 256], BF16)
                Ys = sb(st, "cYs", [128, nk, 256], BF16)
                Bzt, BYre, BYs = Buf(), Buf(), Buf()
                gct = [sb(st, "cgc%d" % i, [128, nk, 128], BF16) for i in range(2)]
                gst = [sb(st, "cgs%d" % i, [128, nk, 128], BF16) for i in range(2)]
                hre = [sb(st, "chre%d" % i, [128, 256], F32) for i in range(2)]
                hs = [sb(st, "chs%d" % i, [128, 256], F32) for i in range(2)]
                Bg, Bh = [Buf(), Buf()], [Buf(), Buf()]
                t1 = sb(st, "ct1", [128, 256], F32)
                t2 = sb(st, "ct2", [128, 256], F32)
                Bt1, Bt2 = Buf(), Buf()
                skb = sb(st, "cskb", [128, 2, 256], F32)
                Bskb = Buf()
                fw.dma("sp", skb[:].rearrange("p a b -> p (a b)"), I["hy_skip"][l].rearrange("a b -> (a b)").partition_broadcast(128),
                       writes=[Bskb])
                zf = [sb(st, "czf%d" % i, [128, 256], F32) for i in range(2)]
                gt = [sb(st, "cgt%d" % i, [128, 256], F32) for i in range(2)]
                zo = [sb(st, "czo%d" % i, [128, 256], F32) for i in range(2)]
                Bzf, Bgt, Bzo = [Buf(), Buf()], [Buf(), Buf()], [Buf(), Buf()]
                for pt in range(npt):
                    i = pt % 2
                    fw.dma("sp", zf[i][:], S["HY"][s0 + pt * 128:s0 + (pt + 1) * 128, 0:256], reads=[B["HY"]], writes=[Bzf[i]])
                    fw.op("act", lambda e: e.activation(out=zt[:, pt, :], in_=zf[i][:], func=AF.Copy), [Bzf[i]], [Bzt])
                for o in range(2):
                    for kt in range(nk):
                        b = kt % 2
                        fw.dma("sp", gct[b][:, 0:npt, :], I[Gc][kt, :, 0:npt, :], writes=[Bg[b]])
                        fw.dma("sp", gst[b][:, 0:npt, :], I[Gs][kt, :, 0:npt, :], writes=[Bg[b]])
                        fw.dma("sp", hre[b][:], S[HF][kt * 128:(kt + 1) * 128, 0, o * 256:(o + 1) * 256], reads=[B[HF]], writes=[Bh[b]])
                        fw.dma("sp", hs[b][:], S[HF][kt * 128:(kt + 1) * 128, 1, o * 256:(o + 1) * 256], reads=[B[HF]], writes=[Bh[b]])
                        pc, ps_ = 2 * b, 2 * b + 1
                        for pt in range(npt):
                            mm(PS[pc][:, 0:256], gct[b][:, pt, :], zt[:, pt, :], pt == 0, pt == npt - 1, [Bg[b], Bzt], [BPS[pc]])
                        for pt in range(npt):
                            mm(PS[ps_][:, 0:256], gst[b][:, pt, :], zt[:, pt, :], pt == 0, pt == npt - 1, [Bg[b], Bzt], [BPS[ps_]])
                        fw.op("dve", lambda e: e.tensor_tensor(out=t1[:], in0=PS[pc][:, 0:256], in1=hre[b][:], op=ALU.mult),
                              [BPS[pc], Bh[b]], [Bt1])
                        fw.op("dve", lambda e: e.tensor_tensor(out=t2[:], in0=PS[ps_][:, 0:256], in1=hs[b][:], op=ALU.mult),
                              [BPS[ps_], Bh[b]], [Bt2])
                        fw.op("pool", lambda e: e.tensor_tensor(out=Yre[:, kt, :], in0=t1[:], in1=t2[:], op=ALU.subtract),
                              [Bt1, Bt2], [BYre])
                        fw.op("dve", lambda e: e.tensor_tensor(out=t1[:], in0=PS[pc][:, 0:256], in1=hs[b][:], op=ALU.mult),
                              [BPS[pc], Bh[b]], [Bt1])
                        fw.op("dve", lambda e: e.tensor_tensor(out=t2[:], in0=PS[ps_][:, 0:256], in1=hre[b][:], op=ALU.mult),
                              [BPS[ps_], Bh[b]], [Bt2])
                        fw.op("pool", lambda e: e.tensor_tensor(out=Ys[:, kt, :], in0=t1[:], in1=t2[:], op=ALU.add),
                              [Bt1, Bt2], [BYs])
                    for nt in range(npt):
                        b = nt % 2
                        fw.dma("sp", gct[b][:], I[Gc][nt, :, 0:nk, :], writes=[Bg[b]])
                        fw.dma("sp", gst[b][:], I[Gs][nt, :, 0:nk, :], writes=[Bg[b]])
                        src = S["HY"][s0 + nt * 128:s0 + (nt + 1) * 128, 0:256] if o == 0 else S["Z2"][s0 + nt * 128:s0 + (nt + 1) * 128, :]
                        fw.dma("sp", zf[b][:], src, reads=[B["HY"], B["Z2"]], writes=[Bzf[b]])
                        fw.dma("sp", gt[b][:], S["HY"][s0 + nt * 128:s0 + (nt + 1) * 128, 256 * (o + 1):256 * (o + 2)], reads=[B["HY"]],
                               writes=[Bgt[b]])
                        pb = 4 + b
                        for kt in range(nk):
                            mm(PS[pb][:, 0:256], gct[b][:, kt, :], Yre[:, kt, :], kt == 0, False, [Bg[b], BYre], [BPS[pb]])
                        for kt in range(nk):
                            mm(PS[pb][:, 0:256], gst[b][:, kt, :], Ys[:, kt, :], False, kt == nk - 1, [Bg[b], BYs], [BPS[pb]])
                        fw.op("pool", lambda e: e.tensor_tensor(out=zo[b][:], in0=zf[b][:], in1=skb[:, o, :], op=ALU.mult),
                              [Bzf[b], Bskb], [Bzo[b]])
                        fw.op("dve", lambda e: e.tensor_tensor(out=zo[b][:], in0=zo[b][:], in1=PS[pb][:, 0:256], op=ALU.add),
                              [Bzo[b], BPS[pb]], [Bzo[b]])
                        fw.op("dve", lambda e: e.tensor_tensor(out=zo[b][:], in0=zo[b][:], in1=gt[b][:], op=ALU.mult),
                              [Bzo[b], Bgt[b]], [Bzo[b]])
                        if o == 0:
                            fw.dma("sp", S["Z2"][s0 + nt * 128:s0 + (nt + 1) * 128, :], zo[b][:], reads=[Bzo[b]], writes=[B["Z2"]])
                            fw.op("act", lambda e: e.activation(out=zt[:, nt, :], in_=zo[b][:], func=AF.Copy), [Bzo[b]], [Bzt])
                        else:
                            fw.dma("sp", S["YMIX"][s0 + nt * 128:s0 + (nt + 1) * 128, 0:256], zo[b][:], reads=[Bzo[b]],
                                   writes=[B["YMIX"]])
            fw.barrier()

        def phase_hyena(l, upd):
            hyena_seg(l, L, LC, "hy_zx", "hy_decx", "gx_c", "gx_s", "ckx", "HFX", 33)
            if upd:
                hyena_seg(l, LC, 0, "hy_zc", "hy_decc", "gc_c", "gc_s", "ckc", "HFC", 3)

        def transpose_into(hn_t, Bhn_t, hT, BhT, t, pb):
            pT = PS[pb][:].bitcast(BF16)
            for kc in range(KC):
                fw.op("pe", lambda e: e.transpose(pT[:, kc * 128:(kc + 1) * 128], hn_t[:, kc * 128:(kc + 1) * 128], ident_bf[:]),
                      [Bhn_t, Bconst], [BPS[pb]])
            c0 = tcol(t)
            fw.op("act", lambda e: e.activation(out=hT[:, :, c0:c0 + 128], in_=pT.rearrange("p (k c) -> p k c", k=KC), func=AF.Copy),
                  [BPS[pb]], [BhT])

        def residual_update(st, tiles_groups, seg, lhs_fn, nk, wmat, Bw, reads_extra):
            gx, gc, Bg = load_mod(st, "gate", seg)
            xt = [sb(st, "rxt%d" % i, [128, D], F32) for i in range(2)]
            tmp = sb(st, "rtmp", [128, D], F32)
            Bxt, Btmp = [Buf(), Buf()], Buf()
            for it, t in enumerate(tiles_groups):
                i = it % 2
                g = gc if t < 2 else gx
                fw.dma("sp", xt[i][:], S["xres"][t * 128:(t + 1) * 128, :], reads=[B["xres"]], writes=[Bxt[i]])
                for cc in range(2):
                    pb = 4 + cc
                    for k in range(nk):
                        mm(PS[pb][:, :], lhs_fn(t, k), wmat[:, k, cc * 512:(cc + 1) * 512], k == 0, k == nk - 1,
                           [Bw] + reads_extra, [BPS[pb]])
                    fw.op("dve", lambda e: e.tensor_tensor(out=tmp[:, cc * 512:(cc + 1) * 512], in0=PS[pb][:, :],
                                                           in1=g[:, cc * 512:(cc + 1) * 512], op=ALU.mult), [BPS[pb], Bg], [Btmp])
                fw.op("pool", lambda e: e.tensor_tensor(out=xt[i][:], in0=xt[i][:], in1=tmp[:], op=ALU.add), [Bxt[i], Btmp], [Bxt[i]])
                fw.dma("sp", S["xres"][t * 128:(t + 1) * 128, :], xt[i][:], reads=[Bxt[i]], writes=[B["xres"]])

        def phase_merge(l, upd, hT, BhT):
            lam_init = 0.8 - 0.6 * math.exp(-0.3 * l)
            tiles = list(range(NT)) if upd else list(range(2, NT))
            with contextlib.ExitStack() as st:
                gw = sb(st, "gw", [128, D], F32)
                Bgw = Buf()
                fw.dma("sp", gw[:], I["mix_norm_w"][l, :].partition_broadcast(128), writes=[Bgw])
                fw.op("dve", lambda e: e.tensor_scalar(out=gw[:, 256:768], in0=gw[:, 256:768], scalar1=1.0 - lam_init, scalar2=None,
                                                       op0=ALU.mult), [Bgw], [Bgw])
                ym = [sb(st, "ym%d" % i, [128, D], F32) for i in range(2)]
                sq = sb(st, "sq", [128, D], F32)
                hn = [sb(st, "mhn%d" % i, [128, D], BF16) for i in range(2)]
                s1 = sb(st, "s1", [128, 16], F32)
                s2 = sb(st, "s2", [128, 16], F32)
                t4 = sb(st, "t4", [128, 4], F32)
                Bym, Bhn = [Buf(), Buf()], [Buf(), Buf()]
                Bsq, Bs1, Bs2, Bt4 = Buf(), Buf(), Buf(), Buf()
                for it, t in enumerate(tiles):
                    i = it % 2
                    fw.dma("sp", ym[i][:], S["YMIX"][t * 128:(t + 1) * 128, :], reads=[B["YMIX"]], writes=[Bym[i]])
                    fw.op("pool", lambda e: e.tensor_tensor(out=sq[:], in0=ym[i][:], in1=ym[i][:], op=ALU.mult), [Bym[i]], [Bsq])
                    fw.op("dve", lambda e: e.reduce_sum(out=s1[:], in_=sq[:].rearrange("p (g c) -> p g c", c=64), axis=AX.X),
                          [Bsq], [Bs1])
                    fw.op("dve", lambda e: e.reduce_sum(out=t4[:], in_=s1[:, 4:12].rearrange("p (g c) -> p g c", c=2), axis=AX.X),
                          [Bs1], [Bt4])
                    fw.op("dve", lambda e: e.tensor_scalar(out=s2[:], in0=s1[:], scalar1=1.0 / 64, scalar2=EPS, op0=ALU.mult,
                                                           op1=ALU.add), [Bs1], [Bs2])
                    for j in range(2):
                        fw.op("dve", lambda e: e.tensor_scalar(out=s2[:, 4 + j:12:2], in0=t4[:], scalar1=1.0 / 128, scalar2=EPS,
                                                               op0=ALU.mult, op1=ALU.add), [Bt4, Bs2], [Bs2])
                    fw.op("act", lambda e: e.activation(out=s2[:], in_=s2[:], func=AF.Sqrt), [Bs2], [Bs2])
                    fw.op("dve", lambda e: e.reciprocal(out=s1[:], in_=s2[:]), [Bs2, Bs1], [Bs1])
                    fw.op("pool", lambda e: e.tensor_tensor(out=sq[:].rearrange("p (g c) -> p g c", c=64),
                                                            in0=ym[i][:].rearrange("p (g c) -> p g c", c=64),
                                                            in1=s1[:, :].unsqueeze(2).to_broadcast([128, 16, 64]), op=ALU.mult),
                          [Bym[i], Bs1, Bsq], [Bsq])
                    fw.op("dve", lambda e: e.tensor_tensor(out=hn[i][:], in0=sq[:], in1=gw[:], op=ALU.mult), [Bsq, Bgw], [Bhn[i]])
                    transpose_into(hn[i], Bhn[i], hT, BhT, t, 6 + i)
            fw.barrier()
            with contextlib.ExitStack() as st:
                wo = sb(st, "wo", [128, KC, D], BF16)
                Bwo = Buf()
                fw.dma("pool", wo[:], I["w_out"][l].rearrange("(k p) c -> p k c", p=128), writes=[Bwo])
                residual_update(st, tiles, 2, lambda t, k: hT[:, k, tcol(t):tcol(t) + 128], KC, wo, Bwo, [BhT])
            fw.barrier()

        def phase_ffn(l, upd, hT, BhT):
            groups = FGROUPS if upd else FGROUPS[1:]
            tiles = list(range(NT)) if upd else list(range(2, NT))
            with contextlib.ExitStack() as st:
                wt = [sb(st, "fwt%d" % i, [128, KC, 128], BF16) for i in range(2)]
                Bwt = [Buf(), Buf()]
                rows = [sb(st, "frow%d" % i, [128, TP], F32) for i in range(2)]
                accs = [sb(st, "facc%d" % i, [128, TP], F32) for i in range(2)]
                Brows, Baccs = [Buf(), Buf()], [Buf(), Buf()]
                for i in range(2):
                    fw.op("dve", lambda e: e.memset(rows[i][:], 0.0), [], [Brows[i]])
                cw = [sb(st, "fcw%d" % i, [128, 4], F32) for i in range(2)]
                Bcw = [Buf(), Buf()]
                orow = [sb(st, "forow%d" % i, [128, T], BF16) for i in range(2)]
                Borow = [Buf(), Buf()]
                for ci in range(DFF // 128):
                    for w_, col0 in ((0, ci * 128), (1, DFF + ci * 128)):
                        fw.dma("pool", wt[w_][:], I["ffn_up"][l, :, col0:col0 + 128].rearrange("(k p) c -> p k c", p=128),
                               writes=[Bwt[w_]])
                        fw.dma("sp", cw[w_][:, 0:3], I["ffn_conv_w"][l, :, col0:col0 + 128].rearrange("j p -> p j"),
                               writes=[Bcw[w_]], allow_slow_non_contiguous=True)
                        fw.dma("sp", cw[w_][:, 3:4], I["ffn_conv_b"][l, col0:col0 + 128].rearrange("(p o) -> p o", o=1),
                               writes=[Bcw[w_]], allow_slow_non_contiguous=True)
                        for gi, (c0, n, s0) in enumerate(groups):
                            pb = 2 * w_ + gi % 2
                            for kc in range(KC):
                                mm(PS[pb][:, 0:n], wt[w_][:, kc, :], hT[:, kc, c0:c0 + n], kc == 0, kc == KC - 1,
                                   [Bwt[w_], BhT], [BPS[pb]])
                            fw.op("act", lambda e: e.activation(out=rows[w_][:, c0:c0 + n], in_=PS[pb][:, 0:n], func=AF.Copy),
                                  [BPS[pb]], [Brows[w_]])
                        acc = accs[w_][:, 0:TP - 2]
                        eng = "dve"
                        fw.op(eng, lambda e: e.tensor_scalar(out=acc, in0=rows[w_][:, 0:TP - 2], scalar1=cw[w_][:, 0:1],
                                                             scalar2=cw[w_][:, 3:4], op0=ALU.mult, op1=ALU.add),
                              [Brows[w_], Bcw[w_]], [Baccs[w_]])
                        for j in (1, 2):
                            fw.op(eng, lambda e: e.scalar_tensor_tensor(out=acc, in0=rows[w_][:, j:TP - 2 + j], scalar=cw[w_][:, j:j + 1],
                                                                        in1=acc, op0=ALU.mult, op1=ALU.add),
                                  [Brows[w_], Bcw[w_], Baccs[w_]], [Baccs[w_]])
                    fw.op("act", lambda e: e.activation(out=accs[1][:, 0:TP - 2], in_=accs[1][:, 0:TP - 2], func=AF.Silu),
                          [Baccs[1]], [Baccs[1]])
                    o = ci % 2
                    fw.op("dve", lambda e: e.tensor_tensor(out=orow[o][:, 0:LC], in0=accs[0][:, 0:LC], in1=accs[1][:, 0:LC], op=ALU.mult),
                          [Baccs[0], Baccs[1]], [Borow[o]])
                    fw.op("dve", lambda e: e.tensor_tensor(out=orow[o][:, LC:T], in0=accs[0][:, LC + 1:LC + 1 + L],
                                                           in1=accs[1][:, LC + 1:LC + 1 + L], op=ALU.mult),
                          [Baccs[0], Baccs[1]], [Borow[o]])
                    fw.dma("sp", S["ACTT"][ci * 128:(ci + 1) * 128, :], orow[o][:], reads=[Borow[o]], writes=[B["ACTT"]])
            fw.barrier()

        def phase_ffn_down(l, upd):
            with contextlib.ExitStack() as st:
                wd = sb(st, "wd", [128, DFF // 128, D], BF16)
                Bwd = Buf()
                for half in range(2):
                    fw.dma("pool", wd[:, half * 11:(half + 1) * 11, :],
                           I["ffn_down"][l, half * 1408:(half + 1) * 1408, :].rearrange("(k p) c -> p k c", p=128), writes=[Bwd])
                at = [sb(st, "at%d" % i, [128, DFF // 128, 512], BF16) for i in range(2)]
                Bat = [Buf(), Buf()]
                gx, gc, Bg = load_mod(st, "g2", 5)
                xt = [sb(st, "dxt%d" % i, [128, D], F32) for i in range(2)]
                tmp = sb(st, "dtmp", [128, D], F32)
                Bxt, Btmp = [Buf(), Buf()], Buf()
                it = 0
                for gi, (q0, n, t0, ntile) in enumerate(QGROUPS):
                    if gi == 0 and not upd:
                        continue
                    a = gi % 2
                    fw.dma("sp", at[a][:, :, 0:n], S["ACTT"][:, q0:q0 + n].rearrange("(k p) t -> p k t", p=128),
                           reads=[B["ACTT"]], writes=[Bat[a]])
                    for tt in range(ntile):
                        t = t0 + tt
                        i = it % 2
                        it += 1
                        g = gc if t < 2 else gx
                        fw.dma("sp", xt[i][:], S["xres"][t * 128:(t + 1) * 128, :], reads=[B["xres"]], writes=[Bxt[i]])
                        for cc in range(2):
                            pb = 4 + cc
                            for k in range(DFF // 128):
                                mm(PS[pb][:, :], at[a][:, k, tt * 128:(tt + 1) * 128], wd[:, k, cc * 512:(cc + 1) * 512], k == 0,
                                   k == DFF // 128 - 1, [Bat[a], Bwd], [BPS[pb]])
                            fw.op("dve", lambda e: e.tensor_tensor(out=tmp[:, cc * 512:(cc + 1) * 512], in0=PS[pb][:, :],
                                                                   in1=g[:, cc * 512:(cc + 1) * 512], op=ALU.mult), [BPS[pb], Bg], [Btmp])
                        fw.op("pool", lambda e: e.tensor_tensor(out=xt[i][:], in0=xt[i][:], in1=tmp[:], op=ALU.add), [Bxt[i], Btmp], [Bxt[i]])
                        fw.dma("sp", S["xres"][t * 128:(t + 1) * 128, :], xt[i][:], reads=[Bxt[i]], writes=[B["xres"]])
            fw.barrier()

        def phase_final():
            with contextlib.ExitStack() as st:
                fnw = sb(st, "fnw", [128, D], F32)
                Bfnw = Buf()
                fw.dma("sp", fnw[:], I["final_norm_w"].partition_broadcast(128), writes=[Bfnw])
                xt = [sb(st, "fxt%d" % i, [128, D], F32) for i in range(2)]
                ot = [sb(st, "fot%d" % i, [128, D], F32) for i in range(2)]
                junk = sb(st, "fjunk", [128, D], BF16)
                ss = [sb(st, "fss%d" % i, [128, 2], F32) for i in range(2)]
                Bxt, Bot, Bss, Bjunk = [Buf(), Buf()], [Buf(), Buf()], [Buf(), Buf()], Buf()
                for t in range(2, NT):
                    i = t % 2
                    fw.dma("sp", xt[i][:], S["xres"][t * 128:(t + 1) * 128, :], reads=[B["xres"]], writes=[Bxt[i]])
                    fw.op("act", lambda e: e.activation(out=junk[:], in_=xt[i][:], func=AF.Square, accum_out=ss[i][:, 0:1]),
                          [Bxt[i]], [Bjunk, Bss[i]])
                    fw.op("dve", lambda e: e.tensor_scalar(out=ss[i][:, 1:2], in0=ss[i][:, 0:1], scalar1=1.0 / D, scalar2=EPS,
                                                           op0=ALU.mult, op1=ALU.add), [Bss[i]], [Bss[i]])
                    fw.op("act", lambda e: e.activation(out=ss[i][:, 1:2], in_=ss[i][:, 1:2], func=AF.Sqrt), [Bss[i]], [Bss[i]])
                    fw.op("dve", lambda e: e.reciprocal(out=ss[i][:, 0:1], in_=ss[i][:, 1:2]), [Bss[i]], [Bss[i]])
                    fw.op("dve", lambda e: e.scalar_tensor_tensor(out=ot[i][:], in0=xt[i][:], scalar=ss[i][:, 0:1], in1=fnw[:],
                                                                  op0=ALU.mult, op1=ALU.mult), [Bxt[i], Bss[i], Bfnw], [Bot[i]])
                    fw.dma("sp", OUT[(t - 2) * 128:(t - 1) * 128, :], ot[i][:], reads=[Bot[i]], writes=[BOUT])
            fw.barrier()

        prog = {"nc": nc, "fw": fw}
        for l in layers:
            upd = l < DEPTH - 1
            phase_mod(l)
            if stop_after == "mod":
                break
            with contextlib.ExitStack() as st:
                hT = sb(st, "hT", [128, KC, TP], BF16)
                BhT = Buf("hT")
                fw.op("dve", lambda e: e.memset(hT[:], 0.0), [], [BhT])
                phase_norm(st, l, 0, hT, BhT, list(range(NT)))
                fw.barrier()
                phase_inproj(l, hT, BhT)
            fw.barrier()
            if stop_after == "inproj":
                break
            if "da" in run_phases:
                phase_da(l, upd)
            if "ml" in run_phases:
                phase_ml(l, upd)
            if "hy" in run_phases:
                phase_hyena(l, upd)
            if stop_after == "attn":
                break
            tiles = list(range(NT)) if upd else list(range(2, NT))
            with contextlib.ExitStack() as st:
                hT = sb(st, "hT", [128, KC, TP], BF16)
                BhT = Buf("hT")
                phase_merge(l, upd, hT, BhT)
            fw.barrier()
            if stop_after == "merge":
                break
            with contextlib.ExitStack() as st:
                hT = sb(st, "hT", [128, KC, TP], BF16)
                BhT = Buf("hT")
                phase_norm(st, l, 1, hT, BhT, tiles)
                fw.barrier()
                phase_ffn(l, upd, hT, BhT)
            fw.barrier()
            phase_ffn_down(l, upd)
            if stop_after == "layer":
                break
        if final and stop_after is None:
            phase_final()
        fw.barrier()
        fw.barrier()
    return nc


def _swap_perm():
    perm = np.zeros(1024, np.int64)
    for blk in range(16):
        for ax in range(2):
            for half in range(2):
                for f in range(16):
                    d = ax * 32 + half * 16 + f
                    ds = ax * 32 + (1 - half) * 16 + f
                    perm[blk * 64 + d] = blk * 64 + ds
    return perm


def make_in_maps(inputs, cores=range(8)):
    C = _consts()
    shared = {}
    for k in WEIGHT_SPECS:
        if k == "w_in_sw":
            continue
        shared[k] = np.ascontiguousarray(np.asarray(inputs[k], dtype=np.float32))
    w_in = shared["w_in"]
    shared["w_in_sw"] = np.ascontiguousarray(w_in[:, :, 768:1792][:, :, _swap_perm()])
    for k in CONST_SPECS:
        shared[k] = C[k]
    maps = []
    for b in cores:
        m = dict(shared)
        m["x"] = np.ascontiguousarray(np.asarray(inputs["x"][b], dtype=np.float32))
        m["ctx"] = np.ascontiguousarray(np.asarray(inputs["ctx"][b], dtype=np.float32))
        m["c2"] = np.ascontiguousarray(np.stack([np.asarray(inputs["c"][b], dtype=np.float32),
                                                 np.asarray(inputs["c_ctx"], dtype=np.float32)], 0))
        maps.append(m)
    return maps


_PROG = {}


def kernel(**inputs):
    if "nc" not in _PROG:
        _PROG["nc"] = build_program()
    maps = make_in_maps(inputs, cores=range(8))
    res = run_bass_kernel_spmd(_PROG["nc"], maps, core_ids=list(range(8)))
    out = np.stack([np.asarray(r["out"], dtype=np.float32) for r in res.results], 0)
    return out
```

```python
import math
import contextlib
import numpy as np
import ml_dtypes
import concourse.bass as bass
import concourse.mybir as mybir
from concourse.bass_utils import run_bass_kernel_spmd

F32 = mybir.dt.float32
BF16 = mybir.dt.bfloat16
AF = mybir.ActivationFunctionType
ALU = mybir.AluOpType
AX = mybir.AxisListType

DEPTH = 4
D = 1024
L = 4096
LC = 256
T = L + LC
NT = T // 128
KC = 8
DFF = 2816
INW = 3344
EPS = 1e-6
TP = T + 3
XOFF = LC + 2
COFF = 1


class Buf:
    __slots__ = ("w", "r", "name")

    def __init__(self, name=""):
        self.w = None
        self.r = {}
        self.name = name


class FW:
    NDMA = 8

    def __init__(self, nc, stack):
        self.nc = nc
        self.eng = {"pe": nc.tensor, "act": nc.scalar, "dve": nc.vector,
                    "pool": nc.gpsimd, "sp": nc.sync}
        self.sem = {}
        self.cnt = {}
        for e in ("pe", "act", "dve", "pool"):
            self.sem[e] = stack.enter_context(nc.semaphore("s_" + e))
            self.cnt[e] = 0
        self.dsem = {}
        self.dcnt = {}
        for q in ("sp", "pool"):
            self.dsem[q] = [stack.enter_context(nc.semaphore("d_%s%d" % (q, i)))
                            for i in range(self.NDMA)]
            self.dcnt[q] = 0
        self.seen = {e: {} for e in self.eng}
        self.nops = 0

    def _wait(self, engname, ev):
        key, sem, val, src = ev
        if src == "pe" and engname == "pe":
            return
        seen = self.seen[engname]
        if seen.get(key, 0) >= val:
            return
        self.eng[engname].wait_ge(sem, val)
        seen[key] = val

    def _deps(self, engname, reads, writes):
        for b in reads:
            if b.w is not None:
                self._wait(engname, b.w)
        for b in writes:
            if b.w is not None:
                self._wait(engname, b.w)
            for ev in b.r.values():
                self._wait(engname, ev)

    def _record(self, ev, reads, writes):
        for b in reads:
            b.r[ev[0]] = ev
        for b in writes:
            b.w = ev
            b.r = {}

    def op(self, engname, fn, reads=(), writes=()):
        self._deps(engname, reads, writes)
        ins = fn(self.eng[engname])
        self.cnt[engname] += 1
        ins.then_inc(self.sem[engname], 1)
        ev = (engname, self.sem[engname], self.cnt[engname], engname)
        self._record(ev, reads, writes)
        self.nops += 1
        return ev

    def dma(self, q, out, in_, reads=(), writes=(), **kw):
        i = self.dcnt[q]
        slot = i % self.NDMA
        sem = self.dsem[q][slot]
        key = "d_%s%d" % (q, slot)
        prev = 16 * (i // self.NDMA)
        if prev > 0:
            self._wait(q, (key, sem, prev, "dma"))
        self._deps(q, reads, writes)
        ins = self.eng[q].dma_start(out=out, in_=in_, **kw)
        ins.then_inc(sem, 16)
        self.dcnt[q] += 1
        ev = (key, sem, prev + 16, "dma")
        self._record(ev, reads, writes)
        self.nops += 1
        return ev

    def barrier(self):
        evs = []
        for e in ("pe", "act", "dve", "pool"):
            if self.cnt[e] > 0:
                evs.append((e, self.sem[e], self.cnt[e], e))
        for q in ("sp", "pool"):
            n = self.dcnt[q]
            for slot in range(self.NDMA):
                k = (n - slot + self.NDMA - 1) // self.NDMA
                if k > 0:
                    evs.append(("d_%s%d" % (q, slot), self.dsem[q][slot], 16 * k, "dma"))
        for e in self.eng:
            for ev in evs:
                key, sem, val, src = ev
                seen = self.seen[e]
                if seen.get(key, 0) >= val:
                    continue
                self.eng[e].wait_ge(sem, val)
                seen[key] = val


_CONST_CACHE = {}


def _bf(a):
    return np.ascontiguousarray(a.astype(ml_dtypes.bfloat16))


def _dft_tables(N, ntile):
    idx = np.arange(128 * ntile, dtype=np.int64)
    prod = (idx[:, None] * idx[None, :]) % N
    ang = prod.astype(np.float64) * (2.0 * np.pi / N)
    c = np.cos(ang).reshape(ntile, 128, ntile, 128)
    s = np.sin(ang).reshape(ntile, 128, ntile, 128)
    c = c.transpose(2, 1, 0, 3)
    s = s.transpose(2, 1, 0, 3)
    return _bf(c), _bf(s)


def _hy_pos_tables(Lh):
    t = np.linspace(0.0, 1.0, Lh, dtype=np.float32)
    pos = np.arange(Lh, dtype=np.float32)
    f = np.linspace(1e-4, 15.0, 16, dtype=np.float32)
    ang = (np.float32(2.0 * math.pi / Lh) * pos[:, None] * f).astype(np.float32)
    z = np.concatenate([t[:, None], np.cos(ang), np.sin(ang)], axis=-1).astype(np.float32)
    deltas = np.abs(np.linspace(math.log(1e-2) / 0.3, math.log(1e-2) / 1.5, 256, dtype=np.float32))
    dec = np.exp(-t[:, None] * deltas[None, :]).astype(np.float32)
    dec4 = np.concatenate([dec, dec, dec, dec], axis=1)
    dec4[0, 256:512] = 0.0
    dec4[0, 768:1024] = 0.0
    return np.ascontiguousarray(z.T), np.ascontiguousarray(dec4)


def _consts():
    if _CONST_CACHE:
        return _CONST_CACHE
    C = {}
    C["ident_bf"] = _bf(np.eye(128, dtype=np.float32))
    C["ident_f"] = np.eye(128, dtype=np.float32)
    rows = np.repeat(np.arange(64, dtype=np.float32), 64)
    cols = np.tile(np.arange(64, dtype=np.float32), 64)
    inv = (np.float32(10000.0) ** (-np.arange(16, dtype=np.float32) / np.float32(16))).astype(np.float32)
    ang = np.concatenate([rows[:, None] * inv, cols[:, None] * inv], axis=-1).astype(np.float32)
    cosv, sinv = np.cos(ang).astype(np.float32), np.sin(ang).astype(np.float32)
    ct = np.zeros((128, L), np.float32)
    st = np.zeros((128, L), np.float32)
    for m in range(2):
        for ax in range(2):
            for half in range(2):
                for f in range(16):
                    p = m * 64 + ax * 32 + half * 16 + f
                    ct[p] = cosv[:, ax * 16 + f]
                    st[p] = sinv[:, ax * 16 + f] * (-1.0 if half == 0 else 1.0)
    C["rope_c"] = ct
    C["rope_s"] = st
    zx, decx = _hy_pos_tables(L)
    zc, decc = _hy_pos_tables(LC)
    C["hy_zx"], C["hy_decx"], C["hy_zc"], C["hy_decc"] = zx, decx, zc, decc
    C["gx_c"], C["gx_s"] = _dft_tables(2 * L, 33)
    C["gc_c"], C["gc_s"] = _dft_tables(2 * LC, 3)
    for nm, Lh, nt in (("ckx", L, 33), ("ckc", LC, 3)):
        k = np.arange(128 * nt)
        ck = np.where((k == 0) | (k == Lh), 1.0, 2.0) / (2.0 * Lh)
        ck = np.where(k <= Lh, ck, 0.0)
        C[nm] = np.ascontiguousarray(ck.reshape(nt, 128).T.astype(np.float32))
    r = np.arange(128)[:, None]
    j = np.arange(512)[None, :]
    mf = np.stack([np.where(128 * o + r > j, -30000.0, 0.0) for o in range(4)], 0)
    mb = np.stack([np.where(128 * o + r < j, -30000.0, 0.0) for o in range(4)], 0)
    C["mask_f"] = _bf(mf.transpose(1, 0, 2))
    C["mask_b"] = _bf(mb.transpose(1, 0, 2))
    sel = np.zeros((16, 8, 128), np.float32)
    cmb = np.zeros((16, 8, 512), np.float32)
    for d in range(2):
        for h in range(4):
            sel[8 * d + 4 + h, d * 4 + h, :] = 1.0
            cmb[8 * d + h, d * 4 + h, :] = 1.0
            cmb[8 * d + 4 + h, d * 4 + h, :] = -1.0
    sel3 = np.zeros((128, 8, 128), np.float32)
    sel3[0:48] = np.concatenate([sel] * 3, 0)
    cmb3 = np.zeros((128, 8), np.float32)
    cmb3[0:48] = np.concatenate([cmb[:, :, 0]] * 3, 0)
    usel3 = np.zeros((128, 8), np.float32)
    usel3[0:48] = np.concatenate([sel[:, :, 0]] * 3, 0)
    C["g_sel"], C["g_c16"] = _bf(sel3), _bf(np.concatenate([cmb3, usel3], 1))
    C["m01_f"] = _bf((mf.transpose(1, 0, 2) == 0).astype(np.float32))
    C["m01_b"] = _bf((mb.transpose(1, 0, 2) == 0).astype(np.float32))
    gm = np.zeros((16, 4), np.float32)
    gm[4:8, 0] = -1.0
    gm[12:16, 0] = -1.0
    gm[0:4, 1] = 1.0
    gm[8:12, 1] = 1.0
    gm[4:8, 2] = 1.0
    gm[12:16, 3] = 1.0
    C["g_fmask"] = gm
    _CONST_CACHE.update(C)
    return C


CONST_SPECS = {
    "ident_bf": ([128, 128], BF16), "ident_f": ([128, 128], F32),
    "rope_c": ([128, L], F32), "rope_s": ([128, L], F32),
    "hy_zx": ([33, L], F32), "hy_decx": ([L, 1024], F32),
    "hy_zc": ([33, LC], F32), "hy_decc": ([LC, 1024], F32),
    "gx_c": ([33, 128, 33, 128], BF16), "gx_s": ([33, 128, 33, 128], BF16),
    "gc_c": ([3, 128, 3, 128], BF16), "gc_s": ([3, 128, 3, 128], BF16),
    "ckx": ([128, 33], F32), "ckc": ([128, 3], F32),
    "mask_f": ([128, 4, 512], BF16), "mask_b": ([128, 4, 512], BF16),
    "g_sel": ([128, 8, 128], BF16), "g_c16": ([128, 16], BF16),
    "m01_f": ([128, 4, 512], BF16), "m01_b": ([128, 4, 512], BF16), "g_fmask": ([16, 4], F32),
}

WEIGHT_SPECS = {
    "ada_w": [DEPTH, D, 6 * D], "ada_b": [DEPTH, 6 * D], "w_in": [DEPTH, D, INW], "w_in_sw": [DEPTH, D, 1024],
    "w_out": [DEPTH, D, D], "hy_conv_w": [DEPTH, 3, 768], "hy_conv_b": [DEPTH, 768],
    "hy_w1": [DEPTH, 33, 64], "hy_b1": [DEPTH, 64], "hy_w2": [DEPTH, 64, 64], "hy_b2": [DEPTH, 64],
    "hy_w3": [DEPTH, 64, 1024], "hy_b3": [DEPTH, 1024], "hy_skip": [DEPTH, 2, 256],
    "da_lambda": [DEPTH, 4, 64], "ml_conv_w": [DEPTH, 3, 512], "ml_conv_b": [DEPTH, 512],
    "ml_gate_b": [DEPTH, 16], "mix_norm_w": [DEPTH, D], "ffn_up": [DEPTH, D, 2 * DFF],
    "ffn_conv_w": [DEPTH, 3, 2 * DFF], "ffn_conv_b": [DEPTH, 2 * DFF], "ffn_down": [DEPTH, DFF, D],
    "final_norm_w": [D],
}


def tcol(t):
    return t * 128 + (1 if t < 2 else 2)


QGROUPS = [(0, 256, 0, 2)] + [(256 + 512 * g, 512, 2 + 4 * g, 4) for g in range(8)]
FGROUPS = [(1, 256, 0)] + [(XOFF + 512 * g, 512, 256 + 512 * g) for g in range(8)]


def build_program(layers=(0, 1, 2, 3), dbg=(), stop_after=None, final=True, run_phases=('da', 'ml', 'hy')):
    nc = bass.Bass("TRN2", target_bir_lowering=False)
    I = {}
    I["x"] = nc.dram_tensor("x", [L, D], F32, kind="ExternalInput").ap()
    I["ctx"] = nc.dram_tensor("ctx", [LC, D], F32, kind="ExternalInput").ap()
    I["c2"] = nc.dram_tensor("c2", [2, D], F32, kind="ExternalInput").ap()
    for k, shp in WEIGHT_SPECS.items():
        I[k] = nc.dram_tensor(k, shp, F32, kind="ExternalInput").ap()
    for k, (shp, dt) in CONST_SPECS.items():
        I[k] = nc.dram_tensor(k, shp, dt, kind="ExternalInput").ap()
    OUT = nc.dram_tensor("out", [L, D], F32, kind="ExternalOutput").ap()

    def scratch(name, shape, dt):
        kind = "ExternalOutput" if name in dbg else "Internal"
        return nc.dram_tensor(name, shape, dt, kind=kind).ap()

    S = {}
    S["xres"] = scratch("xres", [T, D], F32)
    S["modv"] = scratch("modv", [2, 6 * D], F32)
    S["HY"] = scratch("HY", [T, 768], F32)
    S["QT"] = scratch("QT", [4, 128, T], BF16)
    S["KT"] = scratch("KT", [4, 128, T], BF16)
    S["VDA"] = scratch("VDA", [T, 4, 129], BF16)
    S["MQT"] = scratch("MQT", [2, 128, T], BF16)
    S["MKT"] = scratch("MKT", [2, 128, T], BF16)
    S["VML"] = scratch("VML", [T, 4, 65], BF16)
    S["MO"] = scratch("MO", [T, 256], F32)
    S["GA"] = scratch("GA", [16, T], F32)
    S["GA3"] = scratch("GA3", [48, T], BF16)
    S["YMIX"] = scratch("YMIX", [T, D], F32)
    S["Z2"] = scratch("Z2", [T, 256], F32)
    S["HFX"] = scratch("HFX", [33 * 128, 2, 512], F32)
    S["HFC"] = scratch("HFC", [3 * 128, 2, 512], F32)
    S["ACTT"] = scratch("ACTT", [DFF, T], BF16)
    B = {k: Buf(k) for k in S}
    BOUT = Buf("out")

    with contextlib.ExitStack() as gst:
        fw = FW(nc, gst)
        uid = [0]

        def sb(st, name, shape, dt):
            uid[0] += 1
            return st.enter_context(nc.sbuf_tensor("%s_%d" % (name, uid[0]), shape, dt))
        PSALL = gst.enter_context(nc.psum_tensor("psall", [128, 8, 512], F32))
        PS = [PSALL[:, i, :] for i in range(8)]
        BPS = [Buf("ps%d" % i) for i in range(8)]
        ident_bf = sb(gst, "ident_bf", [128, 128], BF16)
        ident_f = sb(gst, "ident_f", [128, 128], F32)
        Bconst = Buf("const")
        fw.dma("sp", ident_bf[:], I["ident_bf"], writes=[Bconst])
        fw.dma("sp", ident_f[:], I["ident_f"], writes=[Bconst])
        fw.dma("sp", S["xres"][0:LC, :], I["ctx"], writes=[B["xres"]])
        for i in range(4):
            fw.dma("sp", S["xres"][LC + i * 1024: LC + (i + 1) * 1024, :], I["x"][i * 1024:(i + 1) * 1024, :],
                   writes=[B["xres"]])

        def mm(out, lhsT, rhs, start, stop, reads, writes):
            fw.op("pe", lambda e: e.matmul(out, lhsT=lhsT, rhs=rhs, start=start, stop=stop), reads, writes)

        def phase_mod(l):
            with contextlib.ExitStack() as st:
                c2T = sb(st, "c2T", [128, KC, 2], F32)
                sT = sb(st, "sT", [128, KC, 2], F32)
                adab = sb(st, "adab", [2, 6 * D], F32)
                modv = sb(st, "modv_s", [2, 6 * D], F32)
                aw = [sb(st, "aw%d" % i, [128, KC, 512], F32) for i in range(2)]
                Bc2T, BsT, Badab, Bmodv = Buf(), Buf(), Buf(), Buf()
                Baw = [Buf(), Buf()]
                for r in range(2):
                    fw.dma("sp", c2T[:, :, r], I["c2"][r, :].rearrange("(k p) -> p k", p=128), writes=[Bc2T],
                           allow_slow_non_contiguous=True)
                fw.dma("sp", adab[:], I["ada_b"][l, :].partition_broadcast(2), writes=[Badab])
                fw.op("act", lambda e: e.activation(out=sT[:], in_=c2T[:], func=AF.Silu), [Bc2T], [BsT])
                for j in range(12):
                    w = aw[j % 2]
                    fw.dma("sp", w[:], I["ada_w"][l, :, j * 512:(j + 1) * 512].rearrange("(k p) c -> p k c", p=128),
                           writes=[Baw[j % 2]])
                    pb = j % 2
                    for kc in range(KC):
                        mm(PS[pb][0:2, :], sT[:, kc, :], w[:, kc, :], kc == 0, kc == KC - 1,
                           [BsT, Baw[j % 2]], [BPS[pb]])
                    fw.op("dve", lambda e: e.tensor_tensor(out=modv[:, j * 512:(j + 1) * 512], in0=PS[pb][0:2, :],
                                                           in1=adab[:, j * 512:(j + 1) * 512], op=ALU.add),
                          [BPS[pb], Badab], [Bmodv])
                for seg in (1, 4):
                    fw.op("dve", lambda e: e.tensor_scalar_add(out=modv[:, seg * D:(seg + 1) * D],
                                                               in0=modv[:, seg * D:(seg + 1) * D], scalar1=1.0),
                          [Bmodv], [Bmodv])
                fw.dma("sp", S["modv"], modv[:], reads=[Bmodv], writes=[B["modv"]])
            fw.barrier()

        def load_mod(st, name, seg):
            tx = sb(st, name + "x", [128, D], F32)
            tc_ = sb(st, name + "c", [128, D], F32)
            Bt = Buf()
            fw.dma("sp", tx[:], S["modv"][0, seg * D:(seg + 1) * D].partition_broadcast(128), reads=[B["modv"]], writes=[Bt])
            fw.dma("sp", tc_[:], S["modv"][1, seg * D:(seg + 1) * D].partition_broadcast(128), reads=[B["modv"]], writes=[Bt])
            return tx, tc_, Bt

        def phase_norm(st0, l, which, hT, BhT, tiles):
            with contextlib.ExitStack() as st:
                shx, shc, Bsh = load_mod(st, "sh", 0 if which == 0 else 3)
                scx, scc, Bsc = load_mod(st, "sc", 1 if which == 0 else 4)
                xt = [sb(st, "xt%d" % i, [128, D], F32) for i in range(2)]
                junk = sb(st, "junk", [128, D], BF16)
                tmp = sb(st, "ntmp", [128, D], F32)
                hn = [sb(st, "hn%d" % i, [128, D], BF16) for i in range(2)]
                ss = [sb(st, "ss%d" % i, [128, 2], F32) for i in range(2)]
                Bxt, Bhn, Bss = [Buf(), Buf()], [Buf(), Buf()], [Buf(), Buf()]
                Bjunk, Btmp = Buf(), Buf()
                for it, t in enumerate(tiles):
                    i = it % 2
                    sh, sc = (shc, scc) if t < 2 else (shx, scx)
                    fw.dma("sp", xt[i][:], S["xres"][t * 128:(t + 1) * 128, :], reads=[B["xres"]], writes=[Bxt[i]])
                    fw.op("act", lambda e: e.activation(out=junk[:], in_=xt[i][:], func=AF.Square,
                                                        accum_out=ss[i][:, 0:1]), [Bxt[i]], [Bjunk, Bss[i]])
                    fw.op("dve", lambda e: e.tensor_scalar(out=ss[i][:, 1:2], in0=ss[i][:, 0:1], scalar1=1.0 / D,
                                                           scalar2=EPS, op0=ALU.mult, op1=ALU.add), [Bss[i]], [Bss[i]])
                    fw.op("act", lambda e: e.activation(out=ss[i][:, 1:2], in_=ss[i][:, 1:2], func=AF.Sqrt), [Bss[i]], [Bss[i]])
                    fw.op("dve", lambda e: e.reciprocal(out=ss[i][:, 0:1], in_=ss[i][:, 1:2]), [Bss[i]], [Bss[i]])
                    fw.op("dve", lambda e: e.scalar_tensor_tensor(out=tmp[:], in0=xt[i][:], scalar=ss[i][:, 0:1],
                                                                  in1=sc[:], op0=ALU.mult, op1=ALU.mult),
                          [Bxt[i], Bss[i], Bsc], [Btmp])
                    fw.op("dve", lambda e: e.tensor_tensor(out=hn[i][:], in0=tmp[:], in1=sh[:], op=ALU.add),
                          [Btmp, Bsh], [Bhn[i]])
                    pb = 6 + i
                    pT = PS[pb][:].bitcast(BF16)
                    for kc in range(KC):
                        fw.op("pe", lambda e: e.transpose(pT[:, kc * 128:(kc + 1) * 128], hn[i][:, kc * 128:(kc + 1) * 128],
                                                          ident_bf[:]), [Bhn[i], Bconst], [BPS[pb]])
                    c0 = tcol(t)
                    fw.op("act", lambda e: e.activation(out=hT[:, :, c0:c0 + 128],
                                                        in_=pT.rearrange("p (k c) -> p k c", k=KC), func=AF.Copy),
                          [BPS[pb]], [BhT])

        def phase_inproj(l, hT, BhT):
            W = I["w_in"]
            with contextlib.ExitStack() as st:
                w32 = sb(st, "w32", [128, KC, 768], F32)
                taps = sb(st, "taps", [128, 3, 768], F32)
                hyb = sb(st, "hyb", [128, 768], F32)
                wj = [sb(st, "wj%d" % j, [128, KC, 768], BF16) for j in range(3)]
                wv = sb(st, "wv", [128, KC, 1024], BF16)
                Bw32, Btaps, Bhyb, Bwj, Bwv = Buf(), Buf(), Buf(), Buf(), Buf()
                fw.dma("sp", w32[:], W[l, :, 0:768].rearrange("(k p) c -> p k c", p=128), writes=[Bw32])
                for j in range(3):
                    fw.dma("sp", taps[:, j, :], I["hy_conv_w"][l, j, :].partition_broadcast(128), writes=[Btaps])
                fw.dma("sp", hyb[:], I["hy_conv_b"][l, :].partition_broadcast(128), writes=[Bhyb])
                fw.dma("pool", wv[:, :, 0:512], W[l, :, 1792:2304].rearrange("(k p) c -> p k c", p=128), writes=[Bwv])
                fw.dma("pool", wv[:, :, 512:1024], W[l, :, 2816:3328].rearrange("(k p) c -> p k c", p=128), writes=[Bwv])
                for j in range(3):
                    for kc in range(KC):
                        fw.op("dve", lambda e: e.tensor_tensor(out=wj[j][:, kc, :], in0=w32[:, kc, :], in1=taps[:, j, :],
                                                               op=ALU.mult), [Bw32, Btaps], [Bwj])
                hyo = [sb(st, "hyo%d" % i, [128, 768], F32) for i in range(2)]
                vda = [sb(st, "vda%d" % i, [128, 4, 129], BF16) for i in range(2)]
                vml = [sb(st, "vml%d" % i, [128, 4, 65], BF16) for i in range(2)]
                mo = [sb(st, "mo%d" % i, [128, 256], F32) for i in range(2)]
                Bhyo, Bvda, Bvml, Bmo = [Buf(), Buf()], [Buf(), Buf()], [Buf(), Buf()], [Buf(), Buf()]
                for i in range(2):
                    fw.op("dve", lambda e: e.memset(vda[i][:, :, 128:129], 1.0), [], [Bvda[i]])
                    fw.op("dve", lambda e: e.memset(vml[i][:, :, 64:65], 1.0), [], [Bvml[i]])
                for t in range(NT):
                    i = t % 2
                    c0 = tcol(t)
                    for cc, (a, n) in enumerate(((0, 512), (512, 256))):
                        pb = cc
                        cnt = 0
                        for j in range(3):
                            for kc in range(KC):
                                mm(PS[pb][:, 0:n], hT[:, kc, c0 + j - 1:c0 + j - 1 + 128], wj[j][:, kc, a:a + n],
                                   cnt == 0, cnt == 23, [BhT, Bwj], [BPS[pb]])
                                cnt += 1
                        fw.op("dve", lambda e: e.tensor_tensor(out=hyo[i][:, a:a + n], in0=PS[pb][:, 0:n],
                                                               in1=hyb[:, a:a + n], op=ALU.add), [BPS[pb], Bhyb], [Bhyo[i]])
                    fw.dma("sp", S["HY"][t * 128:(t + 1) * 128, :], hyo[i][:], reads=[Bhyo[i]], writes=[B["HY"]])
                    for cc in range(2):
                        pb = 2 + cc
                        for kc in range(KC):
                            mm(PS[pb][:, :], hT[:, kc, c0:c0 + 128], wv[:, kc, cc * 512:(cc + 1) * 512],
                               kc == 0, kc == KC - 1, [BhT, Bwv], [BPS[pb]])
                    fw.op("act", lambda e: e.activation(out=vda[i][:, :, 0:128],
                                                        in_=PS[2][:, :].rearrange("p (h c) -> p h c", h=4), func=AF.Copy),
                          [BPS[2]], [Bvda[i]])
                    fw.op("act", lambda e: e.activation(out=vml[i][:, :, 0:64],
                                                        in_=PS[3][:, 0:256].rearrange("p (h c) -> p h c", h=4), func=AF.Copy),
                          [BPS[3]], [Bvml[i]])
                    fw.op("act", lambda e: e.activation(out=mo[i][:], in_=PS[3][:, 256:512], func=AF.Sigmoid),
                          [BPS[3]], [Bmo[i]])
                    fw.dma("sp", S["VDA"][t * 128:(t + 1) * 128, :, :], vda[i][:], reads=[Bvda[i]], writes=[B["VDA"]])
                    fw.dma("sp", S["VML"][t * 128:(t + 1) * 128, :, :], vml[i][:], reads=[Bvml[i]], writes=[B["VML"]])
                    fw.dma("sp", S["MO"][t * 128:(t + 1) * 128, :], mo[i][:], reads=[Bmo[i]], writes=[B["MO"]])
            fw.barrier()
            with contextlib.ExitStack() as st:
                wt = [sb(st, "wt%d" % i, [128, KC, 128], BF16) for i in range(2)]
                Bwt = [Buf(), Buf()]
                rowA = sb(st, "rowA", [128, TP], F32)
                rowB = sb(st, "rowB", [128, TP], F32)
                BrowA, BrowB = Buf(), Buf()
                fw.op("dve", lambda e: e.memset(rowA[:], 0.0), [], [BrowA])
                fw.op("dve", lambda e: e.memset(rowB[:], 0.0), [], [BrowB])
                rcf = sb(st, "rope_c", [128, T], F32)
                rsf = sb(st, "rope_s", [128, T], F32)
                rc = rcf[:, 0:L]
                rs = rsf[:, 0:L]
                Brope = Buf()
                fw.dma("sp", rc, I["rope_c"], writes=[Brope])
                fw.dma("sp", rs, I["rope_s"], writes=[Brope])
                t1f = sb(st, "rt1", [128, T], F32)
                t1 = t1f[:, 0:L]
                Bt1 = Buf()
                orow = [sb(st, "orow%d" % i, [128, T], BF16) for i in range(2)]
                Borow = [Buf(), Buf()]
                cw = sb(st, "cw", [128, 4], F32)
                Bcw = Buf()
                state = {"n": 0, "o": 0}

                def fm_chunk(wsrc, M, row, Brow):
                    k = state["n"] % 2
                    state["n"] += 1
                    fw.dma("pool", wt[k][:, :, 0:M], wsrc.rearrange("(k p) c -> p k c", p=128), writes=[Bwt[k]])
                    for gi, (c0, n, s0) in enumerate(FGROUPS):
                        pb = gi % 2
                        for kc in range(KC):
                            mm(PS[pb][0:M, 0:n], wt[k][:, kc, 0:M], hT[:, kc, c0:c0 + n], kc == 0, kc == KC - 1,
                               [Bwt[k], BhT], [BPS[pb]])
                        fw.op("act", lambda e: e.activation(out=row[0:M, c0:c0 + n], in_=PS[pb][0:M, 0:n], func=AF.Copy),
                              [BPS[pb]], [Brow])

                for kind, col0, sw0, dst in (("q", 768, 0, "QT"), ("k", 1280, 512, "KT")):
                    for h in range(4):
                        fm_chunk(W[l, :, col0 + h * 128: col0 + (h + 1) * 128], 128, rowA, BrowA)
                        fm_chunk(I["w_in_sw"][l, :, sw0 + h * 128: sw0 + (h + 1) * 128], 128, rowB, BrowB)
                        o = state["o"] % 2
                        state["o"] += 1
                        fw.op("dve", lambda e: e.tensor_tensor(out=t1, in0=rowA[:, XOFF:XOFF + L], in1=rc, op=ALU.mult),
                              [BrowA, Brope], [Bt1])
                        fw.op("pool", lambda e: e.tensor_tensor(out=rowB[:, XOFF:XOFF + L], in0=rowB[:, XOFF:XOFF + L],
                                                                in1=rs, op=ALU.mult), [BrowB, Brope], [BrowB])
                        fw.op("dve", lambda e: e.tensor_tensor(out=orow[o][:, LC:T], in0=t1, in1=rowB[:, XOFF:XOFF + L],
                                                               op=ALU.add), [Bt1, BrowB], [Borow[o]])
                        fw.op("act", lambda e: e.activation(out=orow[o][:, 0:LC], in_=rowA[:, COFF:COFF + LC], func=AF.Copy),
                              [BrowA], [Borow[o]])
                        fw.dma("sp", S[dst][h, :, :], orow[o][:], reads=[Borow[o]], writes=[B[dst]])
                for kind, col0, cw0, dst in (("q", 2304, 0, "MQT"), ("k", 2560, 256, "MKT")):
                    for c in range(2):
                        fm_chunk(W[l, :, col0 + c * 128: col0 + (c + 1) * 128], 128, rowA, BrowA)
                        fw.dma("sp", cw[:, 0:3], I["ml_conv_w"][l, :, cw0 + c * 128: cw0 + (c + 1) * 128].rearrange("j p -> p j"),
                               writes=[Bcw], allow_slow_non_contiguous=True)
                        fw.dma("sp", cw[:, 3:4], I["ml_conv_b"][l, cw0 + c * 128: cw0 + (c + 1) * 128].rearrange("(p o) -> p o", o=1),
                               writes=[Bcw], allow_slow_non_contiguous=True)
                        acc = rowB[:, 0:TP - 2]
                        fw.op("dve", lambda e: e.tensor_scalar(out=acc, in0=rowA[:, 0:TP - 2], scalar1=cw[:, 0:1],
                                                               scalar2=cw[:, 3:4], op0=ALU.mult, op1=ALU.add),
                              [BrowA, Bcw], [BrowB])
                        for j in (1, 2):
                            fw.op("dve", lambda e: e.scalar_tensor_tensor(out=acc, in0=rowA[:, j:TP - 2 + j], scalar=cw[:, j:j + 1],
                                                                          in1=acc, op0=ALU.mult, op1=ALU.add),
                                  [BrowA, Bcw, BrowB], [BrowB])
                        o = state["o"] % 2
                        state["o"] += 1
                        fw.op("act", lambda e: e.activation(out=orow[o][:, 0:LC], in_=rowB[:, 0:LC], func=AF.Silu),
                              [BrowB], [Borow[o]])
                        fw.op("act", lambda e: e.activation(out=orow[o][:, LC:T], in_=rowB[:, LC + 1:LC + 1 + L], func=AF.Silu),
                              [BrowB], [Borow[o]])
                        fw.dma("sp", S[dst][c, :, :], orow[o][:], reads=[Borow[o]], writes=[B[dst]])
                        if kind == "q" and c == 1:
                            pass
                fm_chunk(W[l, :, 3328:3344], 16, rowA, BrowA)
                gsb = sb(st, "gsb", [16, 8], F32)
                Bgsb = Buf()
                fw.dma("sp", gsb[:, 0:1], I["ml_gate_b"][l, :].rearrange("(p o) -> p o", o=1), writes=[Bgsb],
                       allow_slow_non_contiguous=True)
                fw.dma("sp", gsb[:, 2:6], I["g_fmask"], writes=[Bgsb])
                g = t1f[0:16, 0:T]
                fw.op("dve", lambda e: e.tensor_scalar(out=g[:, 0:LC], in0=rowA[0:16, COFF:COFF + LC], scalar1=gsb[:, 0:1],
                                                       scalar2=None, op0=ALU.add), [BrowA, Bgsb], [Bt1])
                fw.op("dve", lambda e: e.tensor_scalar(out=g[:, LC:T], in0=rowA[0:16, XOFF:XOFF + L], scalar1=gsb[:, 0:1],
                                                       scalar2=None, op0=ALU.add), [BrowA, Bgsb], [Bt1])
                e1 = rowB[0:16, 0:T]
                fw.op("act", lambda e: e.activation(out=e1, in_=g, func=AF.Exp, scale=-1.0), [Bt1], [BrowB])
                fw.op("act", lambda e: e.activation(out=e1, in_=e1, func=AF.Ln, bias=1.0), [BrowB], [BrowB])
                fw.op("dve", lambda e: e.tensor_scalar(out=e1, in0=e1, scalar1=gsb[:, 2:3], scalar2=None, op0=ALU.mult),
                      [BrowB, Bgsb], [BrowB])
                a0 = rowA[0:16, 0:T]
                fw.op("dve", lambda e: e.scalar_tensor_tensor(out=a0, in0=g, scalar=gsb[:, 3:4], in1=e1, op0=ALU.mult,
                                                              op1=ALU.add), [Bt1, Bgsb, BrowB], [BrowA])
                ones = rcf[0:16, 0:T]
                fw.op("dve", lambda e: e.memset(ones, 1.0), [], [Brope])
                F = t1f[0:16, 0:T]
                fw.op("dve", lambda e: e.tensor_tensor_scan(out=rsf[0:16, 0:T], data0=ones, data1=e1, initial=0.0,
                                                            op0=ALU.mult, op1=ALU.add), [Brope, BrowB], [Brope])
                Fc = rsf[0:16, 0:T]
                fw.op("dve", lambda e: e.tensor_tensor(out=e1, in0=e1, in1=Fc, op=ALU.subtract), [BrowB, Brope], [BrowB])
                fw.op("dve", lambda e: e.tensor_tensor(out=gsb[:, 1:2], in0=Fc[:, LC - 1:LC], in1=Fc[:, T - 1:T], op=ALU.subtract),
                      [Brope, Bgsb], [Bgsb])
                fw.op("dve", lambda e: e.tensor_scalar(out=e1[:, 0:LC], in0=e1[:, 0:LC], scalar1=gsb[:, 1:2], scalar2=None,
                                                       op0=ALU.add), [BrowB, Bgsb], [BrowB])
                fw.op("dve", lambda e: e.tensor_scalar(out=e1[:, LC:T], in0=e1[:, LC:T], scalar1=Fc[:, LC - 1:LC], scalar2=None,
                                                       op0=ALU.add), [BrowB, Brope], [BrowB])
                fw.op("dve", lambda e: e.tensor_scalar(out=F, in0=Fc, scalar1=gsb[:, 4:5], scalar2=None, op0=ALU.mult),
                      [Brope, Bgsb], [Bt1])
                fw.op("dve", lambda e: e.scalar_tensor_tensor(out=F, in0=a0, scalar=gsb[:, 3:4], in1=F, op0=ALU.mult, op1=ALU.add),
                      [BrowA, Bgsb, Bt1], [Bt1])
                fw.op("dve", lambda e: e.scalar_tensor_tensor(out=F, in0=e1, scalar=gsb[:, 5:6], in1=F, op0=ALU.mult, op1=ALU.add),
                      [BrowB, Bgsb, Bt1], [Bt1])
                fw.dma("sp", S["GA"], F, reads=[Bt1], writes=[B["GA"]])
                g3 = orow[0][0:16, 0:T]
                resid = rowA[0:16, 0:T]
                for part in range(3):
                    srcp = F if part == 0 else resid
                    Bsrcp = Bt1 if part == 0 else BrowA
                    fw.op("dve", lambda e: e.tensor_copy(out=g3, in_=srcp), [Bsrcp], [Borow[0]])
                    fw.dma("sp", S["GA3"][part * 16:(part + 1) * 16, :], g3, reads=[Borow[0]], writes=[B["GA3"]])
                    if part < 2:
                        fw.op("dve", lambda e: e.tensor_tensor(out=resid, in0=srcp, in1=g3, op=ALU.subtract), [Bsrcp, Borow[0]], [BrowA])
            fw.barrier()

        def phase_da(l, upd):
            lam_init = 0.8 - 0.6 * math.exp(-0.3 * l)
            with contextlib.ExitStack() as st:
                kz = [sb(st, "kz%d" % i, [128, T], BF16) for i in range(2)]
                qT = sb(st, "qT", [128, T], BF16)
                vx = sb(st, "vx", [128, NT, 129], BF16)
                BkT, BqT, Bvx = Buf(), Buf(), Buf()
                fw.op("dve", lambda e: e.memset(kz[0][64:128, :], 0.0), [], [BkT])
                fw.op("dve", lambda e: e.memset(kz[1][0:64, :], 0.0), [], [BkT])
                dl = sb(st, "dl", [128, 4, 64], F32)
                lm = sb(st, "lm", [128, 8], F32)
                Bdl, Blm = Buf(), Buf()
                fw.dma("sp", dl[:].rearrange("p a b -> p (a b)"), I["da_lambda"][l].rearrange("a b -> (a b)").partition_broadcast(128),
                       writes=[Bdl])
                for a in range(2):
                    fw.op("dve", lambda e: e.tensor_tensor(out=dl[:, 2 * a, :], in0=dl[:, 2 * a, :], in1=dl[:, 2 * a + 1, :],
                                                           op=ALU.mult), [Bdl], [Bdl])
                    fw.op("dve", lambda e: e.reduce_sum(out=lm[:, a:a + 1], in_=dl[:, 2 * a, :], axis=AX.X), [Bdl], [Blm])
                fw.op("act", lambda e: e.activation(out=lm[:, 2:4], in_=lm[:, 0:2], func=AF.Exp), [Blm], [Blm])
                fw.op("dve", lambda e: e.tensor_tensor(out=lm[:, 4:5], in0=lm[:, 3:4], in1=lm[:, 2:3], op=ALU.subtract), [Blm], [Blm])
                fw.op("dve", lambda e: e.tensor_scalar_add(out=lm[:, 5:6], in0=lm[:, 4:5], scalar1=-lam_init), [Blm], [Blm])
                nlam = lm[:, 5:6]
                P = [sb(st, "P%d" % i, [128, 512], BF16) for i in range(2)]
                BP = [Buf(), Buf()]
                o0 = sb(st, "o0", [128, 4, 128], F32)
                yb = [sb(st, "yb%d" % i, [128, 4, 128], F32) for i in range(2)]
                rec = sb(st, "rec", [128, 8], F32)
                Bo0, Byb, Brec = Buf(), [Buf(), Buf()], Buf()
                gcount = 0
                for h in range(4):
                    fw.dma("sp", kz[0][0:64, :], S["KT"][h, 0:64, :], reads=[B["KT"]], writes=[BkT])
                    fw.dma("sp", kz[1][64:128, :], S["KT"][h, 64:128, :], reads=[B["KT"]], writes=[BkT])
                    fw.dma("sp", qT[:], S["QT"][h], reads=[B["QT"]], writes=[BqT])
                    fw.dma("sp", vx[:], S["VDA"][:, h, :].rearrange("(n p) c -> p n c", p=128), reads=[B["VDA"]], writes=[Bvx])
                    for gi, (q0, n, t0, ntile) in enumerate(QGROUPS):
                        if gi == 0 and not upd:
                            continue
                        keys = [0, 1] if gi == 0 else list(range(NT))
                        y = yb[gcount % 2]
                        By = Byb[gcount % 2]
                        gcount += 1
                        for m in range(2):
                            def da_S(ki):
                                b, kt = ki % 2, keys[ki]
                                mm(PS[b][:, 0:n], kz[m][:, kt * 128:(kt + 1) * 128],
                                   qT[:, q0:q0 + n], True, True, [BkT, BqT], [BPS[b]])
                            da_S(0)
                            for ki, kt in enumerate(keys):
                                b = ki % 2
                                if ki + 1 < len(keys):
                                    da_S(ki + 1)
                                fw.op("act", lambda e: e.activation(out=P[b][:, 0:n], in_=PS[b][:, 0:n], func=AF.Exp, scale=0.125),
                                      [BPS[b]], [BP[b]])
                                for qt in range(ntile):
                                    mm(PS[2 + qt][:, 0:129], P[b][:, qt * 128:(qt + 1) * 128], vx[:, kt, :], ki == 0,
                                       ki == len(keys) - 1, [BP[b], Bvx], [BPS[2 + qt]])
                            accv = PSALL[:, 2:2 + ntile, :]
                            Bacc = [BPS[2 + qt] for qt in range(ntile)]
                            fw.op("dve", lambda e: e.reciprocal(out=rec[:, 0:ntile], in_=accv[:, :, 128]), Bacc, [Brec])
                            rb = rec[:, 0:ntile].unsqueeze(2).to_broadcast([128, ntile, 128])
                            if m == 0:
                                fw.op("dve", lambda e: e.tensor_tensor(out=o0[:, 0:ntile, :], in0=accv[:, :, 0:128], in1=rb, op=ALU.mult),
                                      Bacc + [Brec], [Bo0])
                            else:
                                fw.op("dve", lambda e: e.tensor_tensor(out=y[:, 0:ntile, :], in0=accv[:, :, 0:128], in1=rb, op=ALU.mult),
                                      Bacc + [Brec], [By])
                                fw.op("dve", lambda e: e.scalar_tensor_tensor(out=y[:, 0:ntile, :], in0=y[:, 0:ntile, :], scalar=nlam,
                                                                              in1=o0[:, 0:ntile, :], op0=ALU.mult, op1=ALU.add),
                                      [By, Blm, Bo0], [By])
                        fw.dma("sp", S["YMIX"][q0:q0 + n, 256 + h * 128:256 + (h + 1) * 128].rearrange("(n p) c -> p n c", p=128),
                               y[:, 0:ntile, :], reads=[By], writes=[B["YMIX"]])
            fw.barrier()

        def phase_ml(l, upd):
            with contextlib.ExitStack() as st:
                ga = sb(st, "ga", [128, T], BF16)
                sel = sb(st, "sel", [128, 8, 128], BF16)
                cmb8 = sb(st, "cmb8", [128, 8], BF16)
                mkf = sb(st, "mkf", [128, 4, 512], BF16)
                mkb = sb(st, "mkb", [128, 4, 512], BF16)
                cb = sb(st, "cb", [128, NT, 8], F32)
                UBs = [sb(st, "UB%d" % i, [128, 512], F32) for i in range(2)]
                Bga, Bc, Bcb, BUB = Buf(), Buf(), Buf(), [Buf(), Buf()]
                fw.op("dve", lambda e: e.memset(ga[:], 0.0), [], [Bga])
                fw.dma("sp", ga[0:48, :], S["GA3"], reads=[B["GA3"]], writes=[Bga])
                fw.dma("sp", sel[:], I["g_sel"], writes=[Bc])
                fw.dma("sp", cmb8[:], I["g_c16"][:, 0:8], writes=[Bc])
                fw.dma("sp", mkf[:], I["mask_f"], writes=[Bc])
                fw.dma("sp", mkb[:], I["mask_b"], writes=[Bc])
                for t in range(NT):
                    mm(PS[0][:, t * 8:(t + 1) * 8], ga[:, t * 128:(t + 1) * 128], cmb8[:, :], True, True, [Bga, Bc], [BPS[0]])
                fw.op("act", lambda e: e.activation(out=cb[:].rearrange("p t r -> p (t r)"), in_=PS[0][:, 0:NT * 8], func=AF.Copy),
                      [BPS[0]], [Bcb])
                mq = sb(st, "mq", [128, T], BF16)
                mkz = [sb(st, "mkz%d" % i, [128, T], BF16) for i in range(2)]
                vm = sb(st, "vm", [128, NT, 65], BF16)
                Bmq, Bmk, Bvm = Buf(), Buf(), Buf()
                fw.op("dve", lambda e: e.memset(mkz[0][64:128, :], 0.0), [], [Bmk])
                fw.op("dve", lambda e: e.memset(mkz[1][0:64, :], 0.0), [], [Bmk])
                Dt = [sb(st, "Dt%d" % i, [128, 512], F32) for i in range(3)]
                P = [sb(st, "Pm%d" % i, [128, 512], BF16) for i in range(3)]
                BDt, BP = [Buf(), Buf(), Buf()], [Buf(), Buf(), Buf()]
                hf = sb(st, "hf", [128, 4, 64], F32)
                yb = [sb(st, "ym%d" % i, [128, 4, 64], F32) for i in range(2)]
                mot = [sb(st, "mot%d" % i, [128, 4, 64], F32) for i in range(2)]
                rec = sb(st, "mrec", [128, 8], F32)
                Bhf, Byb, Bmot, Brec = Buf(), [Buf(), Buf()], [Buf(), Buf()], Buf()
                gcount = 0
                gd = 0
                for c in range(2):
                    fw.dma("sp", mq[:], S["MQT"][c], reads=[B["MQT"]], writes=[Bmq])
                    fw.dma("sp", mkz[0][0:64, :], S["MKT"][c, 0:64, :], reads=[B["MKT"]], writes=[Bmk])
                    fw.dma("sp", mkz[1][64:128, :], S["MKT"][c, 64:128, :], reads=[B["MKT"]], writes=[Bmk])
                    for hh in range(2):
                        h = 2 * c + hh
                        fw.dma("sp", vm[:], S["VML"][:, h, :].rearrange("(n p) c -> p n c", p=128), reads=[B["VML"]], writes=[Bvm])
                        for gi, (q0, n, t0, ntile) in enumerate(QGROUPS):
                            if gi == 0 and not upd:
                                continue
                            g2 = gcount % 2
                            gcount += 1
                            y, By = yb[g2], Byb[g2]
                            fw.dma("sp", mot[g2][:, 0:ntile, :],
                                   S["MO"][q0:q0 + n, h * 64:(h + 1) * 64].rearrange("(n p) c -> p n c", p=128),
                                   reads=[B["MO"]], writes=[Bmot[g2]])
                            for d in range(2):
                                steps = []
                                if d == 0:
                                    for kt in range(0, t0):
                                        steps.append((kt, None))
                                    for o in range(ntile):
                                        steps.append((t0 + o, (mkf, o)))
                                else:
                                    if gi > 0:
                                        steps += [(0, None), (1, None)]
                                    for o in range(ntile):
                                        steps.append((t0 + o, (mkb, o)))
                                    if gi > 0:
                                        for kt in range(t0 + ntile, NT):
                                            steps.append((kt, None))
                                r = d * 4 + h
                                ub = gd % 2
                                gd += 1
                                mm(PS[3][:, 0:n], sel[:, r, :], ga[:, q0:q0 + n], True, True, [Bc, Bga], [BPS[3]])
                                fw.op("act", lambda e: e.activation(out=UBs[ub][:, 0:n], in_=PS[3][:, 0:n], func=AF.Copy),
                                      [BPS[3]], [BUB[ub]])

                                def ml_SLD(si):
                                    b = si % 3
                                    kt, msk = steps[si]
                                    mm(PS[b][:, 0:n], mkz[hh][:, kt * 128:(kt + 1) * 128],
                                       mq[:, q0:q0 + n], True, True, [Bmk, Bmq], [BPS[b]])
                                ml_SLD(0)
                                if len(steps) > 1:
                                    ml_SLD(1)
                                for si, (kt, msk) in enumerate(steps):
                                    b = si % 3
                                    if si + 2 < len(steps):
                                        ml_SLD(si + 2)
                                    if msk is not None:
                                        mm(PS[3][:, 0:n], sel[:, r, :], ga[:, q0:q0 + n], True, False, [Bc, Bga], [BPS[3]])
                                        mm(PS[3][:, 0:n], ident_bf[:], msk[0][:, msk[1], 0:n], False, True, [Bc, Bconst], [BPS[3]])
                                        fw.op("act", lambda e: e.activation(out=Dt[b][:, 0:n], in_=PS[3][:, 0:n], func=AF.Exp,
                                                                            bias=cb[:, kt, r:r + 1]), [BPS[3], Bcb], [BDt[b]])
                                    else:
                                        fw.op("act", lambda e: e.activation(out=Dt[b][:, 0:n], in_=UBs[ub][:, 0:n], func=AF.Exp,
                                                                            bias=cb[:, kt, r:r + 1]), [BUB[ub], Bcb], [BDt[b]])
                                    fw.op("dve", lambda e: e.scalar_tensor_tensor(out=P[b][:, 0:n], in0=PS[b][:, 0:n], scalar=0.125,
                                                                                  in1=Dt[b][:, 0:n], op0=ALU.mult, op1=ALU.mult),
                                          [BPS[b], BDt[b]], [BP[b]])
                                    for qt in range(ntile):
                                        mm(PS[4 + qt][:, 0:65], P[b][:, qt * 128:(qt + 1) * 128], vm[:, kt, :], si == 0,
                                           si == len(steps) - 1, [BP[b], Bvm], [BPS[4 + qt]])
                                accv = PSALL[:, 4:4 + ntile, :]
                                Bacc = [BPS[4 + qt] for qt in range(ntile)]
                                fw.op("act", lambda e: e.activation(out=rec[:, 4:4 + ntile], in_=accv[:, :, 64], func=AF.Abs), Bacc, [Brec])
                                fw.op("dve", lambda e: e.tensor_scalar_max(out=rec[:, 4:4 + ntile], in0=rec[:, 4:4 + ntile], scalar1=1.0),
                                      [Brec], [Brec])
                                fw.op("dve", lambda e: e.reciprocal(out=rec[:, 0:ntile], in_=rec[:, 4:4 + ntile]), [Brec], [Brec])
                                rb = rec[:, 0:ntile].unsqueeze(2).to_broadcast([128, ntile, 64])
                                if d == 0:
                                    fw.op("dve", lambda e: e.tensor_tensor(out=hf[:, 0:ntile, :], in0=accv[:, :, 0:64], in1=rb, op=ALU.mult),
                                          Bacc + [Brec], [Bhf])
                                else:
                                    fw.op("dve", lambda e: e.tensor_tensor(out=y[:, 0:ntile, :], in0=accv[:, :, 0:64], in1=rb, op=ALU.mult),
                                          Bacc + [Brec], [By])
                                    fw.op("pool", lambda e: e.tensor_tensor(out=y[:, 0:ntile, :], in0=y[:, 0:ntile, :], in1=hf[:, 0:ntile, :],
                                                                            op=ALU.add), [By, Bhf], [By])
                                    fw.op("pool", lambda e: e.tensor_tensor(out=y[:, 0:ntile, :], in0=y[:, 0:ntile, :],
                                                                            in1=mot[g2][:, 0:ntile, :], op=ALU.mult), [By, Bmot[g2]], [By])
                            fw.dma("sp", S["YMIX"][q0:q0 + n, 768 + h * 64:768 + (h + 1) * 64].rearrange("(n p) c -> p n c", p=128),
                                   y[:, 0:ntile, :], reads=[By], writes=[B["YMIX"]])
            fw.barrier()

        def hyena_seg(l, Lh, s0, ztab, dectab, Gc, Gs, ck, HF, nk):
            npt = Lh // 128
            pn = min(512, Lh)
            TWO_PI = 2.0 * math.pi
            OFF = math.pi + TWO_PI * 16
            with contextlib.ExitStack() as st:
                w1 = sb(st, "hw1", [33, 64], F32)
                w2 = sb(st, "hw2", [64, 64], F32)
                w3 = sb(st, "hw3", [64, 1024], F32)
                bb = sb(st, "hbb", [64, 4], F32)
                b3 = sb(st, "hb3", [128, 1024], F32)
                zT = sb(st, "hzT", [33, Lh], F32)
                h1T = sb(st, "hh1", [64, Lh], F32)
                h2T = sb(st, "hh2", [64, Lh], F32)
                tmp = sb(st, "htmp", [64, 512], F32)
                tf = sb(st, "htf", [64, 512], F32)
                ti = sb(st, "hti", [64, 512], mybir.dt.int32)
                Bw, Bbb, BzT, Bh1, Bh2, Btmp, Btf, Bti = Buf(), Buf(), Buf(), Buf(), Buf(), Buf(), Buf(), Buf()
                fw.dma("sp", w1[:], I["hy_w1"][l], writes=[Bw])
                fw.dma("sp", w2[:], I["hy_w2"][l], writes=[Bw])
                fw.dma("sp", w3[:], I["hy_w3"][l], writes=[Bw])
                fw.dma("sp", bb[:, 0:1], I["hy_b1"][l, :].rearrange("(p o) -> p o", o=1), writes=[Bbb], allow_slow_non_contiguous=True)
                fw.dma("sp", bb[:, 1:2], I["hy_b2"][l, :].rearrange("(p o) -> p o", o=1), writes=[Bbb], allow_slow_non_contiguous=True)
                fw.dma("sp", b3[:], I["hy_b3"][l, :].partition_broadcast(128), writes=[Bw])
                fw.dma("sp", zT[:], I[ztab], writes=[BzT])
                fw.op("dve", lambda e: e.tensor_copy(out=bb[:, 2:4], in_=bb[:, 0:2]), [Bbb], [Bbb])
                for (w_, K_, src, Bsrc, dst, Bdst, bc) in ((w1, 33, zT, BzT, h1T, Bh1, 2), (w2, 64, h1T, Bh1, h2T, Bh2, 3)):
                    for pg in range(Lh // pn):
                        b = pg % 2
                        mm(PS[b][0:64, 0:pn], w_[:, :], src[0:K_, pg * pn:(pg + 1) * pn], True, True, [Bw, Bsrc], [BPS[b]])
                        fw.op("dve", lambda e: e.tensor_scalar(out=tmp[:, 0:pn], in0=PS[b][0:64, 0:pn], scalar1=bb[:, bc:bc + 1],
                                                               scalar2=None, op0=ALU.add), [BPS[b], Bbb], [Btmp])
                        fw.op("dve", lambda e: e.tensor_scalar(out=tmp[:, 0:pn], in0=tmp[:, 0:pn], scalar1=1.0 / TWO_PI, scalar2=16.0,
                                                               op0=ALU.mult, op1=ALU.add), [Btmp], [Btmp])
                        fw.op("dve", lambda e: e.tensor_copy(out=ti[:, 0:pn], in_=tmp[:, 0:pn]), [Btmp], [Bti])
                        fw.op("dve", lambda e: e.tensor_copy(out=tf[:, 0:pn], in_=ti[:, 0:pn]), [Bti], [Btf])
                        fw.op("dve", lambda e: e.tensor_tensor(out=tmp[:, 0:pn], in0=tmp[:, 0:pn], in1=tf[:, 0:pn], op=ALU.subtract),
                              [Btmp, Btf], [Btmp])
                        fw.op("act", lambda e: e.activation(out=tf[:, 0:pn], in_=tmp[:, 0:pn], func=AF.Sin, scale=math.pi), [Btmp], [Btf])
                        fw.op("dve", lambda e: e.tensor_scalar(out=tmp[:, 0:pn], in0=tmp[:, 0:pn], scalar1=-math.pi, scalar2=0.5 * math.pi,
                                                               op0=ALU.mult, op1=ALU.add), [Btmp], [Btmp])
                        fw.op("act", lambda e: e.activation(out=tmp[:, 0:pn], in_=tmp[:, 0:pn], func=AF.Sin), [Btmp], [Btmp])
                        fw.op("dve", lambda e: e.scalar_tensor_tensor(out=dst[:, pg * pn:(pg + 1) * pn], in0=tf[:, 0:pn], scalar=2.0,
                                                                      in1=tmp[:, 0:pn], op0=ALU.mult, op1=ALU.mult), [Btf, Btmp], [Bdst])
                Pt = sb(st, "hPt", [128, npt, 512], BF16)
                Mt = sb(st, "hMt", [128, npt, 512], BF16)
                dec = [sb(st, "hdec%d" % i, [128, 1024], F32) for i in range(2)]
                tp = sb(st, "htp", [128, 1024], F32)
                ab = sb(st, "hab", [128, 1024], F32)
                a2 = sb(st, "ha2", [128, 512], F32)
                ones_f = sb(st, "hones", [128, 128], F32)
                BPt, BMt, Bdec, Btp, Bab, Ba2, Bones = Buf(), Buf(), [Buf(), Buf()], Buf(), Buf(), Buf(), Buf()
                fw.op("dve", lambda e: e.memset(ones_f[:], 1.0), [], [Bones])
                v4 = lambda ap: ap.rearrange("p (o d c) -> p o d c", o=2, d=2)
                v3 = lambda ap: ap.rearrange("p (o c) -> p o c", o=2)
                for pt in range(npt):
                    i = pt % 2
                    fw.dma("sp", dec[i][:], I[dectab][pt * 128:(pt + 1) * 128, :], writes=[Bdec[i]])
                    for half in range(2):
                        mm(PS[half][:, :], h2T[:, pt * 128:(pt + 1) * 128], w3[:, half * 512:(half + 1) * 512], True, True,
                           [Bh2, Bw], [BPS[half]])
                        fw.op("dve", lambda e: e.tensor_tensor(out=tp[:, half * 512:(half + 1) * 512], in0=PS[half][:, :],
                                                               in1=b3[:, half * 512:(half + 1) * 512], op=ALU.add), [BPS[half], Bw], [Btp])
                    fw.op("dve", lambda e: e.tensor_tensor(out=tp[:], in0=tp[:], in1=dec[i][:], op=ALU.mult), [Btp, Bdec[i]], [Btp])
                    fw.op("act", lambda e: e.activation(out=ab[:], in_=tp[:], func=AF.Abs), [Btp], [Bab])
                    fw.op("pool", lambda e: e.tensor_tensor(out=v3(a2[:]), in0=v4(ab[:])[:, :, 0, :], in1=v4(ab[:])[:, :, 1, :], op=ALU.add),
                          [Bab], [Ba2])
                    mm(PS[2][:, :], ones_f[:], a2[:], pt == 0, pt == npt - 1, [Bones, Ba2], [BPS[2]])
                    fw.op("dve", lambda e: e.tensor_tensor(out=v3(Pt[:, pt, :]), in0=v4(tp[:])[:, :, 0, :], in1=v4(tp[:])[:, :, 1, :],
                                                           op=ALU.add), [Btp], [BPt])
                    fw.op("pool", lambda e: e.tensor_tensor(out=v3(Mt[:, pt, :]), in0=v4(tp[:])[:, :, 0, :], in1=v4(tp[:])[:, :, 1, :],
                                                            op=ALU.subtract), [Btp], [BMt])
                rn = sb(st, "hrn", [128, 512], F32)
                ckt = sb(st, "hck", [128, nk], F32)
                Brn, Bck = Buf(), Buf()
                fw.op("dve", lambda e: e.reciprocal(out=rn[:], in_=PS[2][:, :]), [BPS[2]], [Brn])
                fw.dma("sp", ckt[:], I[ck], writes=[Bck])
                gct = [sb(st, "hgc%d" % i, [128, npt, 128], BF16) for i in range(2)]
                gst = [sb(st, "hgs%d" % i, [128, npt, 128], BF16) for i in range(2)]
                hre = [sb(st, "hre%d" % i, [128, 512], F32) for i in range(2)]
                hs = [sb(st, "hs%d" % i, [128, 512], F32) for i in range(2)]
                Bg, Bh = [Buf(), Buf()], [Buf(), Buf()]
                for kt in range(nk):
                    b = kt % 2
                    fw.dma("sp", gct[b][:], I[Gc][kt, :, 0:npt, :], writes=[Bg[b]])
                    fw.dma("sp", gst[b][:], I[Gs][kt, :, 0:npt, :], writes=[Bg[b]])
                    for pt in range(npt):
                        mm(PS[4][:, :], gct[b][:, pt, :], Pt[:, pt, :], pt == 0, pt == npt - 1, [Bg[b], BPt], [BPS[4]])
                    for pt in range(npt):
                        mm(PS[5][:, :], gst[b][:, pt, :], Mt[:, pt, :], pt == 0, pt == npt - 1, [Bg[b], BMt], [BPS[5]])
                    fw.op("dve", lambda e: e.scalar_tensor_tensor(out=hre[b][:], in0=PS[4][:, :], scalar=ckt[:, kt:kt + 1], in1=rn[:],
                                                                  op0=ALU.mult, op1=ALU.mult), [BPS[4], Bck, Brn], [Bh[b]])
                    fw.op("dve", lambda e: e.scalar_tensor_tensor(out=hs[b][:], in0=PS[5][:, :], scalar=ckt[:, kt:kt + 1], in1=rn[:],
                                                                  op0=ALU.mult, op1=ALU.mult), [BPS[5], Bck, Brn], [Bh[b]])
                    fw.dma("sp", S[HF][kt * 128:(kt + 1) * 128, 0, :], hre[b][:], reads=[Bh[b]], writes=[B[HF]])
                    fw.dma("sp", S[HF][kt * 128:(kt + 1) * 128, 1, :], hs[b][:], reads=[Bh[b]], writes=[B[HF]])
            fw.barrier()
            with contextlib.ExitStack() as st:
                zt = sb(st, "czt", [128, npt, 256], BF16)
                Yre = sb(st, "cYre", [128, nk, 256], BF16)
                Ys = sb(st, "cYs", [128, nk, 256], BF16)
                Bzt, BYre, BYs = Buf(), Buf(), Buf()
                gct = [sb(st, "cgc%d" % i, [128, nk, 128], BF16) for i in range(2)]
                gst = [sb(st, "cgs%d" % i, [128, nk, 128], BF16) for i in range(2)]
                hre = [sb(st, "chre%d" % i, [128, 256], F32) for i in range(2)]
                hs = [sb(st, "chs%d" % i, [128, 256], F32) for i in range(2)]
                Bg, Bh = [Buf(), Buf()], [Buf(), Buf()]
                t1 = sb(st, "ct1", [128, 256], F32)
                t2 = sb(st, "ct2", [128, 256], F32)
                Bt1, Bt2 = Buf(), Buf()
                skb = sb(st, "cskb", [128, 2, 256], F32)
                Bskb = Buf()
                fw.dma("sp", skb[:].rearrange("p a b -> p (a b)"), I["hy_skip"][l].rearrange("a b -> (a b)").partition_broadcast(128),
                       writes=[Bskb])
                zf = [sb(st, "czf%d" % i, [128, 256], F32) for i in range(2)]
                gt = [sb(st, "cgt%d" % i, [128, 256], F32) for i in range(2)]
                zo = [sb(st, "czo%d" % i, [128, 256], F32) for i in range(2)]
                Bzf, Bgt, Bzo = [Buf(), Buf()], [Buf(), Buf()], [Buf(), Buf()]
                for pt in range(npt):
                    i = pt % 2
                    fw.dma("sp", zf[i][:], S["HY"][s0 + pt * 128:s0 + (pt + 1) * 128, 0:256], reads=[B["HY"]], writes=[Bzf[i]])
                    fw.op("act", lambda e: e.activation(out=zt[:, pt, :], in_=zf[i][:], func=AF.Copy), [Bzf[i]], [Bzt])
                for o in range(2):
                    for kt in range(nk):
                        b = kt % 2
                        fw.dma("sp", gct[b][:, 0:npt, :], I[Gc][kt, :, 0:npt, :], writes=[Bg[b]])
                        fw.dma("sp", gst[b][:, 0:npt, :], I[Gs][kt, :, 0:npt, :], writes=[Bg[b]])
                        fw.dma("sp", hre[b][:], S[HF][kt * 128:(kt + 1) * 128, 0, o * 256:(o + 1) * 256], reads=[B[HF]], writes=[Bh[b]])
                        fw.dma("sp", hs[b][:], S[HF][kt * 128:(kt + 1) * 128, 1, o * 256:(o + 1) * 256], reads=[B[HF]], writes=[Bh[b]])
                        pc, ps_ = 2 * b, 2 * b + 1
                        for pt in range(npt):
                            mm(PS[pc][:, 0:256], gct[b][:, pt, :], zt[:, pt, :], pt == 0, pt == npt - 1, [Bg[b], Bzt], [BPS[pc]])
                        for pt in range(npt):
                            mm(PS[ps_][:, 0:256], gst[b][:, pt, :], zt[:, pt, :], pt == 0, pt == npt - 1, [Bg[b], Bzt], [BPS[ps_]])
                        fw.op("dve", lambda e: e.tensor_tensor(out=t1[:], in0=PS[pc][:, 0:256], in1=hre[b][:], op=ALU.mult),
                              [BPS[pc], Bh[b]], [Bt1])
                        fw.op("dve", lambda e: e.tensor_tensor(out=t2[:], in0=PS[ps_][:, 0:256], in1=hs[b][:], op=ALU.mult),
                              [BPS[ps_], Bh[b]], [Bt2])
                        fw.op("pool", lambda e: e.tensor_tensor(out=Yre[:, kt, :], in0=t1[:], in1=t2[:], op=ALU.subtract),
                              [Bt1, Bt2], [BYre])
                        fw.op("dve", lambda e: e.tensor_tensor(out=t1[:], in0=PS[pc][:, 0:256], in1=hs[b][:], op=ALU.mult),
                              [BPS[pc], Bh[b]], [Bt1])
                        fw.op("dve", lambda e: e.tensor_tensor(out=t2[:], in0=PS[ps_][:, 0:256], in1=hre[b][:], op=ALU.mult),
                              [BPS[ps_], Bh[b]], [Bt2])
                        fw.op("pool", lambda e: e.tensor_tensor(out=Ys[:, kt, :], in0=t1[:], in1=t2[:], op=ALU.add),
                              [Bt1, Bt2], [BYs])
                    for nt in range(npt):
                        b = nt % 2
                        fw.dma("sp", gct[b][:], I[Gc][nt, :, 0:nk, :], writes=[Bg[b]])
                        fw.dma("sp", gst[b][:], I[Gs][nt, :, 0:nk, :], writes=[Bg[b]])
                        src = S["HY"][s0 + nt * 128:s0 + (nt + 1) * 128, 0:256] if o == 0 else S["Z2"][s0 + nt * 128:s0 + (nt + 1) * 128, :]
                        fw.dma("sp", zf[b][:], src, reads=[B["HY"], B["Z2"]], writes=[Bzf[b]])
                        fw.dma("sp", gt[b][:], S["HY"][s0 + nt * 128:s0 + (nt + 1) * 128, 256 * (o + 1):256 * (o + 2)], reads=[B["HY"]],
                               writes=[Bgt[b]])
                        pb = 4 + b
                        for kt in range(nk):
                            mm(PS[pb][:, 0:256], gct[b][:, kt, :], Yre[:, kt, :], kt == 0, False, [Bg[b], BYre], [BPS[pb]])
                        for kt in range(nk):
                            mm(PS[pb][:, 0:256], gst[b][:, kt, :], Ys[:, kt, :], False, kt == nk - 1, [Bg[b], BYs], [BPS[pb]])
                        fw.op("pool", lambda e: e.tensor_tensor(out=zo[b][:], in0=zf[b][:], in1=skb[:, o, :], op=ALU.mult),
                              [Bzf[b], Bskb], [Bzo[b]])
                        fw.op("dve", lambda e: e.tensor_tensor(out=zo[b][:], in0=zo[b][:], in1=PS[pb][:, 0:256], op=ALU.add),
                              [Bzo[b], BPS[pb]], [Bzo[b]])
                        fw.op("dve", lambda e: e.tensor_tensor(out=zo[b][:], in0=zo[b][:], in1=gt[b][:], op=ALU.mult),
                              [Bzo[b], Bgt[b]], [Bzo[b]])
                        if o == 0:
                            fw.dma("sp", S["Z2"][s0 + nt * 128:s0 + (nt + 1) * 128, :], zo[b][:], reads=[Bzo[b]], writes=[B["Z2"]])
                            fw.op("act", lambda e: e.activation(out=zt[:, nt, :], in_=zo[b][:], func=AF.Copy), [Bzo[b]], [Bzt])
                        else:
                            fw.dma("sp", S["YMIX"][s0 + nt * 128:s0 + (nt + 1) * 128, 0:256], zo[b][:], reads=[Bzo[b]],
                                   writes=[B["YMIX"]])
            fw.barrier()

        def phase_hyena(l, upd):
            hyena_seg(l, L, LC, "hy_zx", "hy_decx", "gx_c", "gx_s", "ckx", "HFX", 33)
            if upd:
                hyena_seg(l, LC, 0, "hy_zc", "hy_decc", "gc_c", "gc_s", "ckc", "HFC", 3)

        def transpose_into(hn_t, Bhn_t, hT, BhT, t, pb):
            pT = PS[pb][:].bitcast(BF16)
            for kc in range(KC):
                fw.op("pe", lambda e: e.transpose(pT[:, kc * 128:(kc + 1) * 128], hn_t[:, kc * 128:(kc + 1) * 128], ident_bf[:]),
                      [Bhn_t, Bconst], [BPS[pb]])
            c0 = tcol(t)
            fw.op("act", lambda e: e.activation(out=hT[:, :, c0:c0 + 128], in_=pT.rearrange("p (k c) -> p k c", k=KC), func=AF.Copy),
                  [BPS[pb]], [BhT])

        def residual_update(st, tiles_groups, seg, lhs_fn, nk, wmat, Bw, reads_extra):
            gx, gc, Bg = load_mod(st, "gate", seg)
            xt = [sb(st, "rxt%d" % i, [128, D], F32) for i in range(2)]
            tmp = sb(st, "rtmp", [128, D], F32)
            Bxt, Btmp = [Buf(), Buf()], Buf()
            for it, t in enumerate(tiles_groups):
                i = it % 2
                g = gc if t < 2 else gx
                fw.dma("sp", xt[i][:], S["xres"][t * 128:(t + 1) * 128, :], reads=[B["xres"]], writes=[Bxt[i]])
                for cc in range(2):
                    pb = 4 + cc
                    for k in range(nk):
                        mm(PS[pb][:, :], lhs_fn(t, k), wmat[:, k, cc * 512:(cc + 1) * 512], k == 0, k == nk - 1,
                           [Bw] + reads_extra, [BPS[pb]])
                    fw.op("dve", lambda e: e.tensor_tensor(out=tmp[:, cc * 512:(cc + 1) * 512], in0=PS[pb][:, :],
                                                           in1=g[:, cc * 512:(cc + 1) * 512], op=ALU.mult), [BPS[pb], Bg], [Btmp])
                fw.op("pool", lambda e: e.tensor_tensor(out=xt[i][:], in0=xt[i][:], in1=tmp[:], op=ALU.add), [Bxt[i], Btmp], [Bxt[i]])
                fw.dma("sp", S["xres"][t * 128:(t + 1) * 128, :], xt[i][:], reads=[Bxt[i]], writes=[B["xres"]])

        def phase_merge(l, upd, hT, BhT):
            lam_init = 0.8 - 0.6 * math.exp(-0.3 * l)
            tiles = list(range(NT)) if upd else list(range(2, NT))
            with contextlib.ExitStack() as st:
                gw = sb(st, "gw", [128, D], F32)
                Bgw = Buf()
                fw.dma("sp", gw[:], I["mix_norm_w"][l, :].partition_broadcast(128), writes=[Bgw])
                fw.op("dve", lambda e: e.tensor_scalar(out=gw[:, 256:768], in0=gw[:, 256:768], scalar1=1.0 - lam_init, scalar2=None,
                                                       op0=ALU.mult), [Bgw], [Bgw])
                ym = [sb(st, "ym%d" % i, [128, D], F32) for i in range(2)]
                sq = sb(st, "sq", [128, D], F32)
                hn = [sb(st, "mhn%d" % i, [128, D], BF16) for i in range(2)]
                s1 = sb(st, "s1", [128, 16], F32)
                s2 = sb(st, "s2", [128, 16], F32)
                t4 = sb(st, "t4", [128, 4], F32)
                Bym, Bhn = [Buf(), Buf()], [Buf(), Buf()]
                Bsq, Bs1, Bs2, Bt4 = Buf(), Buf(), Buf(), Buf()
                for it, t in enumerate(tiles):
                    i = it % 2
                    fw.dma("sp", ym[i][:], S["YMIX"][t * 128:(t + 1) * 128, :], reads=[B["YMIX"]], writes=[Bym[i]])
                    fw.op("pool", lambda e: e.tensor_tensor(out=sq[:], in0=ym[i][:], in1=ym[i][:], op=ALU.mult), [Bym[i]], [Bsq])
                    fw.op("dve", lambda e: e.reduce_sum(out=s1[:], in_=sq[:].rearrange("p (g c) -> p g c", c=64), axis=AX.X),
                          [Bsq], [Bs1])
                    fw.op("dve", lambda e: e.reduce_sum(out=t4[:], in_=s1[:, 4:12].rearrange("p (g c) -> p g c", c=2), axis=AX.X),
                          [Bs1], [Bt4])
                    fw.op("dve", lambda e: e.tensor_scalar(out=s2[:], in0=s1[:], scalar1=1.0 / 64, scalar2=EPS, op0=ALU.mult,
                                                           op1=ALU.add), [Bs1], [Bs2])
                    for j in range(2):
                        fw.op("dve", lambda e: e.tensor_scalar(out=s2[:, 4 + j:12:2], in0=t4[:], scalar1=1.0 / 128, scalar2=EPS,
                                                               op0=ALU.mult, op1=ALU.add), [Bt4, Bs2], [Bs2])
                    fw.op("act", lambda e: e.activation(out=s2[:], in_=s2[:], func=AF.Sqrt), [Bs2], [Bs2])
                    fw.op("dve", lambda e: e.reciprocal(out=s1[:], in_=s2[:]), [Bs2, Bs1], [Bs1])
                    fw.op("pool", lambda e: e.tensor_tensor(out=sq[:].rearrange("p (g c) -> p g c", c=64),
                                                            in0=ym[i][:].rearrange("p (g c) -> p g c", c=64),
                                                            in1=s1[:, :].unsqueeze(2).to_broadcast([128, 16, 64]), op=ALU.mult),
                          [Bym[i], Bs1, Bsq], [Bsq])
                    fw.op("dve", lambda e: e.tensor_tensor(out=hn[i][:], in0=sq[:], in1=gw[:], op=ALU.mult), [Bsq, Bgw], [Bhn[i]])
                    transpose_into(hn[i], Bhn[i], hT, BhT, t, 6 + i)
            fw.barrier()
            with contextlib.ExitStack() as st:
                wo = sb(st, "wo", [128, KC, D], BF16)
                Bwo = Buf()
                fw.dma("pool", wo[:], I["w_out"][l].rearrange("(k p) c -> p k c", p=128), writes=[Bwo])
                residual_update(st, tiles, 2, lambda t, k: hT[:, k, tcol(t):tcol(t) + 128], KC, wo, Bwo, [BhT])
            fw.barrier()

        def phase_ffn(l, upd, hT, BhT):
            groups = FGROUPS if upd else FGROUPS[1:]
            tiles = list(range(NT)) if upd else list(range(2, NT))
            with contextlib.ExitStack() as st:
                wt = [sb(st, "fwt%d" % i, [128, KC, 128], BF16) for i in range(2)]
                Bwt = [Buf(), Buf()]
                rows = [sb(st, "frow%d" % i, [128, TP], F32) for i in range(2)]
                accs = [sb(st, "facc%d" % i, [128, TP], F32) for i in range(2)]
                Brows, Baccs = [Buf(), Buf()], [Buf(), Buf()]
                for i in range(2):
                    fw.op("dve", lambda e: e.memset(rows[i][:], 0.0), [], [Brows[i]])
                cw = [sb(st, "fcw%d" % i, [128, 4], F32) for i in range(2)]
                Bcw = [Buf(), Buf()]
                orow = [sb(st, "forow%d" % i, [128, T], BF16) for i in range(2)]
                Borow = [Buf(), Buf()]
                for ci in range(DFF // 128):
                    for w_, col0 in ((0, ci * 128), (1, DFF + ci * 128)):
                        fw.dma("pool", wt[w_][:], I["ffn_up"][l, :, col0:col0 + 128].rearrange("(k p) c -> p k c", p=128),
                               writes=[Bwt[w_]])
                        fw.dma("sp", cw[w_][:, 0:3], I["ffn_conv_w"][l, :, col0:col0 + 128].rearrange("j p -> p j"),
                               writes=[Bcw[w_]], allow_slow_non_contiguous=True)
                        fw.dma("sp", cw[w_][:, 3:4], I["ffn_conv_b"][l, col0:col0 + 128].rearrange("(p o) -> p o", o=1),
                               writes=[Bcw[w_]], allow_slow_non_contiguous=True)
                        for gi, (c0, n, s0) in enumerate(groups):
                            pb = 2 * w_ + gi % 2
                            for kc in range(KC):
                                mm(PS[pb][:, 0:n], wt[w_][:, kc, :], hT[:, kc, c0:c0 + n], kc == 0, kc == KC - 1,
                                   [Bwt[w_], BhT], [BPS[pb]])
                            fw.op("act", lambda e: e.activation(out=rows[w_][:, c0:c0 + n], in_=PS[pb][:, 0:n], func=AF.Copy),
                                  [BPS[pb]], [Brows[w_]])
                        acc = accs[w_][:, 0:TP - 2]
                        eng = "dve"
                        fw.op(eng, lambda e: e.tensor_scalar(out=acc, in0=rows[w_][:, 0:TP - 2], scalar1=cw[w_][:, 0:1],
                                                             scalar2=cw[w_][:, 3:4], op0=ALU.mult, op1=ALU.add),
                              [Brows[w_], Bcw[w_]], [Baccs[w_]])
                        for j in (1, 2):
                            fw.op(eng, lambda e: e.scalar_tensor_tensor(out=acc, in0=rows[w_][:, j:TP - 2 + j], scalar=cw[w_][:, j:j + 1],
                                                                        in1=acc, op0=ALU.mult, op1=ALU.add),
                                  [Brows[w_], Bcw[w_], Baccs[w_]], [Baccs[w_]])
                    fw.op("act", lambda e: e.activation(out=accs[1][:, 0:TP - 2], in_=accs[1][:, 0:TP - 2], func=AF.Silu),
                          [Baccs[1]], [Baccs[1]])
                    o = ci % 2
                    fw.op("dve", lambda e: e.tensor_tensor(out=orow[o][:, 0:LC], in0=accs[0][:, 0:LC], in1=accs[1][:, 0:LC], op=ALU.mult),
                          [Baccs[0], Baccs[1]], [Borow[o]])
                    fw.op("dve", lambda e: e.tensor_tensor(out=orow[o][:, LC:T], in0=accs[0][:, LC + 1:LC + 1 + L],
                                                           in1=accs[1][:, LC + 1:LC + 1 + L], op=ALU.mult),
                          [Baccs[0], Baccs[1]], [Borow[o]])
                    fw.dma("sp", S["ACTT"][ci * 128:(ci + 1) * 128, :], orow[o][:], reads=[Borow[o]], writes=[B["ACTT"]])
            fw.barrier()

        def phase_ffn_down(l, upd):
            with contextlib.ExitStack() as st:
                wd = sb(st, "wd", [128, DFF // 128, D], BF16)
                Bwd = Buf()
                for half in range(2):
                    fw.dma("pool", wd[:, half * 11:(half + 1) * 11, :],
                           I["ffn_down"][l, half * 1408:(half + 1) * 1408, :].rearrange("(k p) c -> p k c", p=128), writes=[Bwd])
                at = [sb(st, "at%d" % i, [128, DFF // 128, 512], BF16) for i in range(2)]
                Bat = [Buf(), Buf()]
                gx, gc, Bg = load_mod(st, "g2", 5)
                xt = [sb(st, "dxt%d" % i, [128, D], F32) for i in range(2)]
                tmp = sb(st, "dtmp", [128, D], F32)
                Bxt, Btmp = [Buf(), Buf()], Buf()
                it = 0
                for gi, (q0, n, t0, ntile) in enumerate(QGROUPS):
                    if gi == 0 and not upd:
                        continue
                    a = gi % 2
                    fw.dma("sp", at[a][:, :, 0:n], S["ACTT"][:, q0:q0 + n].rearrange("(k p) t -> p k t", p=128),
                           reads=[B["ACTT"]], writes=[Bat[a]])
                    for tt in range(ntile):
                        t = t0 + tt
                        i = it % 2
                        it += 1
                        g = gc if t < 2 else gx
                        fw.dma("sp", xt[i][:], S["xres"][t * 128:(t + 1) * 128, :], reads=[B["xres"]], writes=[Bxt[i]])
                        for cc in range(2):
                            pb = 4 + cc
                            for k in range(DFF // 128):
                                mm(PS[pb][:, :], at[a][:, k, tt * 128:(tt + 1) * 128], wd[:, k, cc * 512:(cc + 1) * 512], k == 0,
                                   k == DFF // 128 - 1, [Bat[a], Bwd], [BPS[pb]])
                            fw.op("dve", lambda e: e.tensor_tensor(out=tmp[:, cc * 512:(cc + 1) * 512], in0=PS[pb][:, :],
                                                                   in1=g[:, cc * 512:(cc + 1) * 512], op=ALU.mult), [BPS[pb], Bg], [Btmp])
                        fw.op("pool", lambda e: e.tensor_tensor(out=xt[i][:], in0=xt[i][:], in1=tmp[:], op=ALU.add), [Bxt[i], Btmp], [Bxt[i]])
                        fw.dma("sp", S["xres"][t * 128:(t + 1) * 128, :], xt[i][:], reads=[Bxt[i]], writes=[B["xres"]])
            fw.barrier()

        def phase_final():
            with contextlib.ExitStack() as st:
                fnw = sb(st, "fnw", [128, D], F32)
                Bfnw = Buf()
                fw.dma("sp", fnw[:], I["final_norm_w"].partition_broadcast(128), writes=[Bfnw])
                xt = [sb(st, "fxt%d" % i, [128, D], F32) for i in range(2)]
                ot = [sb(st, "fot%d" % i, [128, D], F32) for i in range(2)]
                junk = sb(st, "fjunk", [128, D], BF16)
                ss = [sb(st, "fss%d" % i, [128, 2], F32) for i in range(2)]
                Bxt, Bot, Bss, Bjunk = [Buf(), Buf()], [Buf(), Buf()], [Buf(), Buf()], Buf()
                for t in range(2, NT):
                    i = t % 2
                    fw.dma("sp", xt[i][:], S["xres"][t * 128:(t + 1) * 128, :], reads=[B["xres"]], writes=[Bxt[i]])
                    fw.op("act", lambda e: e.activation(out=junk[:], in_=xt[i][:], func=AF.Square, accum_out=ss[i][:, 0:1]),
                          [Bxt[i]], [Bjunk, Bss[i]])
                    fw.op("dve", lambda e: e.tensor_scalar(out=ss[i][:, 1:2], in0=ss[i][:, 0:1], scalar1=1.0 / D, scalar2=EPS,
                                                           op0=ALU.mult, op1=ALU.add), [Bss[i]], [Bss[i]])
                    fw.op("act", lambda e: e.activation(out=ss[i][:, 1:2], in_=ss[i][:, 1:2], func=AF.Sqrt), [Bss[i]], [Bss[i]])
                    fw.op("dve", lambda e: e.reciprocal(out=ss[i][:, 0:1], in_=ss[i][:, 1:2]), [Bss[i]], [Bss[i]])
                    fw.op("dve", lambda e: e.scalar_tensor_tensor(out=ot[i][:], in0=xt[i][:], scalar=ss[i][:, 0:1], in1=fnw[:],
                                                                  op0=ALU.mult, op1=ALU.mult), [Bxt[i], Bss[i], Bfnw], [Bot[i]])
                    fw.dma("sp", OUT[(t - 2) * 128:(t - 1) * 128, :], ot[i][:], reads=[Bot[i]], writes=[BOUT])
            fw.barrier()

        prog = {"nc": nc, "fw": fw}
        for l in layers:
            upd = l < DEPTH - 1
            phase_mod(l)
            if stop_after == "mod":
                break
            with contextlib.ExitStack() as st:
                hT = sb(st, "hT", [128, KC, TP], BF16)
                BhT = Buf("hT")
                fw.op("dve", lambda e: e.memset(hT[:], 0.0), [], [BhT])
                phase_norm(st, l, 0, hT, BhT, list(range(NT)))
                fw.barrier()
                phase_inproj(l, hT, BhT)
            fw.barrier()
            if stop_after == "inproj":
                break
            if "da" in run_phases:
                phase_da(l, upd)
            if "ml" in run_phases:
                phase_ml(l, upd)
            if "hy" in run_phases:
                phase_hyena(l, upd)
            if stop_after == "attn":
                break
            tiles = list(range(NT)) if upd else list(range(2, NT))
            with contextlib.ExitStack() as st:
                hT = sb(st, "hT", [128, KC, TP], BF16)
                BhT = Buf("hT")
                phase_merge(l, upd, hT, BhT)
            fw.barrier()
            if stop_after == "merge":
                break
            with contextlib.ExitStack() as st:
                hT = sb(st, "hT", [128, KC, TP], BF16)
                BhT = Buf("hT")
                phase_norm(st, l, 1, hT, BhT, tiles)
                fw.barrier()
                phase_ffn(l, upd, hT, BhT)
            fw.barrier()
            phase_ffn_down(l, upd)
            if stop_after == "layer":
                break
        if final and stop_after is None:
            phase_final()
        fw.barrier()
        fw.barrier()
    return nc


def _swap_perm():
    perm = np.zeros(1024, np.int64)
    for blk in range(16):
        for ax in range(2):
            for half in range(2):
                for f in range(16):
                    d = ax * 32 + half * 16 + f
                    ds = ax * 32 + (1 - half) * 16 + f
                    perm[blk * 64 + d] = blk * 64 + ds
    return perm


def make_in_maps(inputs, cores=range(8)):
    C = _consts()
    shared = {}
    for k in WEIGHT_SPECS:
        if k == "w_in_sw":
            continue
        shared[k] = np.ascontiguousarray(np.asarray(inputs[k], dtype=np.float32))
    w_in = shared["w_in"]
    shared["w_in_sw"] = np.ascontiguousarray(w_in[:, :, 768:1792][:, :, _swap_perm()])
    for k in CONST_SPECS:
        shared[k] = C[k]
    maps = []
    for b in cores:
        m = dict(shared)
        m["x"] = np.ascontiguousarray(np.asarray(inputs["x"][b], dtype=np.float32))
        m["ctx"] = np.ascontiguousarray(np.asarray(inputs["ctx"][b], dtype=np.float32))
        m["c2"] = np.ascontiguousarray(np.stack([np.asarray(inputs["c"][b], dtype=np.float32),
                                                 np.asarray(inputs["c_ctx"], dtype=np.float32)], 0))
        maps.append(m)
    return maps


_PROG = {}


def kernel(**inputs):
    if "nc" not in _PROG:
        _PROG["nc"] = build_program()
    maps = make_in_maps(inputs, cores=range(8))
    res = run_bass_kernel_spmd(_PROG["nc"], maps, core_ids=list(range(8)))
    out = np.stack([np.asarray(r["out"], dtype=np.float32) for r in res.results], 0)
    return out
```
